# Optimizing a Trainium2 kernel written in Bass

```python
import math
import jax
import jax.numpy as jnp
from jax import lax
import numpy as np

D_MODEL = 1024
BATCH = 16
SEQ = 2048
DEPTH = 2

GRID_W = 64
CTX_LEN = 256
EPS = 1e-6
N_MIXERS = 4
D_MIX = D_MODEL
W_GRP = D_MIX // N_MIXERS
CHUNK = 128

SSD_HEAD_DIM = 64
SSD_HEADS = W_GRP // SSD_HEAD_DIM
SSD_STATE = 64
SSD_CONV = 3
SSD_XBC = W_GRP + 2 * SSD_STATE
SSD_COLS = W_GRP + SSD_XBC + 2 * SSD_HEADS

HY_ORDER = 2
HY_EMB = 33
HY_BANDS = (HY_EMB - 1) // 2
HY_FILT = 64
HY_CONV = 3
HY_DECAY_SHORT = 0.3
HY_DECAY_LONG = 1.5
HY_DECAY_TARGET = 1e-2
HY_COLS = (HY_ORDER + 1) * W_GRP

RET_HEADS = 4
RET_HEAD_DIM = W_GRP // RET_HEADS
ROPE_BASE = 10000.0
RET_COLS = 4 * W_GRP

S5_CH = 16
S5_GROUPS = W_GRP // S5_CH
S5_STATE = 64
S5_DT_MIN = 1e-3
S5_DT_MAX = 1e-1
S5_COLS = W_GRP

OFF_HY = SSD_COLS
OFF_RET = OFF_HY + HY_COLS
OFF_S5 = OFF_RET + RET_COLS
D_IN = OFF_S5 + S5_COLS

D_FF = -(-8 * D_MODEL // (3 * 256)) * 256

kernel_name = 'hybrid_ssd_hyena_retention_s5_dit'


def rms_norm(x, g):
    xf = x.astype(jnp.float32)
    y = xf * lax.rsqrt(jnp.mean(xf * xf, axis=-1, keepdims=True) + EPS)
    return (y * g.astype(jnp.float32)).astype(x.dtype)


def modulate(x, shift, scale):
    return x * (1.0 + scale) + shift


def dwconv(x, w, b):
    k = w.shape[0]
    y = lax.conv_general_dilated(x, w[:, None, :].astype(x.dtype), window_strides=(1,),
                                 padding=((k // 2, k // 2),),
                                 dimension_numbers=('NWC', 'WIO', 'NWC'),
                                 feature_group_count=x.shape[-1])
    return y + b


def chunked_scan(q, k, v, log_a, h0):
    f32 = jnp.float32
    bsz, L, H, N = q.shape
    P = v.shape[-1]
    nc = L // CHUNK
    qc = q.astype(f32).reshape(bsz, nc, CHUNK, H, N)
    kc = k.astype(f32).reshape(bsz, nc, CHUNK, H, N)
    vc = v.astype(f32).reshape(bsz, nc, CHUNK, H, P)
    acs = jnp.cumsum(log_a.astype(f32).reshape(bsz, nc, CHUNK, H), axis=2)
    acs_h = jnp.swapaxes(acs, 2, 3)
    seg = acs_h[..., :, None] - acs_h[..., None, :]
    lower = jnp.tril(jnp.ones((CHUNK, CHUNK), dtype=bool))
    decay = jnp.exp(jnp.where(lower, seg, -jnp.inf))
    scores = jnp.einsum('bcihn,bcjhn->bchij', qc, kc) * decay
    y = jnp.einsum('bchij,bcjhp->bcihp', scores, vc)
    to_end = jnp.exp(acs[:, :, -1:, :] - acs)
    states = jnp.einsum('bcjhn,bcjh,bcjhp->bchnp', kc, to_end, vc)
    chunk_decay = jnp.exp(acs[:, :, -1, :])

    def step(h, inp):
        s, d = inp
        return h * d[:, :, None, None] + s, h

    h_last, h_in = lax.scan(step, h0.astype(f32),
                            (jnp.moveaxis(states, 1, 0), jnp.moveaxis(chunk_decay, 1, 0)))
    h_in = jnp.moveaxis(h_in, 0, 1)
    y = y + jnp.einsum('bcihn,bchnp->bcihp', qc * jnp.exp(acs)[..., None], h_in)
    return y.reshape(bsz, L, H, P).astype(v.dtype), h_last


def two_stream_scan(q_c, k_c, v_c, la_c, q_l, k_l, v_l, la_l, reverse):
    if reverse:
        q_c, k_c, v_c, la_c = (jnp.flip(q_c, 1), jnp.flip(k_c, 1), jnp.flip(v_c, 1), jnp.flip(la_c, 1))
        q_l, k_l, v_l, la_l = (jnp.flip(q_l, 1), jnp.flip(k_l, 1), jnp.flip(v_l, 1), jnp.flip(la_l, 1))
    bsz, _, H, N = q_c.shape
    h0 = jnp.zeros((bsz, H, N, v_c.shape[-1]), jnp.float32)
    y_c, h_c = chunked_scan(q_c, k_c, v_c, la_c, h0)
    y_l, _ = chunked_scan(q_l, k_l, v_l, la_l, h_c)
    if reverse:
        y_c, y_l = jnp.flip(y_c, 1), jnp.flip(y_l, 1)
    return y_c, y_l


def ssd_prep(u, conv_w, conv_b, dt_bias):
    bsz, L, _ = u.shape
    z = u[..., :W_GRP]
    xbc = jax.nn.silu(dwconv(u[..., W_GRP:W_GRP + SSD_XBC], conv_w, conv_b))
    xs = xbc[..., :W_GRP].reshape(bsz, L, SSD_HEADS, SSD_HEAD_DIM)
    bm = xbc[..., W_GRP:W_GRP + SSD_STATE]
    cm = xbc[..., W_GRP + SSD_STATE:]
    dt = jax.nn.softplus(u[..., W_GRP + SSD_XBC:].reshape(bsz, L, 2, SSD_HEADS) + dt_bias)
    return z, xs, bm, cm, dt


def ssd_direction_args(xs, bm, cm, dt_dir, a_dir):
    bsz, L = bm.shape[:2]
    q = jnp.broadcast_to(cm[:, :, None, :], (bsz, L, SSD_HEADS, SSD_STATE))
    k = bm[:, :, None, :] * dt_dir[..., None]
    return q, k, xs, dt_dir.astype(jnp.float32) * a_dir


def ssd_mixer(u_c, u_l, conv_w, conv_b, a_log, dt_bias, d_skip, norm_g):
    z_c, xs_c, b_c, c_c, dt_c = ssd_prep(u_c, conv_w, conv_b, dt_bias)
    z_l, xs_l, b_l, c_l, dt_l = ssd_prep(u_l, conv_w, conv_b, dt_bias)
    a = -jnp.exp(a_log.astype(jnp.float32))
    y_c = xs_c * d_skip[:, None]
    y_l = xs_l * d_skip[:, None]
    for r in range(2):
        yc, yl = two_stream_scan(*ssd_direction_args(xs_c, b_c, c_c, dt_c[:, :, r], a[r]),
                                 *ssd_direction_args(xs_l, b_l, c_l, dt_l[:, :, r], a[r]),
                                 reverse=(r == 1))
        y_c = y_c + yc
        y_l = y_l + yl
    out_c = rms_norm(y_c.reshape(z_c.shape) * jax.nn.silu(z_c), norm_g)
    out_l = rms_norm(y_l.reshape(z_l.shape) * jax.nn.silu(z_l), norm_g)
    return out_c, out_l


def hyena_filter(L, w1, b1, freq, w2, b2, w3):
    f32 = jnp.float32
    t = jnp.linspace(0.0, 1.0, L, dtype=f32)[:, None]
    w = 2.0 * math.pi * jnp.arange(L, dtype=f32)[:, None] / L
    bands = jnp.linspace(1e-4, HY_BANDS - 1, HY_BANDS, dtype=f32)[None, :]
    feats = jnp.concatenate([t, jnp.cos(bands * w), -jnp.sin(bands * w)], axis=-1)
    h = jnp.sin(freq * (feats @ w1 + b1))
    h = jnp.sin(freq * (h @ w2 + b2))
    h = (h @ w3).astype(f32)
    max_decay = math.log(HY_DECAY_TARGET) / HY_DECAY_SHORT
    min_decay = math.log(HY_DECAY_TARGET) / HY_DECAY_LONG
    deltas = jnp.abs(jnp.linspace(min_decay, max_decay, h.shape[-1], dtype=f32))
    h = h * jnp.exp(-t * deltas)
    return h.reshape(L, 2, HY_ORDER, W_GRP)


def two_sided_filter(h_f, h_b):
    L, C = h_f.shape
    k = jnp.concatenate([h_f, jnp.zeros((1, C), h_f.dtype), jnp.flip(h_b[1:], axis=0)], axis=0)
    return k / (jnp.sum(jnp.abs(k), axis=0, keepdims=True) + EPS)


def fft_conv(u, filt):
    L = u.shape[1]
    U = jnp.fft.rfft(u.astype(jnp.float32), n=2 * L, axis=1)
    F = jnp.fft.rfft(filt.astype(jnp.float32), axis=0)
    return jnp.fft.irfft(U * F, n=2 * L, axis=1)[:, :L].astype(u.dtype)


def hyena_stream(u, conv_w, conv_b, w1, b1, freq, w2, b2, w3, bias):
    L = u.shape[1]
    u = dwconv(u, conv_w, conv_b)
    x1, x2, z = jnp.split(u, 3, axis=-1)
    filt = hyena_filter(L, w1, b1, freq, w2, b2, w3)
    for o, gate in enumerate((x1, x2)):
        f2 = two_sided_filter(filt[:, 0, o], filt[:, 1, o])
        z = gate * (fft_conv(z, f2) + z * bias[o])
    return z


def rope_half(t, pos):
    f = t.shape[-1] // 2
    inv = ROPE_BASE ** (-jnp.arange(f, dtype=jnp.float32) / f)
    ang = pos.astype(jnp.float32)[:, None] * inv
    cos = jnp.cos(ang)[:, None, :]
    sin = jnp.sin(ang)[:, None, :]
    t1, t2 = t[..., :f], t[..., f:]
    return jnp.concatenate([t1 * cos - t2 * sin, t1 * sin + t2 * cos], axis=-1).astype(t.dtype)


def axial_rope(t, row_id, col_id):
    h = t.shape[-1] // 2
    return jnp.concatenate([rope_half(t[..., :h], row_id), rope_half(t[..., h:], col_id)], axis=-1)


def retention_out(y, g):
    yf = y.astype(jnp.float32)
    mu = jnp.mean(yf, axis=-1, keepdims=True)
    var = jnp.mean(jnp.square(yf - mu), axis=-1, keepdims=True)
    yn = ((yf - mu) * lax.rsqrt(var + EPS)).astype(g.dtype)
    return (jax.nn.silu(g) * yn).reshape(g.shape[0], g.shape[1], W_GRP)


def retention_mixer(u_c, u_l, decay_param, row_id, col_id):
    bc, lc = u_c.shape[:2]
    bl, ll = u_l.shape[:2]
    qc, kc, vc, gc = [t.reshape(bc, lc, RET_HEADS, RET_HEAD_DIM) for t in jnp.split(u_c, 4, axis=-1)]
    ql, kl, vl, gl = [t.reshape(bl, ll, RET_HEADS, RET_HEAD_DIM) for t in jnp.split(u_l, 4, axis=-1)]
    ql = axial_rope(ql, row_id, col_id)
    kl = axial_rope(kl, row_id, col_id)
    scale = RET_HEAD_DIM ** -0.5
    log_gamma = -jnp.exp(decay_param.astype(jnp.float32))
    y_c = jnp.zeros(vc.shape, jnp.float32)
    y_l = jnp.zeros(vl.shape, jnp.float32)
    for r in range(2):
        la_c = jnp.broadcast_to(log_gamma[r], (bc, lc, RET_HEADS))
        la_l = jnp.broadcast_to(log_gamma[r], (bl, ll, RET_HEADS))
        yc, yl = two_stream_scan(qc, kc * scale, vc, la_c, ql, kl * scale, vl, la_l, reverse=(r == 1))
        y_c = y_c + yc
        y_l = y_l + yl
    return retention_out(y_c, gc), retention_out(y_l, gl)


def s5_discretize(a_re, a_im, log_dt):
    f32 = jnp.float32
    dt = jnp.exp(log_dt.astype(f32))[:, None]
    a_re = a_re.astype(f32)
    a_im = a_im.astype(f32)
    mag = jnp.exp(a_re * dt)
    ab_re = mag * jnp.cos(a_im * dt)
    ab_im = mag * jnp.sin(a_im * dt)
    den = a_re * a_re + a_im * a_im
    z_re = ((ab_re - 1.0) * a_re + ab_im * a_im) / den
    z_im = (ab_im * a_re - (ab_re - 1.0) * a_im) / den
    return ab_re, ab_im, z_re, z_im


def s5_combine(e1, e2):
    a1r, a1i, b1r, b1i = e1
    a2r, a2i, b2r, b2i = e2
    return (a2r * a1r - a2i * a1i, a2r * a1i + a2i * a1r,
            a2r * b1r - a2i * b1i + b2r, a2r * b1i + a2i * b1r + b2i)


def s5_scan(ab_re, ab_im, bu_re, bu_im, h0_re, h0_im):
    b_re = bu_re.at[:, 0].add(ab_re * h0_re - ab_im * h0_im)
    b_im = bu_im.at[:, 0].add(ab_re * h0_im + ab_im * h0_re)
    a_re = jnp.broadcast_to(ab_re, b_re.shape)
    a_im = jnp.broadcast_to(ab_im, b_im.shape)
    _, _, h_re, h_im = lax.associative_scan(s5_combine, (a_re, a_im, b_re, b_im), axis=1)
    return h_re, h_im


def s5_direction(u, h0_re, h0_im, ab_re, ab_im, bb_re, bb_im, c_re, c_im, reverse):
    if reverse:
        u = jnp.flip(u, 1)
    bu_re = jnp.einsum('gpc,blgc->blgp', bb_re, u)
    bu_im = jnp.einsum('gpc,blgc->blgp', bb_im, u)
    h_re, h_im = s5_scan(ab_re, ab_im, bu_re, bu_im, h0_re, h0_im)
    y = (jnp.einsum('gcp,blgp->blgc', c_re.astype(jnp.float32), h_re)
         - jnp.einsum('gcp,blgp->blgc', c_im.astype(jnp.float32), h_im))
    if reverse:
        y = jnp.flip(y, 1)
    return y, h_re[:, -1], h_im[:, -1]


def s5_glu(y, w, b, dtype):
    y = jax.nn.gelu(y.reshape(y.shape[0], y.shape[1], W_GRP)).astype(dtype)
    return y * jax.nn.sigmoid(y @ w + b)


def s5_mixer(u_c, u_l, a_re, a_im, log_dt, b_re, b_im, c_re, c_im, d_skip, glu_w, glu_b):
    f32 = jnp.float32
    uc = u_c.astype(f32).reshape(u_c.shape[0], u_c.shape[1], S5_GROUPS, S5_CH)
    ul = u_l.astype(f32).reshape(u_l.shape[0], u_l.shape[1], S5_GROUPS, S5_CH)
    dg = d_skip.astype(f32).reshape(S5_GROUPS, S5_CH)
    y_c = uc * dg
    y_l = ul * dg
    b_re = b_re.astype(f32)
    b_im = b_im.astype(f32)
    zero = jnp.zeros((u_c.shape[0], S5_GROUPS, S5_STATE), f32)
    for r in range(2):
        ab_re, ab_im, z_re, z_im = s5_discretize(a_re[r], a_im[r], log_dt[r])
        bb_re = z_re[..., None] * b_re - z_im[..., None] * b_im
        bb_im = z_re[..., None] * b_im + z_im[..., None] * b_re
        yc, hc_re, hc_im = s5_direction(uc, zero, zero, ab_re, ab_im, bb_re, bb_im,
                                        c_re[r], c_im[r], r == 1)
        yl, _, _ = s5_direction(ul, hc_re, hc_im, ab_re, ab_im, bb_re, bb_im,
                                c_re[r], c_im[r], r == 1)
        y_c = y_c + yc
        y_l = y_l + yl
    return s5_glu(y_c, glu_w, glu_b, u_c.dtype), s5_glu(y_l, glu_w, glu_b, u_l.dtype)


def swiglu(x, w_up, w_down):
    g, u = jnp.split(x @ w_up, 2, axis=-1)
    return (jax.nn.silu(g) * u) @ w_down


def split_cols(u):
    return jnp.split(u, [OFF_HY, OFF_RET, OFF_S5], axis=-1)


def setup_inputs(seed: int = 0) -> dict:
    key = jax.random.key(seed)
    ks = iter(jax.random.split(key, 64))
    f32 = jnp.float32
    L = DEPTH

    def nrm(shape, scale):
        return scale * jax.random.normal(next(ks), shape, f32)

    def unif(shape, lo, hi):
        return jax.random.uniform(next(ks), shape, f32, lo, hi)

    x = nrm((BATCH, SEQ, D_MODEL), 1.0)
    c = nrm((BATCH, D_MODEL), 1.0)
    ctx = nrm((BATCH, CTX_LEN, D_MODEL), 1.0)
    c_ctx = nrm((D_MODEL,), 1.0)
    mod_w = nrm((L, D_MODEL, 6 * D_MODEL), 0.5 * D_MODEL ** -0.5)
    mod_b = nrm((L, 6 * D_MODEL), 0.02)
    norm1_g = 1.0 + nrm((L, D_MODEL), 0.02)
    norm2_g = 1.0 + nrm((L, D_MODEL), 0.02)
    w_in = nrm((L, D_MODEL, D_IN), D_MODEL ** -0.5)
    w_out = nrm((L, D_MIX, D_MODEL), D_MIX ** -0.5)
    ssd_conv_w = nrm((L, SSD_CONV, SSD_XBC), SSD_CONV ** -0.5)
    ssd_conv_b = nrm((L, SSD_XBC), 0.02)
    ssd_a_log = jnp.log(unif((L, 2, SSD_HEADS), 1.0, 16.0))
    dt0 = jnp.exp(unif((L, 2, SSD_HEADS), math.log(1e-3), math.log(1e-1)))
    ssd_dt_bias = dt0 + jnp.log(-jnp.expm1(-dt0))
    ssd_d = 1.0 + nrm((L, SSD_HEADS), 0.02)
    ssd_norm_g = 1.0 + nrm((L, W_GRP), 0.02)
    hy_conv_w = nrm((L, HY_CONV, HY_COLS), HY_CONV ** -0.5)
    hy_conv_b = nrm((L, HY_COLS), 0.02)
    hy_w1 = nrm((L, HY_EMB, HY_FILT), HY_EMB ** -0.5)
    hy_b1 = nrm((L, HY_FILT), 0.02)
    hy_freq = 1.0 + nrm((L, HY_FILT), 0.02)
    hy_w2 = nrm((L, HY_FILT, HY_FILT), HY_FILT ** -0.5)
    hy_b2 = nrm((L, HY_FILT), 0.02)
    hy_w3 = nrm((L, HY_FILT, 2 * HY_ORDER * W_GRP), HY_FILT ** -0.5)
    hy_bias = nrm((L, HY_ORDER, W_GRP), 1.0)
    gam = 1.0 - 2.0 ** (-5.0 - jnp.arange(RET_HEADS, dtype=f32))
    ret_decay = jnp.log(-jnp.log(gam)) + nrm((L, 2, RET_HEADS), 0.01)
    n_idx = jnp.arange(S5_STATE, dtype=f32)
    s5_a_re = -0.5 + nrm((L, 2, S5_GROUPS, S5_STATE), 0.01)
    s5_a_im = math.pi * n_idx + nrm((L, 2, S5_GROUPS, S5_STATE), 0.01)
    s5_log_dt = unif((L, 2, S5_GROUPS), math.log(S5_DT_MIN), math.log(S5_DT_MAX))
    s5_b_re = nrm((L, S5_GROUPS, S5_STATE, S5_CH), (2 * S5_CH) ** -0.5)
    s5_b_im = nrm((L, S5_GROUPS, S5_STATE, S5_CH), (2 * S5_CH) ** -0.5)
    s5_c_re = nrm((L, 2, S5_GROUPS, S5_CH, S5_STATE), (2 * S5_STATE) ** -0.5)
    s5_c_im = nrm((L, 2, S5_GROUPS, S5_CH, S5_STATE), (2 * S5_STATE) ** -0.5)
    s5_d = nrm((L, W_GRP), 1.0)
    s5_glu_w = nrm((L, W_GRP, W_GRP), W_GRP ** -0.5)
    s5_glu_b = nrm((L, W_GRP), 0.02)
    ffn_w_up = nrm((L, D_MODEL, 2 * D_FF), D_MODEL ** -0.5)
    ffn_w_down = nrm((L, D_FF, D_MODEL), D_FF ** -0.5)
    final_norm_g = 1.0 + nrm((D_MODEL,), 0.02)
    return {'x': x, 'c': c, 'ctx': ctx, 'c_ctx': c_ctx, 'mod_w': mod_w, 'mod_b': mod_b,
            'norm1_g': norm1_g, 'norm2_g': norm2_g, 'w_in': w_in, 'w_out': w_out,
            'ssd_conv_w': ssd_conv_w, 'ssd_conv_b': ssd_conv_b, 'ssd_a_log': ssd_a_log,
            'ssd_dt_bias': ssd_dt_bias, 'ssd_d': ssd_d, 'ssd_norm_g': ssd_norm_g,
            'hy_conv_w': hy_conv_w, 'hy_conv_b': hy_conv_b, 'hy_w1': hy_w1, 'hy_b1': hy_b1,
            'hy_freq': hy_freq, 'hy_w2': hy_w2, 'hy_b2': hy_b2, 'hy_w3': hy_w3, 'hy_bias': hy_bias,
            'ret_decay': ret_decay, 's5_a_re': s5_a_re, 's5_a_im': s5_a_im, 's5_log_dt': s5_log_dt,
            's5_b_re': s5_b_re, 's5_b_im': s5_b_im, 's5_c_re': s5_c_re, 's5_c_im': s5_c_im,
            's5_d': s5_d, 's5_glu_w': s5_glu_w, 's5_glu_b': s5_glu_b,
            'ffn_w_up': ffn_w_up, 'ffn_w_down': ffn_w_down, 'final_norm_g': final_norm_g}


def reference(x, c, ctx, c_ctx, mod_w, mod_b, norm1_g, norm2_g, w_in, w_out,
              ssd_conv_w, ssd_conv_b, ssd_a_log, ssd_dt_bias, ssd_d, ssd_norm_g,
              hy_conv_w, hy_conv_b, hy_w1, hy_b1, hy_freq, hy_w2, hy_b2, hy_w3, hy_bias,
              ret_decay, s5_a_re, s5_a_im, s5_log_dt, s5_b_re, s5_b_im, s5_c_re, s5_c_im,
              s5_d, s5_glu_w, s5_glu_b, ffn_w_up, ffn_w_down, final_norm_g):
    n_lat = x.shape[1]
    rows = n_lat // GRID_W
    row_id = jnp.repeat(jnp.arange(rows), GRID_W)
    col_id = jnp.tile(jnp.arange(GRID_W), rows)
    silu_c = jax.nn.silu(c)
    silu_cc = jax.nn.silu(c_ctx)
    h_l, h_c = x, ctx
    for i in range(DEPTH):
        last = i == DEPTH - 1
        m_l = jnp.split((silu_c @ mod_w[i] + mod_b[i])[:, None, :], 6, axis=-1)
        m_c = jnp.split((silu_cc @ mod_w[i] + mod_b[i])[None, None, :], 6, axis=-1)
        ul = split_cols(modulate(rms_norm(h_l, norm1_g[i]), m_l[0], m_l[1]) @ w_in[i])
        uc = split_cols(modulate(rms_norm(h_c, norm1_g[i]), m_c[0], m_c[1]) @ w_in[i])
        ssd_c, ssd_l = ssd_mixer(uc[0], ul[0], ssd_conv_w[i], ssd_conv_b[i], ssd_a_log[i],
                                 ssd_dt_bias[i], ssd_d[i], ssd_norm_g[i])
        hy_args = (hy_conv_w[i], hy_conv_b[i], hy_w1[i], hy_b1[i], hy_freq[i],
                   hy_w2[i], hy_b2[i], hy_w3[i], hy_bias[i])
        hy_l = hyena_stream(ul[1], *hy_args)
        ret_c, ret_l = retention_mixer(uc[2], ul[2], ret_decay[i], row_id, col_id)
        s5c, s5l = s5_mixer(uc[3], ul[3], s5_a_re[i], s5_a_im[i], s5_log_dt[i], s5_b_re[i],
                            s5_b_im[i], s5_c_re[i], s5_c_im[i], s5_d[i], s5_glu_w[i], s5_glu_b[i])
        y_l = jnp.concatenate([ssd_l, hy_l, ret_l, s5l], axis=-1) @ w_out[i]
        h_l = h_l + m_l[2] * y_l
        h_l = h_l + m_l[5] * swiglu(modulate(rms_norm(h_l, norm2_g[i]), m_l[3], m_l[4]),
                                    ffn_w_up[i], ffn_w_down[i])
        if not last:
            hy_c = hyena_stream(uc[1], *hy_args)
            y_c = jnp.concatenate([ssd_c, hy_c, ret_c, s5c], axis=-1) @ w_out[i]
            h_c = h_c + m_c[2] * y_c
            h_c = h_c + m_c[5] * swiglu(modulate(rms_norm(h_c, norm2_g[i]), m_c[3], m_c[4]),
                                        ffn_w_up[i], ffn_w_down[i])
    return rms_norm(h_l, final_norm_g)
```

```python
import math
import os
from contextlib import ExitStack
import numpy as np
import ml_dtypes
import concourse.bass as bass
import concourse.mybir as mybir
from concourse.bass_utils import run_bass_kernel_spmd

F32 = mybir.dt.float32
BF16 = mybir.dt.bfloat16
AF = mybir.ActivationFunctionType
ALU = mybir.AluOpType
AX = mybir.AxisListType

ENGS = ['sp', 'pe', 'dve', 'act', 'pool']
NDMA = 8
NQ = 0
SAME_ENGINE_SYNC = True

D = 1024
NB = 2
LC = 256
LL = 2048
T = NB * (LC + LL)
NT = T // 512
EPS = 1e-6
DIN = 2696
DFF = 2816
C_Z = [(0, 128), (128, 128)]
C_X = [(256, 128), (384, 128)]
C_B = (512, 64)
C_C = (576, 64)
C_DT = (640, 8)
OFF_HY = 648
OFF_RET = 1416
OFF_S5 = 2440


def ctx_off(b):
    return LC * b


def lat_off(b):
    return NB * LC + LL * b


def tile_col(ti):
    if ti == 0:
        return 2
    return (ti - 1) // 4


class Trk:
    __slots__ = ('w', 'r')

    def __init__(self):
        self.w = None
        self.r = []


class Buf:
    def __init__(self, t, name, shape=None):
        self.t = t
        self.name = name
        self.trk = Trk()
        self.subs = {}
        self.shape = shape
        self.psum = False

    def __getitem__(self, idx):
        return self.t[idx]

    def s(self, key):
        if key not in self.subs:
            self.subs[key] = Buf(self.t, f"{self.name}.{key}", self.shape)
        return self.subs[key]


def AP(buf, off, dims):
    t = buf.t if isinstance(buf, Buf) else buf
    tt = t.tensor if isinstance(t, bass.AP) else t
    return bass.AP(tt, off, [list(d) for d in dims])


class Kern:
    def __init__(self):
        self.nc = bass.Bass("TRN2", target_bir_lowering=False)
        self.es = ExitStack()
        self.sems = {}
        for e in ENGS:
            self.sems[('e', e)] = self.es.enter_context(self.nc.semaphore(f"s_{e}"))
        for k in range(NDMA):
            self.sems[('d', k)] = self.es.enter_context(self.nc.semaphore(f"s_d{k}"))
        for k in range(NQ):
            self.sems[('q', k)] = self.es.enter_context(self.nc.semaphore(f"s_q{k}"))
        self.nstage = 0
        self.ninst = 0

    def dram(self, name, shape, dtype, kind="Internal"):
        t = self.nc.dram_tensor(name, list(shape), dtype, kind=kind)
        return Buf(t.ap(), name, shape)

    def sb(self, name, shape, dtype):
        t = self.es.enter_context(self.nc.sbuf_tensor(name, list(shape), dtype))
        return Buf(t, name, shape)

    def stage(self, name):
        self.nstage += 1
        return Stage(self, f"{name}{self.nstage}")

    def close(self):
        self.es.close()


class Stage:
    def __init__(self, K, name):
        self.K = K
        self.nc = K.nc
        self.name = name
        self.ops = {e: [] for e in ENGS}
        self.cnt = {e: 0 for e in ENGS}
        self.dma_n = 0
        self.qn = 0
        self.waited = {e: {} for e in ENGS}
        self.es = ExitStack()
        self.nbuf = 0
        self.touched = {}

    def sb(self, name, shape, dtype):
        self.nbuf += 1
        t = self.es.enter_context(self.nc.sbuf_tensor(f"{self.name}_{name}_{self.nbuf}", list(shape), dtype))
        return Buf(t, name, shape)

    def ps(self, name, shape=(128, 512), dtype=F32):
        self.nbuf += 1
        t = self.es.enter_context(self.nc.psum_tensor(f"{self.name}_{name}_{self.nbuf}", list(shape), dtype))
        b = Buf(t, name, shape)
        b.psum = True
        return b

    def op(self, eng, fn, reads=(), writes=(), dma=False):
        pr = [b for b in reads if b.psum]
        if pr:
            reads = [b for b in reads if not b.psum]
            writes = list(writes) + [b for b in pr if b not in writes]
        deps = []
        own = ('e', eng)
        for b in reads:
            if b.trk.w is not None:
                deps.append(b.trk.w)
        for b in writes:
            if b.trk.w is not None:
                deps.append(b.trk.w)
            deps.extend(b.trk.r)
        waits = {}
        for (sk, val) in deps:
            if sk == ('e', eng) and (eng == 'pe' or not SAME_ENGINE_SYNC):
                continue
            if waits.get(sk, 0) < val:
                waits[sk] = val
        if dma and eng == 'pool' and NQ > 0:
            assert self.qn < NQ
            done = (('q', self.qn), 16)
            self.qn += 1
            inc = 16
        elif dma:
            k = self.dma_n % NDMA
            gen = self.dma_n // NDMA
            self.dma_n += 1
            sk = ('d', k)
            if gen > 0 and waits.get(sk, 0) < 16 * gen:
                waits[sk] = 16 * gen
            done = (sk, 16 * (gen + 1))
            inc = 16
        else:
            self.cnt[eng] += 1
            done = (('e', eng), self.cnt[eng])
            inc = 1
        wl = []
        for sk, val in waits.items():
            if self.waited[eng].get(sk, 0) >= val:
                continue
            self.waited[eng][sk] = val
            wl.append((sk, val))
        self.ops[eng].append((wl, fn, done[0], inc))
        for b in reads:
            b.trk.r.append(done)
            self.touched[id(b)] = b
        for b in writes:
            b.trk.w = done
            b.trk.r = []
            self.touched[id(b)] = b
        return done

    def dma(self, out_ap, in_ap, R=(), W=(), eng='sp', **kw):
        return self.op(eng, lambda e: e.dma_start(out=out_ap, in_=in_ap, **kw), R, W, dma=True)

    def mm(self, out, lhsT, rhs, start, stop, R, W):
        return self.op('pe', lambda e: e.matmul(out, lhsT=lhsT, rhs=rhs, start=start, stop=stop), R, W)

    def tr(self, out, in_, ident, R, W):
        return self.op('pe', lambda e: e.transpose(out=out, in_=in_, identity=ident), R, W)

    def act(self, out, in_, func, R, W, **kw):
        return self.op('act', lambda e: e.activation(out=out, in_=in_, func=func, **kw), R, W)

    def tt(self, eng, out, in0, in1, op, R, W):
        return self.op(eng, lambda e: e.tensor_tensor(out=out, in0=in0, in1=in1, op=op), R, W)

    def ts(self, eng, out, in0, s1, s2, op0, op1, R, W):
        if op1 is None:
            return self.op(eng, lambda e: e.tensor_scalar(out=out, in0=in0, scalar1=s1, scalar2=None, op0=op0), R, W)
        return self.op(eng, lambda e: e.tensor_scalar(out=out, in0=in0, scalar1=s1, scalar2=s2, op0=op0, op1=op1), R, W)

    def stt(self, out, in0, scalar, in1, op0, op1, R, W):
        return self.op('dve', lambda e: e.scalar_tensor_tensor(out=out, in0=in0, scalar=scalar, in1=in1, op0=op0, op1=op1), R, W)

    def cp(self, eng, out, in_, R, W):
        if eng == 'act':
            return self.op('act', lambda e: e.activation(out=out, in_=in_, func=AF.Identity), R, W)
        return self.op(eng, lambda e: e.tensor_copy(out=out, in_=in_), R, W)

    def ms(self, eng, ap, val, W):
        return self.op(eng, lambda e: e.memset(ap, val), (), W)

    def emit(self):
        nc = self.nc
        sems = self.K.sems
        with nc.Block() as blk:
            def clr(e):
                for h in sems.values():
                    e.sem_clear(h)
            blk.sync(clr)
        ops = self.ops
        dma_n = self.dma_n
        qn = self.qn

        def mk(engname):
            def body(e):
                for (wl, fn, sk, inc) in ops[engname]:
                    for (wsk, val) in wl:
                        e.wait_ge(sems[wsk], val)
                    fn(e).then_inc(sems[sk], inc)
                if engname == 'sp':
                    for k in range(min(NDMA, dma_n)):
                        tot = (dma_n - k + NDMA - 1) // NDMA
                        e.wait_ge(sems[('d', k)], 16 * tot)
                    for k in range(qn):
                        e.wait_ge(sems[('q', k)], 16)
            return body
        with nc.Block() as blk:
            blk.sync(mk('sp'))
            blk.tensor(mk('pe'))
            blk.vector(mk('dve'))
            blk.scalar(mk('act'))
            blk.gpsimd(mk('pool'))
        n = 0
        for e in ENGS:
            n += len(ops[e]) + sum(len(o[0]) for o in ops[e])
        self.K.ninst += n
        for b in self.touched.values():
            b.trk.w = None
            b.trk.r = []
        self.touched = {}
        self.es.close()
        self.ops = None


_CONST_CACHE = {}


def dft_tables(L):
    nt = L // 128
    idx = np.arange(L, dtype=np.int64)
    prod = (idx[:, None] * idx[None, :]) % (2 * L)
    ang = np.pi * prod.astype(np.float64) / L
    Cfull = np.cos(ang)
    Sfull = -np.sin(ang)
    alt = np.where(idx % 2 == 0, 1.0, -1.0)
    Sf_full = Sfull.copy()
    Sf_full[0, :] = alt
    def blk_fwd(M):
        return M.reshape(nt, 128, nt, 128).transpose(0, 3, 2, 1)
    def blk_inv(M):
        return M.reshape(nt, 128, nt, 128).transpose(2, 1, 0, 3)
    C = np.ascontiguousarray(blk_fwd(Cfull)).astype(ml_dtypes.bfloat16)
    Sf = np.ascontiguousarray(blk_fwd(Sf_full)).astype(ml_dtypes.bfloat16)
    Si = np.ascontiguousarray(blk_inv(Sf_full)).astype(ml_dtypes.bfloat16)
    return C, Sf, Si


def hy_feats(L):
    f32 = np.float32
    t = np.linspace(0.0, 1.0, L, dtype=f32)[:, None]
    w = (2.0 * math.pi * np.arange(L, dtype=f32)[:, None] / L).astype(f32)
    bands = np.linspace(1e-4, 16 - 1, 16, dtype=f32)[None, :]
    feats = np.concatenate([t, np.cos(bands * w), -np.sin(bands * w)], axis=-1).astype(f32)
    max_decay = math.log(1e-2) / 0.3
    min_decay = math.log(1e-2) / 1.5
    deltas = np.abs(np.linspace(min_decay, max_decay, 1024, dtype=f32))
    dec = np.exp(-t * deltas).astype(f32)
    return np.ascontiguousarray(feats.T), dec


def host_consts():
    if _CONST_CACHE:
        return _CONST_CACHE
    c = {}
    c['ident_b'] = np.eye(128).astype(ml_dtypes.bfloat16)
    c['ident_f'] = np.eye(128, dtype=np.float32)
    c['ones_f'] = np.ones((128, 128), np.float32)
    k = np.arange(128)
    c['ule'] = (k[:, None] <= k[None, :]).astype(np.float32)
    c['uge'] = (k[:, None] >= k[None, :]).astype(np.float32)
    sel = np.zeros((8, 8, 128), np.float32)
    for i in range(8):
        sel[i, i, :] = 1.0
    c['sel'] = sel
    mn = np.zeros((2, 128, 128), np.float32)
    mn[0][k[:, None] > k[None, :]] = -30000.0
    mn[1][k[:, None] < k[None, :]] = -30000.0
    c['mneg4'] = np.ascontiguousarray(np.broadcast_to(mn.transpose(1, 0, 2)[:, :, None, :], (128, 2, 4, 128))).astype(ml_dtypes.bfloat16)
    c['idiff'] = (k[None, :] - k[:, None]).astype(np.float32)
    c['ramp'] = np.broadcast_to((k[None, :] + 1).astype(np.float32), (128, 128)).copy()
    c['pidx'] = k[:, None].astype(np.float32).copy()
    t = np.arange(LL)
    row = (t // 64).astype(np.float32)
    col = (t % 64).astype(np.float32)
    inv = (10000.0 ** (-np.arange(16, dtype=np.float32) / 16)).astype(np.float32)
    cosT = np.zeros((128, LL), np.float32)
    sinT = np.zeros((128, LL), np.float32)
    Pm = np.zeros((128, 128), np.float32)
    for p in range(128):
        d = p % 64
        half = d // 32
        i = d % 16
        pos = row if half == 0 else col
        ang = (pos * inv[i]).astype(np.float32)
        cosT[p] = np.cos(ang)
        sinT[p] = np.sin(ang)
        if (d % 32) < 16:
            Pm[p + 16, p] = -1.0
        else:
            Pm[p - 16, p] = 1.0
    c['rope_cos'] = cosT
    c['rope_sin'] = sinT
    c['rope_p'] = Pm.astype(ml_dtypes.bfloat16)
    for L, tag in ((LL, 'L'), (LC, 'C')):
        C, Sf, Si = dft_tables(L)
        c[f'dftc_{tag}'] = C
        c[f'dftsf_{tag}'] = Sf
        c[f'dftsi_{tag}'] = Si
        ft, dec = hy_feats(L)
        c[f'feat_{tag}'] = ft
        c[f'dec_{tag}'] = dec
        nt = L // 128
        wf = np.full((128, nt), 2.0 / (2 * L), np.float32)
        wf[0, 0] = 1.0 / (2 * L)
        c[f'wf_{tag}'] = wf
        alt = np.where(np.arange(L) % 2 == 0, 1.0, -1.0).astype(np.float32)
        c[f'alt_{tag}'] = np.ascontiguousarray(alt.reshape(nt, 128).T).astype(ml_dtypes.bfloat16)
    _CONST_CACHE.update(c)
    return _CONST_CACHE

class Prog:
    pass


def declare(K, dbg):
    P = Prog()
    P.K = K
    cst = host_consts()
    P.cin = {}

    def inp(name, shape, dt=F32):
        b = K.dram(name, shape, dt, kind="ExternalInput")
        setattr(P, name, b)
        return b
    inp('xinT', [D, T])
    inp('cT', [128, 8, 3])
    inp('mod_w', [2, D, 6 * D])
    inp('mod_bT', [2, 128, 48])
    inp('norm1_gT', [2, 128, 8])
    inp('norm2_gT', [2, 128, 8])
    inp('final_gT', [128, 8])
    inp('w_in', [2, D, DIN])
    inp('w_out', [2, D, D])
    inp('ffn_w_up', [2, D, 2 * DFF])
    inp('ffn_w_down', [2, DFF, D])
    inp('ssd_cw', [2, 128, 4, 4])
    inp('ssd_alog8', [2, 8, 1])
    inp('ssd_dtb8', [2, 8, 1])
    inp('ssd_dexp', [2, 256])
    inp('ssd_norm_g', [2, 256])
    inp('hy_cw', [2, 128, 6, 4])
    inp('hy_w1', [2, 33, 64])
    inp('hy_b1c', [2, 64, 1])
    inp('hy_freqc', [2, 64, 1])
    inp('hy_w2', [2, 64, 64])
    inp('hy_b2c', [2, 64, 1])
    inp('hy_w3', [2, 64, 1024])
    inp('hy_bias', [2, 512])
    inp('ret_decay8', [2, 8])
    inp('s5_are', [2, 2, 128, 8])
    inp('s5_aim', [2, 2, 128, 8])
    inp('s5_ldt', [2, 2, 128, 8])
    inp('s5_bre', [2, 128, 8, 16])
    inp('s5_bim', [2, 128, 8, 16])
    inp('s5_cre', [2, 2, 128, 8, 16])
    inp('s5_cim', [2, 2, 128, 8, 16])
    inp('s5_dT', [2, 128, 2])
    inp('s5_glu_w', [2, 256, 256])
    inp('s5_glu_bT', [2, 128, 2])
    for name, arr in cst.items():
        dt = BF16 if arr.dtype == ml_dtypes.bfloat16 else F32
        b = K.dram('k_' + name, list(arr.shape), dt, kind="ExternalInput")
        setattr(P, 'k_' + name, b)
    kind_dbg = "ExternalOutput" if dbg else "Internal"
    P.out = K.dram('out', [D, NB * LL], F32, kind="ExternalOutput")
    P.hT = K.dram('hT', [D, T], F32, kind=kind_dbg)
    P.uT = K.dram('uT', [DIN, T], BF16, kind=kind_dbg)
    P.udt = K.dram('udt', [8, T], F32, kind=kind_dbg)
    P.mixT = K.dram('mixT', [D, T], BF16, kind=kind_dbg)
    P.w_out_b = K.dram('w_out_b', [2, D, D], BF16)
    P.w_up_b = K.dram('w_up_b', [2, D, 2 * DFF], BF16)
    P.w_dn_b = K.dram('w_dn_b', [2, DFF, D], BF16)
    P.glu_w_b = K.dram('glu_w_b', [2, 256, 256], BF16)
    P.ident_b = K.sb('ident_b', [128, 128], BF16)
    P.ident_f = K.sb('ident_f', [128, 128], F32)
    P.ones_f = K.sb('ones_f', [128, 128], F32)
    P.mod = [K.sb(f'mod{l}', [128, 48, 3], F32) for l in range(2)]
    P.scs = K.sb('scs', [128, 8, 3], F32)
    P.A1 = [K.sb(f'A1_{l}', [128, 8, 3], F32) for l in range(2)]
    P.A2 = [K.sb(f'A2_{l}', [128, 8, 3], F32) for l in range(2)]
    return P


def conv_job(S, P, l, engs, small_only=False, bw=1024, nbuf=4):
    stf = [S.sb(f'cvf{l}{i}', [128, bw], F32) for i in range(nbuf)]
    stb = [S.sb(f'cvb{l}{i}', [128, bw], BF16) for i in range(nbuf)]
    if small_only:
        jobs = ((P.glu_w_b, P.s5_glu_w, 256, 256),)
    else:
        jobs = ((P.w_out_b, P.w_out, D, D), (P.w_up_b, P.ffn_w_up, D, 2 * DFF), (P.w_dn_b, P.ffn_w_down, DFF, D))
    blocks = []
    for (dst, src, R_, C_) in jobs:
        for r0 in range(0, R_, 128):
            for c0 in range(0, C_, bw):
                blocks.append((dst, src, r0, c0, min(bw, C_ - c0)))

    def store(i):
        dst, src, r0, c0, cw_ = blocks[i]
        b_ = stb[i % nbuf]
        S.dma(dst[l, r0:r0 + 128, c0:c0 + cw_], b_[:, 0:cw_], [b_], [dst.s((l, r0, c0))])
    for i, (dst, src, r0, c0, cw_) in enumerate(blocks):
        f_ = stf[i % nbuf]
        b_ = stb[i % nbuf]
        S.dma(f_[:, 0:cw_], src[l, r0:r0 + 128, c0:c0 + cw_], [], [f_])
        S.cp(engs[i % len(engs)], b_[:, 0:cw_], f_[:, 0:cw_], [f_], [b_])
        if i >= 2:
            store(i - 2)
        yield
    for i in range(max(0, len(blocks) - 2), len(blocks)):
        store(i)
    yield


def mod_body(S, P, l, sw=512):
    scs = P.scs
    wsl = [S.sb(f'mwsl{l}{i}', [128, 8, sw], F32) for i in range(2)]
    pm = S.ps(f'pmod{l}')
    mb = S.sb(f'mb{l}', [128, 48], F32)
    g1 = S.sb(f'g1{l}', [128, 8], F32)
    g2 = S.sb(f'g2{l}', [128, 8], F32)
    tmp = S.sb(f'mtmp{l}', [128, 8, 3], F32)
    n = 0
    for cs in range(6144 // sw):
        w = wsl[n % 2]
        n += 1
        S.dma(w[:], P.mod_w[l, :, cs * sw:(cs + 1) * sw].rearrange("(k p) c -> p k c", p=128), [], [w])
        for j in range(sw // 128):
            fc = cs * (sw // 128) + j
            for k in range(8):
                S.mm(pm[:, fc * 3:(fc + 1) * 3], w[:, k, j * 128:(j + 1) * 128], scs[:, k, :], k == 0, k == 7, [w, scs], [pm])
        yield
    S.dma(mb[:], P.mod_bT[l], [], [mb])
    S.tt('dve', P.mod[l][:], pm[:, 0:144].rearrange("p (j c) -> p j c", c=3),
         AP(mb, 0, [[48, 128], [1, 48], [0, 3]]), ALU.add, [pm, mb], [P.mod[l]])
    S.dma(g1[:], P.norm1_gT[l], [], [g1])
    S.dma(g2[:], P.norm2_gT[l], [], [g2])
    for (A, g, j0) in ((P.A1[l], g1, 8), (P.A2[l], g2, 32)):
        S.ts('dve', tmp[:], P.mod[l][:, j0:j0 + 8, :], 1.0, None, ALU.add, None, [P.mod[l]], [tmp])
        S.tt('dve', A[:], tmp[:], AP(g, 0, [[8, 128], [1, 8], [0, 3]]), ALU.mult, [tmp, g], [A])
    yield


def stage_p0(P):
    K = P.K
    S = K.stage("p0")
    S.dma(P.ident_b[:], P.k_ident_b[:], [P.k_ident_b], [P.ident_b])
    S.dma(P.ident_f[:], P.k_ident_f[:], [P.k_ident_f], [P.ident_f])
    S.dma(P.ones_f[:], P.k_ones_f[:], [P.k_ones_f], [P.ones_f])
    for l_ in range(2):
        for _ in conv_job(S, P, l_, ('dve', 'pool'), small_only=True, bw=256, nbuf=2):
            pass
    cts = S.sb('cts', [128, 8, 3], F32)
    S.dma(cts[:], P.cT[:], [P.cT], [cts])
    S.act(P.scs[:], cts[:], AF.Silu, [cts], [P.scs])
    for _ in mod_body(S, P, 0):
        pass
    S.emit()


def norm_mod(S, P, h, xn, A, shift_j0, l, col, sq, pss, rstd, tmpn):
    S.act(sq[:], h[:], AF.Square, [h], [sq])
    for k in range(8):
        S.mm(pss[:], P.ones_f[:], sq[:, k, :], k == 0, k == 7, [P.ones_f, sq], [pss])
    S.act(rstd[:], pss[:], AF.Sqrt, [pss], [rstd], scale=1.0 / D, bias=EPS)
    S.op('dve', lambda e: e.reciprocal(out=rstd[:], in_=rstd[:]), [rstd], [rstd])
    for k in range(8):
        S.tt('dve', tmpn[:, k, :], h[:, k, :], rstd[:], ALU.mult, [h, rstd], [tmpn.s(k)])
        S.act(xn[:, k, :], tmpn[:, k, :], AF.Identity, [tmpn.s(k), A, P.mod[l]], [xn.s(k)],
              scale=A[:, k, col:col + 1], bias=P.mod[l][:, shift_j0 + k, col:col + 1])


IP_CHUNKS = (C_Z + C_X + [C_B, C_C, C_DT] + [(OFF_HY + 128 * i, 128) for i in range(6)]
             + [(OFF_RET + 128 * i, 128) for i in range(8)] + [(OFF_S5 + 128 * i, 128) for i in range(2)])


def stage_ip(P, l):
    K = P.K
    S = K.stage(f"ip{l}")
    win = S.sb('win', [128, 8, DIN], BF16)
    wst = [S.sb(f'wst{i}', [128, DIN], F32) for i in range(2)]
    for k in range(8):
        S.dma(wst[k % 2][:], P.w_in[l, k * 128:(k + 1) * 128, :], [], [wst[k % 2]])
        S.cp(('act', 'dve')[k % 2], win[:, k, :], wst[k % 2][:], [wst[k % 2]], [win.s(k)])
    hb = [S.sb(f'h{i}', [128, 8, 512], F32) for i in range(2)]
    sq = S.sb('sq', [128, 8, 512], F32)
    tmpn = S.sb('tmpn', [128, 8, 512], F32)
    xnb = [S.sb(f'xn{i}', [128, 8, 512], BF16) for i in range(2)]
    rstd = S.sb('rstd', [128, 512], F32)
    pss = S.ps('pss')
    pu = [S.ps(f'pu{i}') for i in range(4)]
    ust = [S.sb(f'ust{i}', [128, 512], BF16) for i in range(4)]
    udts = S.sb('udts', [8, 512], F32)
    u8s = [S.sb(f'u8s{i}', [128, 8, 64], BF16) for i in range(2)]
    m = 0

    def prep(ti):
        h = hb[ti % 2]
        hsrc = P.xinT if l == 0 else P.hT
        S.dma(h[:], hsrc[:, ti * 512:(ti + 1) * 512].rearrange("(k p) t -> p k t", p=128), [], [h])
        norm_mod(S, P, h, xnb[ti % 2], P.A1[l], 0, l, tile_col(ti), sq, pss, rstd, tmpn)
    prep(0)
    for ti in range(NT):
        xn = xnb[ti % 2]
        if ti + 1 < NT:
            prep(ti + 1)
        xr = [xn.s(k) for k in range(8)]
        for (c0, M) in IP_CHUNKS:
            p_ = pu[m % 4]
            u_ = ust[m % 4]
            m += 1
            for k in range(8):
                S.mm(p_[0:M, :], win[:, k, c0:c0 + M], xn[:, k, :], k == 0, k == 7, [win.s(k), xr[k]], [p_])
            if (c0, M) == C_DT:
                S.cp('dve', udts[:], p_[0:8, :], [p_], [udts])
                S.dma(P.udt[:, ti * 512:(ti + 1) * 512], udts[:], [udts], [P.udt.s(ti)])
            else:
                S.cp('act' if m % 2 else 'dve', u_[0:M, :], p_[0:M, :], [p_], [u_])
                S.dma(P.uT[c0:c0 + M, ti * 512:(ti + 1) * 512], u_[0:M, :], [u_], [P.uT.s((c0, ti))])
                if c0 >= OFF_S5:
                    u8_ = u8s[m % 2]
                    S.cp('pool', u8_[:], u_[:, :].rearrange("p (a j) -> p j a", j=8), [u_], [u8_])
                    S.dma(P.u8T[c0 - OFF_S5:c0 - OFF_S5 + 128, :, ti * 64:(ti + 1) * 64], u8_[:], [u8_], [P.u8T.s((c0, ti))])
    S.emit()

import os
KCUT = os.environ.get('KCUT', '')


class CutStage(Exception):
    pass


def cutpt(name):
    if KCUT == name:
        raise CutStage()


NCH = (LC + LL) // 128
LB = LC + LL
FWD_CHAIN = list(range(NCH))
BWD_CHAIN = [1, 0] + list(range(NCH - 1, 1, -1))


def load_seq(S, dst_ap_fn, P, row0, nrows, b, dstbuf, eng='sp'):
    S.dma(dst_ap_fn(0, LC), P.uT[row0:row0 + nrows, ctx_off(b):ctx_off(b) + LC], [P.uT], [dstbuf], eng=eng)
    S.dma(dst_ap_fn(LC, LB), P.uT[row0:row0 + nrows, lat_off(b):lat_off(b) + LL], [P.uT], [dstbuf], eng=eng)


def store_mix(S, P, mixs, row0, b, skip_ctx=False):
    for k in range(2):
        if not skip_ctx:
            S.dma(P.mixT[row0 + k * 128:row0 + (k + 1) * 128, ctx_off(b):ctx_off(b) + LC], mixs[:, k, 0:LC], [mixs], [P.mixT.s((row0, k, b, 0))])
        S.dma(P.mixT[row0 + k * 128:row0 + (k + 1) * 128, lat_off(b):lat_off(b) + LL], mixs[:, k, LC:LB], [mixs], [P.mixT.s((row0, k, b, 1))])


def stage_ssd(P, l):
    K = P.K
    S = K.stage(f"ssd{l}")
    try:
        _stage_ssd(P, l, S)
    except CutStage:
        pass
    S.emit()


def _stage_ssd(P, l, S):
    K = P.K
    ule = S.sb('ule', [128, 128], F32)
    uge = S.sb('uge', [128, 128], F32)
    S.dma(ule[:], P.k_ule[:], [P.k_ule], [ule])
    S.dma(uge[:], P.k_uge[:], [P.k_uge], [uge])
    mneg4 = S.sb('mneg4', [128, 2, 512], BF16)
    S.dma(mneg4[:], P.k_mneg4[:].rearrange("p r h i -> p r (h i)"), [], [mneg4])
    rbs = [S.sb(f'rb{i}', [8, 4, 128], F32) for i in range(4)]
    cw = S.sb('cw', [128, 4, 4], F32)
    S.dma(cw[:], P.ssd_cw[l], [P.ssd_cw], [cw])
    dtb = S.sb('dtb', [8, 1], F32)
    S.dma(dtb[:], P.ssd_dtb8[l], [P.ssd_dtb8], [dtb])
    a_bc = S.sb('a_bc', [128, 8], F32)
    S.dma(a_bc[:], AP(P.ssd_alog8, l * 8, [[0, 128], [1, 8]]), [P.ssd_alog8], [a_bc])
    S.act(a_bc[:], a_bc[:], AF.Exp, [a_bc], [a_bc])
    S.ts('dve', a_bc[:], a_bc[:], -1.0, None, ALU.mult, None, [a_bc], [a_bc])
    dsk = S.sb('dsk', [128, 256], F32)
    gnm = S.sb('gnm', [128, 256], F32)
    S.dma(dsk[:], AP(P.ssd_dexp, l * 256, [[0, 128], [1, 256]]), [P.ssd_dexp], [dsk])
    S.dma(gnm[:], AP(P.ssd_norm_g, l * 256, [[0, 128], [1, 256]]), [P.ssd_norm_g], [gnm])
    xr = S.sb('xr', [128, 2, LB], BF16)
    br = S.sb('br', [64, LB], BF16)
    cr = S.sb('cr', [64, LB], BF16)
    zr = S.sb('zr', [128, 2, LB], BF16)
    dtr = S.sb('dtr', [8, LB], F32)
    acc = S.sb('acc', [128, LB], F32)
    xa = S.sb('xa', [128, 2, LB], BF16)
    ba = S.sb('ba', [64, LB], BF16)
    ca = S.sb('ca', [64, LB], BF16)
    xtok = S.sb('xtok', [128, NCH, 256], BF16)
    btok = S.sb('btok', [128, NCH, 64], BF16)
    zs = S.sb('zs', [128, NCH, 256], BF16)
    dtk = S.sb('dtk', [128, NCH, 8], F32)
    lak = S.sb('lak', [128, 8], F32)
    acsk = S.sb('acsk', [128, NCH, 8], F32)
    acsT = S.sb('acsT', [8, NCH, 256], F32)
    etot = S.sb('etot', [64, NCH, 8], F32)
    gmt = S.sb('gmt', [128, NCH, 2, 128], F32)
    sball = S.sb('sball', [64, NCH, 256], F32)
    hinf = S.sb('hinf', [64, NCH, 256], BF16)
    hinb = S.sb('hinb', [64, NCH, 256], BF16)
    hst = S.sb('hst', [64, 256], F32)
    tmp8 = S.sb('tmp8', [128, 8], F32)
    wcol = S.sb('wcol', [128, 8], F32)
    xw = S.sb('xw', [128, 4, 2, 64], BF16)
    mixs = S.sb('mixs', [128, 2, LB], BF16)
    T1 = [S.sb(f'T1{i}', [128, 4, 128], F32) for i in range(2)]
    ST = [S.sb(f'ST{i}', [128, 4, 128], BF16) for i in range(4)]
    Ee = [S.sb(f'E{i}', [64, 4, 128], F32) for i in range(2)]
    CsT = [S.sb(f'CsT{i}', [64, 4, 128], BF16) for i in range(4)]
    acsk2 = S.sb('acsk2', [128, NCH, 8], F32)
    lnd = S.sb('lnd', [128, 8], F32)
    y1 = S.sb('y1', [128, 256], F32)
    y2 = S.sb('y2', [128, 256], F32)
    y3 = S.sb('y3', [128, 256], BF16)
    junk = S.sb('junk', [128, 256], F32)
    ss = S.sb('ss', [128, 1], F32)
    ptb = S.ps('ptb', [128, 1024], BF16)
    ptb2 = S.ps('ptb2', [128, 1024], BF16)
    ptf = S.ps('ptf')
    ptf2 = S.ps('ptf2')
    pg = S.ps('pg')
    pst = S.ps('pst')
    pa = [S.ps(f'pa{i}') for i in range(2)]
    py = pst

    for b in range(NB):
        for k in range(2):
            load_seq(S, lambda a, e, k=k: xr[:, k, a:e], P, C_X[k][0], 128, b, xr.s(k))
            load_seq(S, lambda a, e, k=k: zr[:, k, a:e], P, C_Z[k][0], 128, b, zr.s(k))
        load_seq(S, lambda a, e: br[:, a:e], P, C_B[0], 64, b, br)
        load_seq(S, lambda a, e: cr[:, a:e], P, C_C[0], 64, b, cr)
        S.dma(dtr[:, 0:LC], P.udt[:, ctx_off(b):ctx_off(b) + LC], [P.udt], [dtr])
        S.dma(dtr[:, LC:LB], P.udt[:, lat_off(b):lat_off(b) + LL], [P.udt], [dtr])
        cutpt('load')
        for (src, srcb, dst, dstb, ci, np_) in ((lambda a, e: xr[:, 0, a:e], xr.s(0), lambda a, e: xa[:, 0, a:e], xa.s(0), 0, 128),
                                                 (lambda a, e: xr[:, 1, a:e], xr.s(1), lambda a, e: xa[:, 1, a:e], xa.s(1), 1, 128),
                                                 (lambda a, e: br[:, a:e], br, lambda a, e: ba[:, a:e], ba, 2, 64),
                                                 (lambda a, e: cr[:, a:e], cr, lambda a, e: ca[:, a:e], ca, 3, 64)):
            for (a, e) in ((0, LC), (LC, LB)):
                S.ts('dve', acc[0:np_, a:e], src(a, e), cw[0:np_, ci, 1:2], cw[0:np_, ci, 3:4], ALU.mult, ALU.add, [srcb, cw], [acc])
                S.stt(acc[0:np_, a + 1:e], src(a, e - 1), cw[0:np_, ci, 0:1], acc[0:np_, a + 1:e], ALU.mult, ALU.add, [srcb, cw, acc], [acc])
                S.stt(acc[0:np_, a:e - 1], src(a + 1, e), cw[0:np_, ci, 2:3], acc[0:np_, a:e - 1], ALU.mult, ALU.add, [srcb, cw, acc], [acc])
            S.act(dst(0, LB)[0:np_], acc[0:np_, :], AF.Silu, [acc], [dstb])
        cutpt('conv')
        S.act(dtr[:], dtr[:], AF.Exp, [dtr, dtb], [dtr], bias=dtb[:, 0:1], scale=1.0)
        S.act(dtr[:], dtr[:], AF.Ln, [dtr], [dtr], bias=1.0, scale=1.0)
        for k in range(2):
            S.act(zr[:, k, :], zr[:, k, :], AF.Silu, [zr.s(k)], [zr.s(k)])
        cutpt('dt')
        S.ms('dve', hst[:], 0.0, [hst])
        xws = [xw, S.sb(f'xwb{b}', [128, 4, 2, 64], BF16)]

        def ssd_a1(ci):
            xw = xws[ci % 2]
            c0, c1 = ci * 128, (ci + 1) * 128
            for k in range(2):
                S.tr(ptb[:, k * 128:(k + 1) * 128], xa[:, k, c0:c1], P.ident_b[:], [xa.s(k), P.ident_b], [ptb])
            S.tr(ptb[:, 256:320], ba[0:64, c0:c1], P.ident_b[0:64, 0:64], [ba, P.ident_b], [ptb])
            S.cp('act', xtok[:, ci, :], ptb[:, 0:256], [ptb], [xtok.s(ci)])
            S.cp('act', btok[:, ci, :], ptb[:, 256:320], [ptb], [btok.s(ci)])
            for k in range(2):
                S.tr(ptb2[:, k * 128:(k + 1) * 128], zr[:, k, c0:c1], P.ident_b[:], [zr.s(k), P.ident_b], [ptb2])
            S.cp('act', zs[:, ci, :], ptb2[:, 0:256], [ptb2], [zs.s(ci)])
            S.tr(ptf[:, 0:8], dtr[0:8, c0:c1], P.ident_f[0:8, 0:8], [dtr, P.ident_f], [ptf])
            S.cp('dve', dtk[:, ci, :], ptf[:, 0:8], [ptf], [dtk.s(ci)])
            S.tt('dve', lak[:], dtk[:, ci, :], a_bc[:], ALU.mult, [dtk.s(ci), a_bc], [lak])
            S.mm(ptf[:, 8:12], ule[:], lak[:, 0:4], True, True, [ule, lak], [ptf])
            S.mm(ptf[:, 12:16], uge[:], lak[:, 4:8], True, True, [uge, lak], [ptf])
            S.mm(ptf[:, 16:24], P.ones_f[:], lak[:], True, True, [P.ones_f, lak], [ptf])
            S.mm(ptf2[0:8, 0:128], lak[:], ule[:], True, True, [ule, lak], [ptf2])
            S.mm(ptf2[0:8, 128:256], lak[:], uge[:], True, True, [uge, lak], [ptf2])
            S.cp('dve', acsk[:, ci, :], ptf[:, 8:16], [ptf], [acsk.s(ci)])
            S.cp('act', acsT[:, ci, :], ptf2[0:8, 0:256], [ptf2], [acsT.s(ci)])
            S.act(lnd[:], dtk[:, ci, :], AF.Ln, [dtk.s(ci)], [lnd])
            S.tt('dve', acsk2[:, ci, :], acsk[:, ci, :], lnd[:], ALU.subtract, [acsk.s(ci), lnd], [acsk2.s(ci)])
            S.tt('dve', tmp8[:], ptf[:, 16:24], acsk[:, ci, :], ALU.subtract, [ptf, acsk.s(ci)], [tmp8])
            S.act(tmp8[:], tmp8[:], AF.Exp, [tmp8], [tmp8])
            S.tt('dve', wcol[:], tmp8[:], dtk[:, ci, :], ALU.mult, [tmp8, dtk.s(ci)], [wcol])
            S.act(etot[:, ci, :], ptf[0:64, 16:24], AF.Exp, [ptf], [etot.s(ci)])
            S.mm(pg[:, 0:128], ba[0:64, c0:c1], ca[0:64, c0:c1], True, True, [ba, ca], [pg])
            S.tt('dve', gmt[:, ci, 0, :], pg[:, 0:128], ule[:], ALU.mult, [pg, ule], [gmt.s(ci)])
            S.tt('dve', gmt[:, ci, 1, :], pg[:, 0:128], uge[:], ALU.mult, [pg, uge], [gmt.s(ci)])
            S.tt('dve', xw[:], AP(xtok, ci * 256, [[NCH * 256, 128], [64, 4], [0, 2], [1, 64]]),
                 AP(wcol, 0, [[8, 128], [1, 4], [4, 2], [0, 64]]), ALU.mult, [xtok.s(ci), wcol], [xw])

        def ssd_a2(ci):
            xw = xws[ci % 2]
            c0, c1 = ci * 128, (ci + 1) * 128
            for h in range(4):
                S.mm(pst[0:64, h * 128:(h + 1) * 128], btok[:, ci, :], xw[:, h, :, :].rearrange("p r q -> p (r q)"), True, True, [btok.s(ci), xw], [pst])
            S.cp('act', hinf[:, ci, :], hst[:], [hst], [hinf.s(ci)])
            S.tt('dve', hst[:].rearrange("n (h p) -> n h p", h=4), hst[:].rearrange("n (h p) -> n h p", h=4),
                 AP(etot, ci * 8, [[NCH * 8, 64], [1, 4], [0, 64]]), ALU.mult, [hst, etot.s(ci)], [hst])
            S.tt('dve', hst[:].rearrange("n (h p) -> n h p", h=4), AP(pst, 0, [[512, 64], [128, 4], [1, 64]]),
                 hst[:].rearrange("n (h p) -> n h p", h=4), ALU.add, [hst, pst], [hst])
            S.cp('act', sball[:, ci, :].rearrange("n (h p) -> n h p", h=4), AP(pst, 64, [[512, 64], [128, 4], [1, 64]]), [pst], [sball.s(ci)])

        ssd_a1(0)
        for ci in FWD_CHAIN:
            if ci + 1 < NCH:
                ssd_a1(ci + 1)
            ssd_a2(ci)
        cutpt('passA')
        S.ms('dve', hst[:], 0.0, [hst])
        for ci in BWD_CHAIN:
            S.cp('act', hinb[:, ci, :], hst[:], [hst], [hinb.s(ci)])
            S.tt('dve', hst[:].rearrange("n (h p) -> n h p", h=4), hst[:].rearrange("n (h p) -> n h p", h=4),
                 AP(etot, ci * 8 + 4, [[NCH * 8, 64], [1, 4], [0, 64]]), ALU.mult, [hst, etot.s(ci)], [hst])
            S.tt('dve', hst[:], hst[:], sball[:, ci, :], ALU.add, [hst, sball.s(ci)], [hst])
        cutpt('bwd')
        pes = [ptf2, pg]
        pys = [pst, ptf]

        def ssd_rb(ci):
            for r in range(2):
                rb = rbs[2 * (ci % 2) + r]
                S.tt('dve', rb[:], AP(acsT, ci * 256 + r * 128, [[NCH * 256, 8], [0, 4], [1, 128]]),
                     AP(P.ident_f, 4 * r, [[128, 8], [1, 4], [0, 128]]), ALU.mult, [acsT.s(ci), P.ident_f], [rb])

        def ssd_front(ci):
            c0, c1 = ci * 128, (ci + 1) * 128
            for r in range(2):
                p_ = pa[r]
                pe_ = pes[r]
                t1 = T1[r]
                e_ = Ee[r]
                rb = rbs[2 * (ci % 2) + r]
                st = ST[(2 * (ci % 2) + r)]
                cs = CsT[(2 * (ci % 2) + r)]
                rb2 = rb[:].rearrange("c h i -> c (h i)")
                S.mm(p_[:, 0:512], P.ones_f[0:8, :], rb2, True, False, [P.ones_f, rb], [p_])
                S.mm(p_[:, 0:512], P.ident_b[:], mneg4[:, r, :], False, True, [P.ident_b, mneg4], [p_])
                S.mm(pe_[0:64, 0:512], P.ones_f[0:8, 0:64], rb2, True, True, [P.ones_f, rb], [pe_])
                pv = p_[:].rearrange("p (h i) -> p h i", h=4)
                S.tt('dve', t1[:], pv, AP(acsk2, ci * 8 + 4 * r, [[NCH * 8, 128], [1, 4], [0, 128]]), ALU.subtract, [p_, acsk2.s(ci)], [t1])
                S.act(t1[:], t1[:], AF.Exp, [t1], [t1])
                S.tt('dve', st[:], t1[:], AP(gmt, (ci * 2 + r) * 128, [[NCH * 256, 128], [0, 4], [1, 128]]), ALU.mult, [t1, gmt.s(ci)], [st])
                S.act(e_[:], pe_[0:64, :].rearrange("p (h i) -> p h i", h=4), AF.Exp, [pe_], [e_])
                S.tt('pool', cs[:], e_[:], AP(ca, c0, [[LB, 64], [0, 4], [1, 128]]), ALU.mult, [ca, e_], [cs])

        y3s = [y3, S.sb(f'y3b{b}', [128, 256], BF16)]

        def ssd_mm(ci):
            py = pys[ci % 2]
            sts = [ST[2 * (ci % 2)], ST[2 * (ci % 2) + 1]]
            css = [CsT[2 * (ci % 2)], CsT[2 * (ci % 2) + 1]]
            for h in range(4):
                hs_ = slice(h * 64, (h + 1) * 64)
                S.mm(py[:, hs_], sts[0][:, h, :], xtok[:, ci, hs_], True, False, [sts[0], xtok.s(ci)], [py])
                S.mm(py[:, hs_], css[0][:, h, :], hinf[:, ci, hs_], False, False, [css[0], hinf.s(ci)], [py])
                S.mm(py[:, hs_], sts[1][:, h, :], xtok[:, ci, hs_], False, False, [sts[1], xtok.s(ci)], [py])
                S.mm(py[:, hs_], css[1][:, h, :], hinb[:, ci, hs_], False, True, [css[1], hinb.s(ci)], [py])

        def ssd_epi(ci):
            py = pys[ci % 2]
            S.tt('pool', junk[:], xtok[:, ci, :], dsk[:], ALU.mult, [xtok.s(ci), dsk], [junk])
            S.tt('dve', y1[:], py[:, 0:256], junk[:], ALU.add, [py, junk], [y1])
            S.tt('dve', y2[:], y1[:], zs[:, ci, :], ALU.mult, [y1, zs.s(ci)], [y2])
            S.act(junk[:], y2[:], AF.Square, [y2], [junk, ss], accum_out=ss[:])
            S.act(ss[:], ss[:], AF.Ln, [ss], [ss], scale=1.0 / 256, bias=EPS)
            S.act(ss[:], ss[:], AF.Exp, [ss], [ss], scale=-0.5)
            S.stt(y3s[ci % 2][:], y2[:], ss[:, 0:1], gnm[:], ALU.mult, ALU.mult, [y2, ss, gnm], [y3s[ci % 2]])

        def ssd_tail(ci):
            c0, c1 = ci * 128, (ci + 1) * 128
            y3_ = y3s[ci % 2]
            for k in range(2):
                S.tr(ptb[:, 512 + k * 128:512 + (k + 1) * 128], y3_[:, k * 128:(k + 1) * 128], P.ident_b[:], [y3_, P.ident_b], [ptb])
            S.cp('act', mixs[:, :, c0:c1], ptb[:, 512:768].rearrange("p (k t) -> p k t", k=2), [ptb], [mixs])

        c_lo = 2 if l == 1 else 0
        ssd_rb(c_lo)
        ssd_rb(c_lo + 1)
        ssd_front(c_lo)
        ssd_rb(c_lo + 2)
        ssd_front(c_lo + 1)
        ssd_mm(c_lo)
        for ci in range(c_lo, NCH):
            if ci + 3 < NCH:
                ssd_rb(ci + 3)
            if ci + 2 < NCH:
                ssd_front(ci + 2)
            if ci + 1 < NCH:
                ssd_mm(ci + 1)
            ssd_epi(ci)
            if ci >= c_lo + 1:
                ssd_tail(ci - 1)
        ssd_tail(NCH - 1)
        store_mix(S, P, mixs, 0, b, skip_ctx=(l == 1))


def stage_ret(P, l):
    K = P.K
    S = K.stage(f"ret{l}")
    scale = 64 ** -0.5
    ule = S.sb('ule', [128, 128], F32)
    uge = S.sb('uge', [128, 128], F32)
    idf = S.sb('idf', [128, 128], F32)
    ramp = S.sb('ramp', [128, 128], F32)
    pidx = S.sb('pidx', [128, 1], F32)
    S.dma(ule[:], P.k_ule[:], [P.k_ule], [ule])
    S.dma(uge[:], P.k_uge[:], [P.k_uge], [uge])
    S.dma(idf[:], P.k_idiff[:], [P.k_idiff], [idf])
    S.dma(ramp[:], P.k_ramp[:], [P.k_ramp], [ramp])
    S.dma(pidx[:], P.k_pidx[:], [P.k_pidx], [pidx])
    rcos = S.sb('rcos', [64, LL], F32)
    rsin = S.sb('rsin', [64, LL], F32)
    rp = S.sb('rp', [64, 64], BF16)
    S.dma(rcos[:], P.k_rope_cos[0:64, :], [P.k_rope_cos], [rcos])
    S.dma(rsin[:], P.k_rope_sin[0:64, :], [P.k_rope_sin], [rsin])
    S.dma(rp[:], P.k_rope_p[0:64, 0:64], [P.k_rope_p], [rp])
    lg = S.sb('lg', [128, 8], F32)
    S.dma(lg[:], AP(P.ret_decay8, l * 8, [[0, 128], [1, 8]]), [P.ret_decay8], [lg])
    S.act(lg[:], lg[:], AF.Exp, [lg], [lg])
    S.ts('dve', lg[:], lg[:], -1.0, None, ALU.mult, None, [lg], [lg])
    Dm = S.sb('Dm', [128, 8, 128], F32)
    Ec = S.sb('Ec', [64, 8, 128], F32)
    wc = S.sb('wc', [128, 8], F32)
    et = S.sb('et', [64, 8], F32)
    tmpd = S.sb('tmpd', [128, 128], F32)
    tmpc = S.sb('tmpc', [128, 1], F32)
    for r in range(2):
        for h in range(4):
            c8 = 4 * r + h
            sgn = 1.0 if r == 0 else -1.0
            S.ts('dve', tmpd[:], idf[:], lg[:, c8:c8 + 1], sgn, ALU.mult, ALU.mult, [idf, lg], [tmpd])
            S.ts('dve', tmpd[:], tmpd[:], 0.0, None, ALU.min, None, [tmpd], [tmpd])
            S.act(tmpd[:], tmpd[:], AF.Exp, [tmpd], [tmpd])
            S.stt(Dm[:, c8, :], tmpd[:], scale, (ule if r == 0 else uge)[:], ALU.mult, ALU.mult, [tmpd, ule, uge], [Dm])
            if r == 0:
                S.ts('dve', tmpd[0:64, :], ramp[0:64, :], lg[0:64, c8:c8 + 1], None, ALU.mult, None, [ramp, lg], [tmpd])
            else:
                S.ts('dve', tmpd[0:64, :], ramp[0:64, :], -1.0, 129.0, ALU.mult, ALU.add, [ramp], [tmpd])
                S.ts('dve', tmpd[0:64, :], tmpd[0:64, :], lg[0:64, c8:c8 + 1], None, ALU.mult, None, [tmpd, lg], [tmpd])
            S.act(Ec[:, c8, :], tmpd[0:64, :], AF.Exp, [tmpd], [Ec])
            if r == 0:
                S.ts('dve', tmpc[:], pidx[:], -1.0, 127.0, ALU.mult, ALU.add, [pidx], [tmpc])
                S.tt('dve', tmpc[:], tmpc[:], lg[:, c8:c8 + 1], ALU.mult, [tmpc, lg], [tmpc])
            else:
                S.tt('dve', tmpc[:], pidx[:], lg[:, c8:c8 + 1], ALU.mult, [pidx, lg], [tmpc])
            S.act(wc[:, c8:c8 + 1], tmpc[:], AF.Exp, [tmpc], [wc])
    S.ts('dve', wc[:], wc[:], scale, None, ALU.mult, None, [wc], [wc])
    S.ts('dve', et[:], lg[0:64, :], 128.0, None, ALU.mult, None, [lg], [et])
    S.act(et[:], et[:], AF.Exp, [et], [et])
    qh = S.sb('qh', [64, 4, LB], BF16)
    kh = S.sb('kh', [64, 4, LB], BF16)
    qa = qh
    ka = kh
    vr = S.sb('vr', [128, 2, LB], BF16)
    gr = S.sb('gr', [128, 2, LB], BF16)
    vtok = S.sb('vtok', [128, NCH, 256], BF16)
    ktok = S.sb('ktok', [128, NCH, 256], BF16)
    gs = S.sb('gs', [128, NCH, 256], BF16)
    sball = S.sb('sball', [64, NCH, 256], F32)
    hinf = S.sb('hinf', [64, NCH, 256], BF16)
    hinb = S.sb('hinb', [64, NCH, 256], BF16)
    hst = S.sb('hst', [64, 256], F32)
    xw = S.sb('xw', [128, 4, 2, 64], BF16)
    mixs = S.sb('mixs', [128, 2, LB], BF16)
    rt1 = S.sb('rt1', [64, 512], F32)
    rt2 = S.sb('rt2', [64, 512], F32)
    rt1b = S.sb('rt1b', [64, 512], F32)
    rt2b = S.sb('rt2b', [64, 512], F32)
    ST = [S.sb(f'ST{i}', [128, 4, 128], BF16) for i in range(4)]
    CsT = [S.sb(f'CsT{i}', [64, 4, 128], BF16) for i in range(4)]
    ysb = S.sb('ysb', [128, 4, 64], F32)
    yc = S.sb('yc', [128, 4, 64], F32)
    ysq = S.sb('ysq', [128, 4, 64], F32)
    s1 = S.sb('s1', [128, 4], F32)
    s2 = S.sb('s2', [128, 4], F32)
    s3 = S.sb('s3', [128, 4], F32)
    y3 = S.sb('y3', [128, 256], BF16)
    ptb = S.ps('ptb', [128, 1024], BF16)
    ptb2 = S.ps('ptb2', [128, 1024], BF16)
    prp = S.ps('prp')
    pg = S.ps('pg')
    pst = S.ps('pst')
    py = S.ps('py')

    for b in range(NB):
        for h in range(4):
            load_seq(S, lambda a, e, h=h: qh[:, h, a:e], P, OFF_RET + 64 * h, 64, b, qh.s(h))
            load_seq(S, lambda a, e, h=h: kh[:, h, a:e], P, OFF_RET + 256 + 64 * h, 64, b, kh.s(h))
        for k in range(2):
            load_seq(S, lambda a, e, k=k: vr[:, k, a:e], P, OFF_RET + 512 + 128 * k, 128, b, vr.s(k))
            load_seq(S, lambda a, e, k=k: gr[:, k, a:e], P, OFF_RET + 768 + 128 * k, 128, b, gr.s(k))
        for (src, dst) in ((qh, qa), (kh, ka)):
            for h in range(4):
                for tt_ in range(4):
                    a, e = LC + tt_ * 512, LC + (tt_ + 1) * 512
                    prp_ = (prp, pg, pst, py)[tt_]
                    r1_ = (rt1, rt1b)[tt_ % 2]
                    r2_ = (rt2, rt2b)[tt_ % 2]
                    S.mm(prp_[0:64, :], rp[:], src[:, h, a:e], True, True, [rp, src.s(h)], [prp_])
                    S.tt('dve', r1_[:], src[:, h, a:e], rcos[:, tt_ * 512:(tt_ + 1) * 512], ALU.mult, [src.s(h), rcos], [r1_])
                    S.tt('dve', r2_[:], prp_[0:64, :], rsin[:, tt_ * 512:(tt_ + 1) * 512], ALU.mult, [prp_, rsin], [r2_])
                    S.tt('pool', dst[:, h, a:e], r1_[:], r2_[:], ALU.add, [r1_, r2_], [dst.s(h)])
        for k in range(2):
            S.act(gr[:, k, :], gr[:, k, :], AF.Silu, [gr.s(k)], [gr.s(k)])
        S.ms('dve', hst[:], 0.0, [hst])
        xws = [xw, S.sb(f'xwb{b}', [128, 4, 2, 64], BF16)]

        def ret_a1(ci):
            xw = xws[ci % 2]
            c0, c1 = ci * 128, (ci + 1) * 128
            for k in range(2):
                S.tr(ptb[:, k * 128:(k + 1) * 128], vr[:, k, c0:c1], P.ident_b[:], [vr.s(k), P.ident_b], [ptb])
            for h in range(4):
                S.tr(ptb[:, 256 + h * 64:256 + (h + 1) * 64], ka[0:64, h, c0:c1], P.ident_b[0:64, 0:64], [ka.s(h), P.ident_b], [ptb])
            S.cp('act', vtok[:, ci, :], ptb[:, 0:256], [ptb], [vtok.s(ci)])
            S.cp('dve', ktok[:, ci, :], ptb[:, 256:512], [ptb], [ktok.s(ci)])
            for k in range(2):
                S.tr(ptb2[:, k * 128:(k + 1) * 128], gr[:, k, c0:c1], P.ident_b[:], [gr.s(k), P.ident_b], [ptb2])
            S.cp('act', gs[:, ci, :], ptb2[:, 0:256], [ptb2], [gs.s(ci)])
            S.tt('dve', xw[:], AP(vtok, ci * 256, [[NCH * 256, 128], [64, 4], [0, 2], [1, 64]]),
                 AP(wc, 0, [[8, 128], [1, 4], [4, 2], [0, 64]]), ALU.mult, [vtok.s(ci), wc], [xw])

        def ret_a2(ci):
            xw = xws[ci % 2]
            for h in range(4):
                S.mm(pst[0:64, h * 128:(h + 1) * 128], ktok[:, ci, h * 64:(h + 1) * 64], xw[:, h, :, :].rearrange("p r q -> p (r q)"), True, True, [ktok.s(ci), xw], [pst])
            S.cp('act', hinf[:, ci, :], hst[:], [hst], [hinf.s(ci)])
            S.tt('dve', hst[:].rearrange("n (h p) -> n h p", h=4), hst[:].rearrange("n (h p) -> n h p", h=4),
                 AP(et, 0, [[8, 64], [1, 4], [0, 64]]), ALU.mult, [hst, et], [hst])
            S.tt('dve', hst[:].rearrange("n (h p) -> n h p", h=4), AP(pst, 0, [[512, 64], [128, 4], [1, 64]]),
                 hst[:].rearrange("n (h p) -> n h p", h=4), ALU.add, [hst, pst], [hst])
            S.cp('act', sball[:, ci, :].rearrange("n (h p) -> n h p", h=4), AP(pst, 64, [[512, 64], [128, 4], [1, 64]]), [pst], [sball.s(ci)])

        ret_a1(0)
        for ci in FWD_CHAIN:
            if ci + 1 < NCH:
                ret_a1(ci + 1)
            ret_a2(ci)
        S.ms('dve', hst[:], 0.0, [hst])
        for ci in BWD_CHAIN:
            S.cp('act', hinb[:, ci, :], hst[:], [hst], [hinb.s(ci)])
            S.tt('dve', hst[:].rearrange("n (h p) -> n h p", h=4), hst[:].rearrange("n (h p) -> n h p", h=4),
                 AP(et, 4, [[8, 64], [1, 4], [0, 64]]), ALU.mult, [hst, et], [hst])
            S.tt('dve', hst[:], hst[:], sball[:, ci, :], ALU.add, [hst, sball.s(ci)], [hst])
        pgs = [pg, prp]
        pys = [py, pst]

        def ret_front(ci):
            c0, c1 = ci * 128, (ci + 1) * 128
            pg_ = pgs[ci % 2]
            for h in range(4):
                S.mm(pg_[:, h * 128:(h + 1) * 128], ka[0:64, h, c0:c1], qa[0:64, h, c0:c1], True, True, [ka.s(h), qa.s(h)], [pg_])
            for r in range(2):
                st = ST[2 * (ci % 2) + r]
                cs = CsT[2 * (ci % 2) + r]
                S.tt('dve', st[:], pg_[:].rearrange("p (h i) -> p h i", h=4), Dm[:, 4 * r:4 * r + 4, :], ALU.mult, [pg_, Dm], [st])
                S.tt('pool', cs[:], AP(qa, c0, [[4 * LB, 64], [LB, 4], [1, 128]]), Ec[:, 4 * r:4 * r + 4, :], ALU.mult, [qa.s(0), qa.s(1), qa.s(2), qa.s(3), Ec], [cs])

        y3s = [y3, S.sb(f'y3b{b}', [128, 256], BF16)]

        def ret_mm(ci):
            py_ = pys[ci % 2]
            sts = [ST[2 * (ci % 2)], ST[2 * (ci % 2) + 1]]
            css = [CsT[2 * (ci % 2)], CsT[2 * (ci % 2) + 1]]
            for h in range(4):
                hs_ = slice(h * 64, (h + 1) * 64)
                S.mm(py_[:, hs_], sts[0][:, h, :], vtok[:, ci, hs_], True, False, [sts[0], vtok.s(ci)], [py_])
                S.mm(py_[:, hs_], css[0][:, h, :], hinf[:, ci, hs_], False, False, [css[0], hinf.s(ci)], [py_])
                S.mm(py_[:, hs_], sts[1][:, h, :], vtok[:, ci, hs_], False, False, [sts[1], vtok.s(ci)], [py_])
                S.mm(py_[:, hs_], css[1][:, h, :], hinb[:, ci, hs_], False, True, [css[1], hinb.s(ci)], [py_])

        def ret_epi(ci):
            py_ = pys[ci % 2]
            S.cp('act', ysb[:], py_[:, 0:256].rearrange("p (h q) -> p h q", h=4), [py_], [ysb])
            S.tt('pool', ysq[:], ysb[:], ysb[:], ALU.mult, [ysb], [ysq])
            S.op('dve', lambda e: e.tensor_reduce(out=s1[:], in_=ysb[:], axis=AX.X, op=ALU.add), [ysb], [s1])
            S.op('dve', lambda e: e.tensor_reduce(out=s2[:], in_=ysq[:], axis=AX.X, op=ALU.add), [ysq], [s2])
            S.ts('dve', s1[:], s1[:], 1.0 / 64, None, ALU.mult, None, [s1], [s1])
            S.tt('dve', s3[:], s1[:], s1[:], ALU.mult, [s1], [s3])
            S.stt(s2[:], s2[:], 1.0 / 64, s3[:], ALU.mult, ALU.subtract, [s2, s3], [s2])
            S.act(s2[:], s2[:], AF.Ln, [s2], [s2], scale=1.0, bias=EPS)
            S.act(s2[:], s2[:], AF.Exp, [s2], [s2], scale=-0.5)
            S.tt('dve', yc[:], ysb[:], AP(s1, 0, [[4, 128], [1, 4], [0, 64]]), ALU.subtract, [ysb, s1], [yc])
            S.tt('pool', yc[:], yc[:], AP(s2, 0, [[4, 128], [1, 4], [0, 64]]), ALU.mult, [yc, s2], [yc])
            S.tt('dve', y3s[ci % 2][:], yc[:].rearrange("p h q -> p (h q)"), gs[:, ci, :], ALU.mult, [yc, gs.s(ci)], [y3s[ci % 2]])

        def ret_tail(ci):
            c0, c1 = ci * 128, (ci + 1) * 128
            y3_ = y3s[ci % 2]
            for k in range(2):
                S.tr(ptb[:, 512 + k * 128:512 + (k + 1) * 128], y3_[:, k * 128:(k + 1) * 128], P.ident_b[:], [y3_, P.ident_b], [ptb])
            S.cp('act', mixs[:, :, c0:c1], ptb[:, 512:768].rearrange("p (k t) -> p k t", k=2), [ptb], [mixs])

        c_lo = 2 if l == 1 else 0
        ret_front(c_lo)
        ret_front(c_lo + 1)
        ret_mm(c_lo)
        for ci in range(c_lo, NCH):
            if ci + 2 < NCH:
                ret_front(ci + 2)
            if ci + 1 < NCH:
                ret_mm(ci + 1)
            ret_epi(ci)
            if ci >= c_lo + 1:
                ret_tail(ci - 1)
        ret_tail(NCH - 1)
        store_mix(S, P, mixs, 512, b, skip_ctx=(l == 1))
    S.emit()

PI = math.pi


def hy_seq(tag):
    L = LL if tag == 'L' else LC
    off = lat_off if tag == 'L' else ctx_off
    return L, L // 128, off


def hyf_body(S, P, l, tag):
    K = P.K
    L, nt, _ = hy_seq(tag)
    pk = [S.ps(f'pk{i}') for i in range(2)]
    feat = S.sb('feat', [33, L], F32)
    w1 = S.sb('w1', [33, 64], F32)
    w2 = S.sb('w2', [64, 64], F32)
    w3 = S.sb('w3', [64, 1024], F32)
    b1 = S.sb('b1', [64, 1], F32)
    b2 = S.sb('b2', [64, 1], F32)
    fq = S.sb('fq', [64, 1], F32)
    fb1 = S.sb('fb1', [64, 1], F32)
    fb2 = S.sb('fb2', [64, 1], F32)
    S.dma(feat[:], getattr(P, f'k_feat_{tag}')[:], [], [feat])
    S.dma(w1[:], P.hy_w1[l], [], [w1])
    S.dma(w2[:], P.hy_w2[l], [], [w2])
    S.dma(w3[:], P.hy_w3[l], [], [w3])
    S.dma(b1[:], P.hy_b1c[l], [], [b1])
    S.dma(b2[:], P.hy_b2c[l], [], [b2])
    S.dma(fq[:], P.hy_freqc[l], [], [fq])
    S.tt('dve', fb1[:], b1[:], fq[:], ALU.mult, [b1, fq], [fb1])
    S.tt('dve', fb2[:], b2[:], fq[:], ALU.mult, [b2, fq], [fb2])
    h1 = S.sb('h1', [64, L], F32)
    h2 = S.sb('h2', [64, L], F32)
    arg = S.sb('arg', [64, 512], F32)
    msk = S.sb('msk', [64, 512], F32)
    ph = pk[0]
    W = min(512, L)
    for (wm, src, fb, dst, kk) in ((w1, feat, fb1, h1, 33), (w2, h1, fb2, h2, 64)):
        for t0 in range(0, L, W):
            S.mm(ph[0:64, 0:W], wm[0:kk, :], src[0:kk, t0:t0 + W], True, True, [wm, src], [ph])
            S.ts('dve', arg[:, 0:W], ph[0:64, 0:W], fq[:, 0:1], fb[:, 0:1], ALU.mult, ALU.add, [ph, fq, fb], [arg])
            S.ts('dve', msk[:, 0:W], arg[:, 0:W], 1e30, -PI * 1e30, ALU.mult, ALU.add, [arg], [msk])
            S.ts('dve', msk[:, 0:W], msk[:, 0:W], 0.0, 1.0, ALU.max, ALU.min, [msk], [msk])
            S.stt(arg[:, 0:W], msk[:, 0:W], -2 * PI, arg[:, 0:W], ALU.mult, ALU.add, [msk, arg], [arg])
            S.ts('dve', msk[:, 0:W], arg[:, 0:W], -1e30, -PI * 1e30, ALU.mult, ALU.add, [arg], [msk])
            S.ts('dve', msk[:, 0:W], msk[:, 0:W], 0.0, 1.0, ALU.max, ALU.min, [msk], [msk])
            S.stt(arg[:, 0:W], msk[:, 0:W], 2 * PI, arg[:, 0:W], ALU.mult, ALU.add, [msk, arg], [arg])
            S.act(dst[:, t0:t0 + W], arg[:, 0:W], AF.Sin, [arg], [dst])
            yield
    dec = [S.sb(f'dec{i}', [128, 1024], F32) for i in range(2)]
    fsb = S.sb('fsb', [128, 1024], F32)
    absf = S.sb('absf', [128, 1024], F32)
    hs = S.sb('hs', [128, nt, 512], BF16)
    hd = S.sb('hd', [128, nt, 512], BF16)
    pf = [S.ps(f'pf{i}') for i in range(2)]
    pn = [S.ps(f'pn{i}') for i in range(2)]
    dect = getattr(P, f'k_dec_{tag}')
    for mc in range(nt):
        d_ = dec[mc % 2]
        S.dma(d_[:], dect[mc * 128:(mc + 1) * 128, :], [], [d_])
        for hf in range(2):
            S.mm(pf[hf][:], h2[:, mc * 128:(mc + 1) * 128], w3[:, hf * 512:(hf + 1) * 512], True, True, [h2, w3], [pf[hf]])
            S.tt('dve', fsb[:, hf * 512:(hf + 1) * 512], pf[hf][:], d_[:, hf * 512:(hf + 1) * 512], ALU.mult, [pf[hf], d_], [fsb])
        if mc == 0:
            S.ms('dve', fsb[0:1, 512:1024], 0.0, [fsb])
        S.act(absf[:], fsb[:], AF.Abs, [fsb], [absf])
        for hf in range(2):
            S.mm(pn[hf][:], P.ones_f[:], absf[:, hf * 512:(hf + 1) * 512], mc == 0, mc == nt - 1, [P.ones_f, absf], [pn[hf]])
        S.tt('dve', hs[:, mc, :], fsb[:, 0:512], fsb[:, 512:1024], ALU.add, [fsb], [hs.s(mc)])
        S.tt('pool', hd[:, mc, :], fsb[:, 0:512], fsb[:, 512:1024], ALU.subtract, [fsb], [hd.s(mc)])
        yield
    inv = S.sb('inv', [128, 512], F32)
    S.cp('dve', inv[:], pn[0][:], [pn[0]], [inv])
    S.tt('dve', inv[:], pn[1][:], inv[:], ALU.add, [inv, pn[1]], [inv])
    S.ts('dve', inv[:], inv[:], EPS, None, ALU.add, None, [inv], [inv])
    S.op('dve', lambda e: e.reciprocal(out=inv[:], in_=inv[:]), [inv], [inv])
    bias = S.sb('bias', [128, 512], F32)
    S.dma(bias[:], AP(P.hy_bias, l * 512, [[0, 128], [1, 512]]), [], [bias])
    wf = S.sb('wf', [128, nt], F32)
    S.dma(wf[:], getattr(P, f'k_wf_{tag}')[:], [], [wf])
    alt = S.sb('alt', [128, nt], BF16)
    S.dma(alt[:], getattr(P, f'k_alt_{tag}')[:], [], [alt])
    hsr = [hs.s(c) for c in range(nt)]
    hdr = [hd.s(c) for c in range(nt)]
    pq = pf[0]
    for c in range(nt):
        S.mm(pq[0:1, :], alt[:, c:c + 1], hs[:, c, :], c == 0, c == nt - 1, [alt, hsr[c]], [pq])
    knq = S.sb('knq', [1, 512], F32)
    S.tt('dve', knq[:], pq[0:1, :], inv[0:1, :], ALU.mult, [pq, inv], [knq])
    S.tt('dve', knq[:], knq[:], bias[0:1, :], ALU.add, [knq, bias], [knq])
    S.ts('dve', knq[:], knq[:], wf[0:1, 0:1], None, ALU.mult, None, [knq, wf], [knq])
    cb = [S.sb(f'cb{i}', [128, nt, 128], BF16) for i in range(2)]
    sbk = [S.sb(f'sbk{i}', [128, nt, 128], BF16) for i in range(2)]
    kr = [S.sb(f'kr{i}', [128, 512], BF16) for i in range(2)]
    ki = [S.sb(f'ki{i}', [128, 512], BF16) for i in range(2)]
    tk = S.sb('tk', [128, 512], F32)
    dc = getattr(P, f'k_dftc_{tag}')
    dsf = getattr(P, f'k_dftsf_{tag}')
    hyK = P.hyK[tag]
    for ft in range(nt):
        c_ = cb[ft % 2]
        s_ = sbk[ft % 2]
        S.dma(c_[:], dc[ft], [], [c_])
        S.dma(s_[:], dsf[ft], [], [s_])
        for c in range(nt):
            S.mm(pk[0][:], c_[:, c, :], hs[:, c, :], c == 0, c == nt - 1, [c_, hsr[c]], [pk[0]])
        for c in range(nt):
            S.mm(pk[1][:], s_[:, c, :], hd[:, c, :], c == 0, c == nt - 1, [s_, hdr[c]], [pk[1]])
        kr_ = kr[ft % 2]
        ki_ = ki[ft % 2]
        S.tt('dve', tk[:], pk[0][:], inv[:], ALU.mult, [pk[0], inv], [tk])
        S.tt('pool', tk[:], tk[:], bias[:], ALU.add, [tk, bias], [tk])
        S.ts('dve', kr_[:], tk[:], wf[:, ft:ft + 1], None, ALU.mult, None, [tk, wf], [kr_])
        S.stt(ki_[:], pk[1][:], wf[:, ft:ft + 1], inv[:], ALU.mult, ALU.mult, [pk[1], wf, inv], [ki_])
        if ft == 0:
            S.cp('dve', ki_[0:1, :], knq[:], [knq, ki_], [ki_])
        S.dma(hyK[0, ft * 128:(ft + 1) * 128, :], kr_[:], [kr_], [hyK.s((0, ft))])
        S.dma(hyK[1, ft * 128:(ft + 1) * 128, :], ki_[:], [ki_], [hyK.s((1, ft))])
        yield


def hyp_body(S, P, l, tag):
    K = P.K
    L, nt, off = hy_seq(tag)
    cw = S.sb('cw', [128, 6, 4], F32)
    S.dma(cw[:], P.hy_cw[l], [], [cw])
    raw = S.sb('raw', [128, 6, L], BF16)
    acc = [S.sb(f'acc{i}', [128, L], F32) for i in range(2)]
    cv = S.sb('cv', [128, 6, L], BF16)
    ptb = [S.ps(f'ptb{i}', [128, 1024], BF16) for i in range(2)]
    stg = [S.sb(f'stg{i}', [128, 3, 256], BF16) for i in range(2)]
    hyX = P.hyX[tag]
    n = 0
    for b in range(NB):
        for k in range(6):
            S.dma(raw[:, k, :], P.uT[OFF_HY + k * 128:OFF_HY + (k + 1) * 128, off(b):off(b) + L], [], [raw.s(k)])
        for k in range(6):
            a_ = acc[k % 2]
            S.ts('dve', a_[:], raw[:, k, :], cw[:, k, 1:2], cw[:, k, 3:4], ALU.mult, ALU.add, [raw.s(k), cw], [a_])
            S.stt(a_[:, 1:L], raw[:, k, 0:L - 1], cw[:, k, 0:1], a_[:, 1:L], ALU.mult, ALU.add, [raw.s(k), cw, a_], [a_])
            S.stt(a_[:, 0:L - 1], raw[:, k, 1:L], cw[:, k, 2:3], a_[:, 0:L - 1], ALU.mult, ALU.add, [raw.s(k), cw, a_], [a_])
            S.cp('act', cv[:, k, :], a_[:], [a_], [cv.s(k)])
            yield
        for tc in range(nt):
            p_ = ptb[n % 2]
            s_ = stg[n % 2]
            n += 1
            for k in range(6):
                S.tr(p_[:, k * 128:(k + 1) * 128], cv[:, k, tc * 128:(tc + 1) * 128], P.ident_b[:], [cv.s(k), P.ident_b], [p_])
            S.cp('act' if n % 2 else 'dve', s_[:].rearrange("p j c -> p (j c)"), p_[:, 0:768], [p_], [s_])
            S.dma(hyX[:, tc * 128:(tc + 1) * 128, b * 256:(b + 1) * 256].rearrange("j p c -> p j c"), s_[:], [s_], [hyX.s((tc, b))])
            yield


def stage_hy_conv(P, l, tag, side=None):
    S = P.K.stage(f"hyc{l}{tag}")
    gens = [hyc_body(S, P, l, tag)]
    if side is not None:
        gens.append(side(S))
    live = list(gens)
    while live:
        for g in list(live):
            try:
                next(g)
            except StopIteration:
                live.remove(g)
    S.emit()


def hyc_body(S, P, l, tag):
    K = P.K
    L, nt, off = hy_seq(tag)
    hyX = P.hyX[tag]
    hyK = P.hyK[tag]
    Z = [S.sb(f'Z{i}', [128, nt, 512], BF16) for i in range(2)]
    for tc in range(nt):
        S.dma(Z[0][:, tc, :], hyX[2, tc * 128:(tc + 1) * 128, :], [], [Z[0].s(tc)])
    Yr = S.sb('Yr', [128, nt, 512], BF16)
    Yi = S.sb('Yi', [128, nt, 512], BF16)
    cb = [S.sb(f'cb{i}', [128, nt, 128], BF16) for i in range(3)]
    sbk = [S.sb(f'sbk{i}', [128, nt, 128], BF16) for i in range(3)]
    kr = [S.sb(f'kr{i}', [128, 256], BF16) for i in range(2)]
    ki = [S.sb(f'ki{i}', [128, 256], BF16) for i in range(2)]
    gt = [S.sb(f'gt{i}', [128, 512], BF16) for i in range(2)]
    t1 = S.sb('t1', [128, 512], F32)
    t2 = S.sb('t2', [128, 512], F32)
    t3 = S.sb('t3', [128, 512], F32)
    t4 = S.sb('t4', [128, 512], F32)
    zo = S.sb('zo', [128, 512], BF16)
    mixh = S.sb('mixh', [128, 2, NB, L], BF16)
    px = [S.ps(f'px{i}') for i in range(4)]
    py = [S.ps(f'py{i}') for i in range(2)]
    ptb = S.ps('ptb', [128, 1024], BF16)
    dc = getattr(P, f'k_dftc_{tag}')
    dsf = getattr(P, f'k_dftsf_{tag}')
    dsi = getattr(P, f'k_dftsi_{tag}')
    n = 0
    for o in range(2):
        zin = Z[o]
        zr_ = [zin.s(c) for c in range(nt)]
        for ft in range(nt):
            c_ = cb[n % 3]
            s_ = sbk[n % 3]
            kr_ = kr[n % 2]
            ki_ = ki[n % 2]
            pr = px[(2 * n) % 4]
            pi_ = px[(2 * n + 1) % 4]
            n += 1
            S.dma(c_[:], dc[ft], [], [c_])
            S.dma(s_[:], dsf[ft], [], [s_])
            S.dma(kr_[:], hyK[0, ft * 128:(ft + 1) * 128, o * 256:(o + 1) * 256], [], [kr_])
            S.dma(ki_[:], hyK[1, ft * 128:(ft + 1) * 128, o * 256:(o + 1) * 256], [], [ki_])
            for c in range(nt):
                S.mm(pr[:], c_[:, c, :], zin[:, c, :], c == 0, c == nt - 1, [c_, zr_[c]], [pr])
            for c in range(nt):
                S.mm(pi_[:], s_[:, c, :], zin[:, c, :], c == 0, c == nt - 1, [s_, zr_[c]], [pi_])
            krb = AP(kr_, 0, [[256, 128], [0, 2], [1, 256]])
            kib = AP(ki_, 0, [[256, 128], [0, 2], [1, 256]])
            v3 = lambda t: t[:].rearrange("p (b c) -> p b c", b=2)
            S.tt('dve', v3(t1), v3(pr), krb, ALU.mult, [pr, kr_], [t1])
            S.tt('dve', v3(t2), v3(pi_), kib, ALU.mult, [pi_, ki_], [t2])
            S.tt('dve', v3(t3), v3(pr), kib, ALU.mult, [pr, ki_], [t3])
            S.tt('dve', v3(t4), v3(pi_), krb, ALU.mult, [pi_, kr_], [t4])
            S.tt('pool', Yr[:, ft, :], t1[:], t2[:], ALU.subtract, [t1, t2], [Yr.s(ft)])
            S.tt('pool', Yi[:, ft, :], t3[:], t4[:], ALU.add, [t3, t4], [Yi.s(ft)])
            if ft == 0:
                S.tt('dve', Yr[0:1, 0, :].rearrange("p (b c) -> p b c", b=2), pr[0:1, :].rearrange("p (b c) -> p b c", b=2),
                     AP(kr_, 0, [[256, 1], [0, 2], [1, 256]]), ALU.mult, [pr, kr_, Yr.s(0)], [Yr.s(0)])
                S.tt('dve', Yi[0:1, 0, :].rearrange("p (b c) -> p b c", b=2), pi_[0:1, :].rearrange("p (b c) -> p b c", b=2),
                     AP(ki_, 0, [[256, 1], [0, 2], [1, 256]]), ALU.mult, [pi_, ki_, Yi.s(0)], [Yi.s(0)])
            yield
        yrr = [Yr.s(c) for c in range(nt)]
        yir = [Yi.s(c) for c in range(nt)]
        for tt_ in range(nt):
            c_ = cb[n % 3]
            s_ = sbk[n % 3]
            g_ = gt[n % 2]
            p_ = py[n % 2]
            n += 1
            S.dma(c_[:], dc[tt_], [], [c_])
            S.dma(s_[:], dsi[tt_], [], [s_])
            S.dma(g_[:], hyX[o, tt_ * 128:(tt_ + 1) * 128, :], [], [g_])
            for c in range(nt):
                S.mm(p_[:], c_[:, c, :], Yr[:, c, :], c == 0, False, [c_, yrr[c]], [p_])
            for c in range(nt):
                S.mm(p_[:], s_[:, c, :], Yi[:, c, :], False, c == nt - 1, [s_, yir[c]], [p_])
            if o == 0:
                S.tt('dve', Z[1][:, tt_, :], p_[:], g_[:], ALU.mult, [p_, g_], [Z[1].s(tt_)])
            else:
                S.tt('dve', zo[:], p_[:], g_[:], ALU.mult, [p_, g_], [zo])
                for b in range(NB):
                    for k in range(2):
                        j = b * 2 + k
                        S.tr(ptb[:, j * 128:(j + 1) * 128], zo[:, b * 256 + k * 128:b * 256 + (k + 1) * 128], P.ident_b[:], [zo, P.ident_b], [ptb])
                S.cp('act', mixh[:, :, :, tt_ * 128:(tt_ + 1) * 128].rearrange("p k b t -> p b k t"),
                     ptb[:, 0:512].rearrange("p (b k t) -> p b k t", b=2, k=2), [ptb], [mixh])
            yield
    for b in range(NB):
        for k in range(2):
            S.dma(P.mixT[256 + k * 128:256 + (k + 1) * 128, off(b):off(b) + L], mixh[:, k, b, :], [mixh], [P.mixT.s((256, k, b, tag))])
    yield


def stage_hy_fp(P, l, tag):
    S = P.K.stage(f"hyfp{l}{tag}")
    gens = [hyp_body(S, P, l, tag), hyf_body(S, P, l, tag)]
    live = list(gens)
    while live:
        for g in list(live):
            try:
                next(g)
            except StopIteration:
                live.remove(g)
    S.emit()

TCW = 288
W8 = 8


def stage_s5(P, l):
    K = P.K
    S = K.stage(f"s5{l}")
    NK = 8
    bre = S.sb('bre', [128, NK, 16], F32)
    bim = S.sb('bim', [128, NK, 16], F32)
    S.dma(bre[:], P.s5_bre[l], [], [bre])
    S.dma(bim[:], P.s5_bim[l], [], [bim])
    Rr = S.sb('Rr', [128, NK, TCW], F32)
    Ri = S.sb('Ri', [128, NK, TCW], F32)
    rho8 = S.sb('rho8', [128, NK], F32)
    cth = S.sb('cth', [128, NK], F32)
    sth = S.sb('sth', [128, NK], F32)
    BdT = [[[S.sb(f'BdT{m}{k}{c}', [128, 128], BF16) for c in range(2)] for k in range(NK)] for m in range(W8)]
    CdA = [[[S.sb(f'CdA{m}{k}{c}', [128, 128], BF16) for c in range(2)] for k in range(NK)] for m in range(W8)]
    KernT = [[S.sb(f'KT{m}{kc}', [128, 128], BF16) for kc in range(2)] for m in range(W8)]
    CdF = [[S.sb(f'CdF{k}{c}', [128, 128], F32) for c in range(2)] for k in range(NK)]
    bdz = [[S.sb(f'bdz{q}{c}', [128, 128], F32) for c in range(2)] for q in range(4)]
    are = S.sb('are', [128, NK], F32)
    aim = S.sb('aim', [128, NK], F32)
    dt = S.sb('dt', [128, NK], F32)
    t_a = S.sb('t_a', [128, NK], F32)
    t_b = S.sb('t_b', [128, NK], F32)
    t_c = S.sb('t_c', [128, NK], F32)
    mag = S.sb('mag', [128, NK], F32)
    cc = S.sb('cc', [128, NK], F32)
    sn = S.sb('sn', [128, NK], F32)
    zr = S.sb('zr', [128, NK], F32)
    zi = S.sb('zi', [128, NK], F32)
    ar_ = S.sb('ar_', [128, NK], F32)
    ai_ = S.sb('ai_', [128, NK], F32)
    pwr = S.sb('pwr', [128, W8 + 1, NK], F32)
    pwi = S.sb('pwi', [128, W8 + 1, NK], F32)
    bbr = S.sb('bbr', [128, NK, 16], F32)
    bbi = S.sb('bbi', [128, NK, 16], F32)
    xr_ = S.sb('xr_', [128, NK, 16], F32)
    xi_ = S.sb('xi_', [128, NK, 16], F32)
    tb1 = S.sb('tb1', [128, NK, 16], F32)
    tb2 = S.sb('tb2', [128, NK, 16], F32)
    cre = S.sb('cre', [128, NK, 16], F32)
    cim = S.sb('cim', [128, NK, 16], F32)
    wr = S.sb('wr', [128, NK], F32)
    wi = S.sb('wi', [128, NK], F32)
    ptp = S.ps('ptp')
    pkn = S.ps('pkn')
    hpi = math.pi / 2
    NWB = LB // W8
    u8 = S.sb('u8', [128, 2, W8, NWB], BF16)
    ysA = S.sb('ysA', [128, 2, LB], F32)
    Gr = S.sb('Gr', [128, NK, TCW], F32)
    Gi = S.sb('Gi', [128, NK, TCW], F32)
    Hr = S.sb('Hr', [128, NK, TCW + 1], BF16)
    Hi = S.sb('Hi', [128, NK, TCW + 1], BF16)
    hpr = S.sb('hpr', [128, NK], F32)
    hpi_ = S.sb('hpi', [128, NK], F32)
    inr = S.sb('inr', [128, NK], F32)
    ini = S.sb('ini', [128, NK], F32)
    w1 = [S.sb(f'w1{i}', [128, TCW], F32) for i in range(2)]
    w2 = [S.sb(f'w2{i}', [128, TCW], F32) for i in range(2)]
    w3 = [S.sb('w3', [128, TCW], F32)] * 2
    w4 = [S.sb('w4', [128, TCW], F32)] * 2
    zr_ = [S.sb(f'zr{i}', [128, TCW], F32) for i in range(2)]
    zi_ = [S.sb(f'zi{i}', [128, TCW], F32) for i in range(2)]
    pb = [S.ps(f'pb{i}') for i in range(4)]
    pyy = [S.ps(f'pyy{i}') for i in range(2)]

    def V(eng, out, a, b, op):
        S.tt(eng, out[:], a[:], b[:], op, [a, b], [out])

    def cmul(outr, outi, ar, ai, br, bi, R, ta, tb):
        S.tt('dve', ta[0], ar, br, ALU.mult, R, [ta[1]])
        S.tt('dve', tb[0], ai, bi, ALU.mult, R, [tb[1]])
        S.tt('dve', outr[0], ta[0], tb[0], ALU.subtract, [ta[1], tb[1]], [outr[1]])
        S.tt('dve', ta[0], ar, bi, ALU.mult, R, [ta[1]])
        S.tt('dve', tb[0], ai, br, ALU.mult, R, [tb[1]])
        S.tt('dve', outi[0], ta[0], tb[0], ALU.add, [ta[1], tb[1]], [outi[1]])

    for q in range(4):
        for c in range(2):
            S.ms('pool', bdz[q][c][:], 0.0, [bdz[q][c]])
    for m in range(W8):
        for k in range(NK):
            for c in range(2):
                S.ms('pool', CdA[m][k][c][:], 0.0, [CdA[m][k][c]])
    for k in range(NK):
        for c in range(2):
            S.ms('pool', CdF[k][c][:], 0.0, [CdF[k][c]])

    segs = ((0, 0, LC // W8), (LC, LC // W8, LL // W8))

    bg = conv_job(S, P, l, ('act',))

    def bgstep(n=1):
        for _ in range(n):
            next(bg, None)

    for r in range(2):
        rv = (r == 1)
        S.dma(are[:], P.s5_are[l, r], [], [are])
        S.dma(aim[:], P.s5_aim[l, r], [], [aim])
        S.dma(dt[:], P.s5_ldt[l, r], [], [dt])
        S.act(dt[:], dt[:], AF.Exp, [dt], [dt])
        V('dve', t_a, are, dt, ALU.mult)
        S.act(mag[:], t_a[:], AF.Exp, [t_a], [mag])
        V('dve', t_a, aim, dt, ALU.mult)
        S.ts('dve', t_a[:], t_a[:], 1.0 / 16, None, ALU.mult, None, [t_a], [t_a])
        S.act(sn[:], t_a[:], AF.Sin, [t_a], [sn])
        S.ts('dve', t_b[:], t_a[:], hpi, None, ALU.add, None, [t_a], [t_b])
        S.act(cc[:], t_b[:], AF.Sin, [t_b], [cc])

        def sq_angle():
            V('dve', t_a, cc, cc, ALU.mult)
            V('dve', t_b, sn, sn, ALU.mult)
            V('dve', t_c, sn, cc, ALU.mult)
            V('dve', cc, t_a, t_b, ALU.subtract)
            S.ts('dve', sn[:], t_c[:], 2.0, None, ALU.mult, None, [t_c], [sn])
        for _ in range(4):
            sq_angle()
        V('dve', ar_, cc, mag, ALU.mult)
        V('dve', ai_, sn, mag, ALU.mult)
        S.ts('dve', t_c[:], ar_[:], -1.0, None, ALU.add, None, [ar_], [t_c])
        V('dve', t_a, are, are, ALU.mult)
        V('dve', t_b, aim, aim, ALU.mult)
        V('dve', t_a, t_a, t_b, ALU.add)
        S.op('dve', lambda e: e.reciprocal(out=t_a[:], in_=t_a[:]), [t_a], [t_a])
        V('dve', t_b, t_c, are, ALU.mult)
        V('dve', zr, ai_, aim, ALU.mult)
        V('dve', t_b, t_b, zr, ALU.add)
        V('dve', zr, t_b, t_a, ALU.mult)
        V('dve', t_b, ai_, are, ALU.mult)
        V('dve', zi, t_c, aim, ALU.mult)
        V('dve', t_b, t_b, zi, ALU.subtract)
        V('dve', zi, t_b, t_a, ALU.mult)
        for _ in range(3):
            sq_angle()
        S.cp('dve', cth[:], cc[:], [cc], [cth])
        S.cp('dve', sth[:], sn[:], [sn], [sth])
        V('dve', t_a, mag, mag, ALU.mult)
        V('dve', t_b, t_a, t_a, ALU.mult)
        V('dve', rho8, t_b, t_b, ALU.mult)
        S.ms('dve', pwr[:, 0, :], 1.0, [pwr])
        S.ms('dve', pwi[:, 0, :], 0.0, [pwi])
        for m in range(1, W8 + 1):
            cmul((pwr[:, m, :], pwr), (pwi[:, m, :], pwi), pwr[:, m - 1, :], pwi[:, m - 1, :], ar_[:], ai_[:],
                 [pwr, pwi, ar_, ai_], (t_a[:], t_a), (t_b[:], t_b))
        zrb = AP(zr, 0, [[NK, 128], [1, NK], [0, 16]])
        zib = AP(zi, 0, [[NK, 128], [1, NK], [0, 16]])
        cmul((bbr[:], bbr), (bbi[:], bbi), bre[:], bim[:], zrb, zib, [bre, bim, zr, zi], (tb1[:], tb1), (tb2[:], tb2))
        S.dma(cre[:], P.s5_cre[l, r], [], [cre])
        S.dma(cim[:], P.s5_cim[l, r], [], [cim])
        for k in range(NK):
            c0 = 32 * (k % 4)
            for c, (src, sg) in enumerate(((cre, 1.0), (cim, -1.0))):
                d_ = CdF[k][c]
                S.ts('dve', d_[0:64, c0:c0 + 16], src[0:64, k, :], sg, None, ALU.mult, None, [src, d_], [d_])
                S.ts('dve', d_[64:128, c0 + 16:c0 + 32], src[64:128, k, :], sg, None, ALU.mult, None, [src, d_], [d_])
        for m in range(W8):
            pmr = AP(pwr, m * NK, [[(W8 + 1) * NK, 128], [1, NK], [0, 16]])
            pmi = AP(pwi, m * NK, [[(W8 + 1) * NK, 128], [1, NK], [0, 16]])
            cmul((xr_[:], xr_), (xi_[:], xi_), bbr[:], bbi[:], pmr, pmi, [bbr, bbi, pwr, pwi], (tb1[:], tb1), (tb2[:], tb2))
            for kc in range(2):
                for kk in range(4):
                    k = kc * 4 + kk
                    c0 = 32 * kk
                    for c, src in enumerate((xr_, xi_)):
                        bd = bdz[kk][c]
                        S.cp('dve', bd[0:64, c0:c0 + 16], src[0:64, k, :], [src, bd], [bd])
                        S.cp('dve', bd[64:128, c0 + 16:c0 + 32], src[64:128, k, :], [src, bd], [bd])
                        S.tr(ptp[:, c * 128:(c + 1) * 128], bd[:], P.ident_f[:], [bd, P.ident_f], [ptp])
                        S.cp('act', BdT[m][k][c][:], ptp[:, c * 128:(c + 1) * 128], [ptp], [BdT[m][k][c]])
                        S.mm(pkn[:, kc * 128:(kc + 1) * 128], bd[:], CdF[k][c][:], kk == 0 and c == 0, kk == 3 and c == 1, [bd, CdF[k][c]], [pkn])
                S.cp('act', KernT[m][kc][:], pkn[:, kc * 128:(kc + 1) * 128], [pkn], [KernT[m][kc]])
            pmr1 = AP(pwr, (m + 1) * NK, [[(W8 + 1) * NK, 128], [1, NK], [0, 16]])
            pmi1 = AP(pwi, (m + 1) * NK, [[(W8 + 1) * NK, 128], [1, NK], [0, 16]])
            cmul((xr_[:], xr_), (xi_[:], xi_), cre[:], cim[:], pmr1, pmi1, [cre, cim, pwr, pwi], (tb1[:], tb1), (tb2[:], tb2))
            for k in range(NK):
                c0 = 32 * (k % 4)
                for c, (src, sg) in enumerate(((xr_, 1.0), (xi_, -1.0))):
                    d_ = CdA[m][k][c]
                    S.ts('dve', d_[0:64, c0:c0 + 16], src[0:64, k, :], sg, None, ALU.mult, None, [src, d_], [d_])
                    S.ts('dve', d_[64:128, c0 + 16:c0 + 32], src[64:128, k, :], sg, None, ALU.mult, None, [src, d_], [d_])
        S.ms('dve', Rr[:, :, 0:1], 1.0, [Rr])
        S.ms('dve', Ri[:, :, 0:1], 0.0, [Ri])
        S.cp('dve', wr[:], cth[:], [cth], [wr])
        S.ts('dve', wi[:], sth[:], -1.0, None, ALU.mult, None, [sth], [wi])
        n = 1
        while n < TCW:
            m_ = min(n, TCW - n)
            wrb = AP(wr, 0, [[NK, 128], [1, NK], [0, m_]])
            wib = AP(wi, 0, [[NK, 128], [1, NK], [0, m_]])
            tmpa, tmpb = Gr, Gi
            S.tt('dve', tmpa[:, :, 0:m_], Rr[:, :, 0:m_], wrb, ALU.mult, [Rr, wr], [tmpa])
            S.tt('dve', tmpb[:, :, 0:m_], Ri[:, :, 0:m_], wib, ALU.mult, [Ri, wi], [tmpb])
            S.tt('dve', Rr[:, :, n:n + m_], tmpa[:, :, 0:m_], tmpb[:, :, 0:m_], ALU.subtract, [tmpa, tmpb, Rr], [Rr])
            S.tt('dve', tmpa[:, :, 0:m_], Rr[:, :, 0:m_], wib, ALU.mult, [Rr, wi], [tmpa])
            S.tt('dve', tmpb[:, :, 0:m_], Ri[:, :, 0:m_], wrb, ALU.mult, [Ri, wr], [tmpb])
            S.tt('dve', Ri[:, :, n:n + m_], tmpa[:, :, 0:m_], tmpb[:, :, 0:m_], ALU.add, [tmpa, tmpb, Ri], [Ri])
            V('dve', t_a, wr, wr, ALU.mult)
            V('dve', t_b, wi, wi, ALU.mult)
            V('dve', t_c, wr, wi, ALU.mult)
            V('dve', wr, t_a, t_b, ALU.subtract)
            S.ts('dve', wi[:], t_c[:], 2.0, None, ALU.mult, None, [t_c], [wi])
            n *= 2
        nW = NWB
        for b in range(NB):
            cw_, lw_ = LC // W8, LL // W8
            for k in range(2):
                csrc = P.u8T[k * 128:(k + 1) * 128, :, ctx_off(b) // W8:ctx_off(b) // W8 + cw_]
                lsrc = P.u8T[k * 128:(k + 1) * 128, :, lat_off(b) // W8:lat_off(b) // W8 + lw_]
                if rv:
                    S.dma(u8[:, k, :, 0:lw_], lsrc, [], [u8.s(k)])
                    S.dma(u8[:, k, :, lw_:NWB], csrc, [], [u8.s(k)])
                else:
                    S.dma(u8[:, k, :, 0:cw_], csrc, [], [u8.s(k)])
                    S.dma(u8[:, k, :, cw_:NWB], lsrc, [], [u8.s(k)])
            u = 0
            ny = 0

            def uwin(kc, j):
                if rv:
                    return AP(u8, (kc * W8 + (W8 - 1 - j)) * NWB + nW - 1, [[2 * W8 * NWB, 128], [-1, nW]])
                return AP(u8, (kc * W8 + j) * NWB, [[2 * W8 * NWB, 128], [1, nW]])
            hks = [Hr.s(k) for k in range(NK)] + [Hi.s(k) for k in range(NK)]
            S.ms('pool', AP(Hr, 0, [[NK * (TCW + 1), 128], [TCW + 1, NK]]), 0.0, hks[:NK])
            S.ms('pool', AP(Hi, 0, [[NK * (TCW + 1), 128], [TCW + 1, NK]]), 0.0, hks[NK:])
            for k in range(NK):
                bgstep(2)
                kc = k // 4
                i2 = u % 2
                pzr = pb[(2 * u) % 4]
                pzi = pb[(2 * u + 1) % 4]
                u += 1
                for j in range(W8):
                    S.mm(pzr[:, 0:nW], BdT[W8 - 1 - j][k][0][:], uwin(kc, j), j == 0, j == W8 - 1, [BdT[W8 - 1 - j][k][0], u8.s(kc)], [pzr])
                for j in range(W8):
                    S.mm(pzi[:, 0:nW], BdT[W8 - 1 - j][k][1][:], uwin(kc, j), j == 0, j == W8 - 1, [BdT[W8 - 1 - j][k][1], u8.s(kc)], [pzi])
                rr_, ri_ = Rr[:, k, 0:nW], Ri[:, k, 0:nW]
                S.tt('dve', w1[i2][:, 0:nW], pzr[:, 0:nW], rr_, ALU.mult, [pzr, Rr], [w1[i2]])
                S.tt('dve', w2[i2][:, 0:nW], pzi[:, 0:nW], ri_, ALU.mult, [pzi, Ri], [w2[i2]])
                S.tt('dve', w3[i2][:, 0:nW], pzi[:, 0:nW], rr_, ALU.mult, [pzi, Rr], [w3[i2]])
                S.tt('dve', w4[i2][:, 0:nW], pzr[:, 0:nW], ri_, ALU.mult, [pzr, Ri], [w4[i2]])
                S.tt('pool', zr_[i2][:, 0:nW], w1[i2][:, 0:nW], w2[i2][:, 0:nW], ALU.subtract, [w1[i2], w2[i2]], [zr_[i2]])
                S.tt('pool', zi_[i2][:, 0:nW], w3[i2][:, 0:nW], w4[i2][:, 0:nW], ALU.add, [w3[i2], w4[i2]], [zi_[i2]])
                rhob = AP(rho8, k, [[NK, 128], [0, nW]])
                S.op('dve', lambda e, k=k, i2=i2, rhob=rhob: e.tensor_tensor_scan(out=Gr[:, k, 0:nW], data0=rhob, data1=zr_[i2][:, 0:nW], initial=0.0, op0=ALU.mult, op1=ALU.add),
                     [rho8, zr_[i2]], [Gr.s(k)])
                S.op('dve', lambda e, k=k, i2=i2, rhob=rhob: e.tensor_tensor_scan(out=Gi[:, k, 0:nW], data0=rhob, data1=zi_[i2][:, 0:nW], initial=0.0, op0=ALU.mult, op1=ALU.add),
                     [rho8, zi_[i2]], [Gi.s(k)])
                S.tt('dve', w1[i2][:, 0:nW], Gr[:, k, 0:nW], rr_, ALU.mult, [Gr.s(k), Rr], [w1[i2]])
                S.tt('pool', w2[i2][:, 0:nW], Gi[:, k, 0:nW], ri_, ALU.mult, [Gi.s(k), Ri], [w2[i2]])
                S.tt('dve', w3[i2][:, 0:nW], Gi[:, k, 0:nW], rr_, ALU.mult, [Gi.s(k), Rr], [w3[i2]])
                S.tt('pool', w4[i2][:, 0:nW], Gr[:, k, 0:nW], ri_, ALU.mult, [Gr.s(k), Ri], [w4[i2]])
                S.tt('pool', Hr[:, k, 1:nW + 1], w1[i2][:, 0:nW], w2[i2][:, 0:nW], ALU.add, [w1[i2], w2[i2]], [Hr.s(k)])
                S.tt('pool', Hi[:, k, 1:nW + 1], w3[i2][:, 0:nW], w4[i2][:, 0:nW], ALU.subtract, [w3[i2], w4[i2]], [Hi.s(k)])
            for kc in range(2):
                for j in range(W8):
                    bgstep()
                    p_ = pyy[ny % 2]
                    ny += 1
                    first = True
                    for kk in range(4):
                        k = kc * 4 + kk
                        S.mm(p_[:, 0:nW], CdA[j][k][0][:], Hr[:, k, 0:nW], first, False, [CdA[j][k][0], Hr.s(k)], [p_])
                        first = False
                        S.mm(p_[:, 0:nW], CdA[j][k][1][:], Hi[:, k, 0:nW], False, False, [CdA[j][k][1], Hi.s(k)], [p_])
                    for jp in range(j + 1):
                        S.mm(p_[:, 0:nW], KernT[j - jp][kc][:], uwin(kc, jp), False, jp == j, [KernT[j - jp][kc], u8.s(kc)], [p_])
                    if rv:
                        S.cp('act', AP(ysA, kc * LB + LC - 1 - j, [[2 * LB, 128], [-W8, cw_]]), p_[:, 0:cw_], [p_], [ysA])
                        S.cp('act', AP(ysA, kc * LB + LC + LL - 1 - j, [[2 * LB, 128], [-W8, lw_]]), p_[:, cw_:nW], [p_], [ysA])
                    else:
                        S.cp('act', AP(ysA, kc * LB + j, [[2 * LB, 128], [W8, nW]]), p_[:, 0:nW], [p_], [ysA])
            for kc in range(2):
                S.dma(P.s5y[r, b, kc * 128:(kc + 1) * 128, :], ysA[:, kc, :], [ysA], [P.s5y.s((r, b, kc))])
    for _ in bg:
        pass
    S.emit()


def stage_s5_epi(P, l):
    K = P.K
    S = K.stage(f"s5e{l}")
    dT = S.sb('dT', [128, 2], F32)
    S.dma(dT[:], P.s5_dT[l], [], [dT])
    gb = S.sb('gb', [128, 2], F32)
    S.dma(gb[:], P.s5_glu_bT[l], [], [gb])
    gw = S.sb('gw', [128, 2, 256], BF16)
    S.dma(gw[:], P.glu_w_b[l].rearrange("(k p) c -> p k c", p=128), [], [gw])
    u5 = S.sb('u5', [128, 2, LB], BF16)
    ysA = S.sb('ysA', [128, 2, LB], F32)
    ysB = S.sb('ysB', [128, 2, LB], F32)
    yv = S.sb('yv', [128, LB], F32)
    x2 = S.sb('x2', [128, LB], F32)
    gy = S.sb('gy', [128, 2, LB], BF16)
    sg = [S.sb(f'sg{i}', [128, 512], F32) for i in range(2)]
    mixs = S.sb('mixs', [128, 2, LB], BF16)
    pgl = [S.ps(f'pgl{i}') for i in range(2)]
    for b in range(NB):
        for k in range(2):
            load_seq(S, lambda a, e, k=k: u5[:, k, a:e], P, OFF_S5 + 128 * k, 128, b, u5.s(k))
            S.dma(ysA[:, k, :], P.s5y[0, b, k * 128:(k + 1) * 128, :], [], [ysA.s(k)])
            S.dma(ysB[:, k, :], P.s5y[1, b, k * 128:(k + 1) * 128, :], [], [ysB.s(k)])
        for kc in range(2):
            S.stt(yv[:], u5[:, kc, :], dT[:, kc:kc + 1], ysA[:, kc, :], ALU.mult, ALU.add, [u5.s(kc), dT, ysA.s(kc)], [yv])
            S.tt('pool', yv[:], yv[:], ysB[:, kc, :], ALU.add, [yv, ysB.s(kc)], [yv])
            S.tt('pool', x2[:], yv[:], yv[:], ALU.mult, [yv], [x2])
            S.ts('dve', x2[:], x2[:], 0.0713548162726, 1.5957691216057, ALU.mult, ALU.add, [x2], [x2])
            S.tt('dve', x2[:], x2[:], yv[:], ALU.mult, [x2, yv], [x2])
            S.act(x2[:], x2[:], AF.Sigmoid, [x2], [x2])
            S.tt('dve', gy[:, kc, :], yv[:], x2[:], ALU.mult, [yv, x2], [gy.s(kc)])
        n = 0
        for a in range(0, LB, 512):
            cw_ = min(512, LB - a)
            for oc in range(2):
                p_ = pgl[n % 2]
                s_ = sg[n % 2]
                n += 1
                for kc in range(2):
                    S.mm(p_[:, 0:cw_], gw[:, kc, oc * 128:(oc + 1) * 128], gy[:, kc, a:a + cw_], kc == 0, kc == 1, [gw, gy.s(kc)], [p_])
                S.act(s_[:, 0:cw_], p_[:, 0:cw_], AF.Sigmoid, [p_, gb], [s_], bias=gb[:, oc:oc + 1], scale=1.0)
                S.tt('dve', mixs[:, oc, a:a + cw_], gy[:, oc, a:a + cw_], s_[:, 0:cw_], ALU.mult, [gy.s(oc), s_], [mixs])
        store_mix(S, P, mixs, 768, b)
    S.emit()


def stage_of(P, l, last):
    K = P.K
    S = K.stage(f"of{l}")
    wout = S.sb('wout', [128, 8, D], BF16)
    S.dma(wout[:], P.w_out_b[l].rearrange("(k p) c -> p k c", p=128), [], [wout])
    fg = S.sb('fg', [128, 8], F32)
    if last:
        S.dma(fg[:], P.final_gT[:], [], [fg])
    mixb = [S.sb(f'mix{i}', [128, 8, 512], BF16) for i in range(2)]
    hb = [S.sb(f'h{i}', [128, 8, 512], F32) for i in range(2)]
    sq = [S.sb(f'sq{i}', [128, 512], F32) for i in range(2)]
    tmpn = [S.sb(f'tmpn{i}', [128, 512], F32) for i in range(2)]
    xnb = [S.sb(f'xn{i}', [128, 8, 512], BF16) for i in range(2)]
    rstd = S.sb('rstd', [128, 512], F32)
    wsl = [S.sb(f'wsl{i}', [128, 8, 512], BF16) for i in range(3)]
    HT = S.sb('HT', [128, 22, 512], BF16)
    wd = [S.sb(f'wd{i}', [128, 22, 512], BF16) for i in range(2)]
    sg = [S.sb(f'sg{i}', [128, 512], F32) for i in range(2)]
    osb = [S.sb(f'osb{i}', [128, D], F32) for i in range(2)]
    B = [S.ps(f'B{i}') for i in range(8)]
    mod = P.mod[l]
    pss = B[2]
    tiles = [ti for ti in range(NT) if not (last and ti == 0)]
    cnt = {'nsl': 0}

    def prep(idx):
        ti = tiles[idx]
        mix, h, xn = mixb[idx % 2], hb[idx % 2], xnb[idx % 2]
        col = tile_col(ti)
        tsl = slice(ti * 512, (ti + 1) * 512)
        S.dma(mix[:], P.mixT[:, tsl].rearrange("(k p) t -> p k t", p=128), [], [mix])
        hsrc = P.xinT if l == 0 else P.hT
        S.dma(h[:], hsrc[:, tsl].rearrange("(k p) t -> p k t", p=128), [P.hT.s(ti)], [h])
        for dc in range(8):
            p_ = B[dc % 2]
            for k in range(8):
                S.mm(p_[:], wout[:, k, dc * 128:(dc + 1) * 128], mix[:, k, :], k == 0, k == 7, [wout, mix], [p_])
            S.stt(h[:, dc, :], p_[:], mod[:, 16 + dc, col:col + 1], h[:, dc, :], ALU.mult, ALU.add, [p_, mod, h], [h])

    def prep_b(idx):
        ti = tiles[idx]
        mix, h, xn = mixb[idx % 2], hb[idx % 2], xnb[idx % 2]
        col = tile_col(ti)
        for k in range(8):
            S.act(sq[k % 2][:], h[:, k, :], AF.Square, [h], [sq[k % 2]])
            S.mm(pss[:], P.ones_f[:], sq[k % 2][:], k == 0, k == 7, [P.ones_f, sq[k % 2]], [pss])
        S.act(rstd[:], pss[:], AF.Sqrt, [pss], [rstd], scale=1.0 / D, bias=EPS)
        S.op('dve', lambda e: e.reciprocal(out=rstd[:], in_=rstd[:]), [rstd], [rstd])
        for k in range(8):
            t_ = tmpn[k % 2]
            S.tt('dve', t_[:], h[:, k, :], rstd[:], ALU.mult, [h, rstd], [t_])
            S.act(xn[:, k, :], t_[:], AF.Identity, [t_, P.A2[l], mod], [xn.s(k)],
                  scale=P.A2[l][:, k, col:col + 1], bias=mod[:, 24 + k, col:col + 1])

    def up(idx):
        xn = xnb[idx % 2]
        xr = [xn.s(k) for k in range(8)]
        for s_ in range(11):
            w = wsl[cnt['nsl'] % 3]
            cnt['nsl'] += 1
            S.dma(w[:, :, 0:256], P.w_up_b[l, :, 256 * s_:256 * s_ + 256].rearrange("(k p) c -> p k c", p=128), [], [w])
            S.dma(w[:, :, 256:512], P.w_up_b[l, :, DFF + 256 * s_:DFF + 256 * s_ + 256].rearrange("(k p) c -> p k c", p=128), [], [w])
            for jj in range(2):
                j = 2 * s_ + jj
                pg_ = B[(j % 2) * 2]
                pu_ = B[(j % 2) * 2 + 1]
                for k in range(8):
                    S.mm(pg_[:], w[:, k, jj * 128:(jj + 1) * 128], xn[:, k, :], k == 0, k == 7, [w, xr[k]], [pg_])
                for k in range(8):
                    S.mm(pu_[:], w[:, k, 256 + jj * 128:256 + (jj + 1) * 128], xn[:, k, :], k == 0, k == 7, [w, xr[k]], [pu_])
                S.act(sg[j % 2][:], pg_[:], AF.Silu, [pg_], [sg[j % 2]])
                S.tt('dve', HT[:, j, :], pu_[:], sg[j % 2][:], ALU.mult, [pu_, sg[j % 2]], [HT.s(j)])

    def down(idx, halves):
        ti = tiles[idx]
        h = hb[idx % 2]
        col = tile_col(ti)
        tsl = slice(ti * 512, (ti + 1) * 512)
        hr = [HT.s(j) for j in range(22)]
        for half in halves:
            w = wd[half]
            S.dma(w[:, 0:11, :], P.w_dn_b[l, 0:11 * 128, half * 512:(half + 1) * 512].rearrange("(j p) c -> p j c", p=128), [], [w])
            S.dma(w[:, 11:22, :], P.w_dn_b[l, 11 * 128:22 * 128, half * 512:(half + 1) * 512].rearrange("(j p) c -> p j c", p=128), [], [w])
            for dd in range(4):
                dc = half * 4 + dd
                p_ = B[4 + dd]
                for j in range(22):
                    S.mm(p_[:], w[:, j, dd * 128:(dd + 1) * 128], HT[:, j, :], j == 0, j == 21, [w, hr[j]], [p_])
                S.stt(h[:, dc, :], p_[:], mod[:, 40 + dc, col:col + 1], h[:, dc, :], ALU.mult, ALU.add, [p_, mod, h], [h])
        if 1 not in halves:
            return
        if not last:
            S.dma(P.hT[:, tsl].rearrange("(k p) t -> p k t", p=128), h[:], [h], [P.hT.s(ti)])
        else:
            for k in range(8):
                S.act(sq[k % 2][:], h[:, k, :], AF.Square, [h], [sq[k % 2]])
                S.mm(pss[:], P.ones_f[:], sq[k % 2][:], k == 0, k == 7, [P.ones_f, sq[k % 2]], [pss])
            S.act(rstd[:], pss[:], AF.Sqrt, [pss], [rstd], scale=1.0 / D, bias=EPS)
            S.op('dve', lambda e: e.reciprocal(out=rstd[:], in_=rstd[:]), [rstd], [rstd])
            for k in range(8):
                S.stt(h[:, k, :], h[:, k, :], fg[:, k:k + 1], rstd[:], ALU.mult, ALU.mult, [h, fg, rstd], [h])
            S.dma(P.out[:, (ti - 1) * 512:ti * 512].rearrange("(k p) t -> p k t", p=128), h[:], [h], [P.out.s(ti)])

    prep(0)
    prep_b(0)
    for idx in range(len(tiles)):
        up(idx)
        nxt = idx + 1 < len(tiles)
        if nxt:
            prep(idx + 1)
        down(idx, [0])
        if nxt:
            prep_b(idx + 1)
        down(idx, [1])
    S.emit()


def build(dbg=False, upto=None):
    K = Kern()
    P = declare(K, dbg)
    P.hyK = {'L': K.dram('hyK_L', [2, LL, 512], BF16), 'C': K.dram('hyK_C', [2, LC, 512], BF16)}
    P.s5y = K.dram('s5y', [2, NB, 256, LB], F32)
    P.u8T = K.dram('u8T', [256, 8, T // 8], BF16)
    P.hyX = {'L': K.dram('hyX_L', [3, LL, 512], BF16), 'C': K.dram('hyX_C', [3, LC, 512], BF16)}
    steps = [('p0', lambda: stage_p0(P))]
    for l in range(2):
        last = (l == 1)
        steps.append((f'ip{l}', lambda l=l: stage_ip(P, l)))
        steps.append((f'ssd{l}', lambda l=l: stage_ssd(P, l)))
        steps.append((f'ret{l}', lambda l=l: stage_ret(P, l)))
        steps.append((f's5m{l}', lambda l=l: stage_s5(P, l)))
        steps.append((f's5{l}', lambda l=l: stage_s5_epi(P, l)))
        for tag in (['L'] if last else ['L', 'C']):
            steps.append((f'hyp{l}{tag}', lambda l=l, tag=tag: stage_hy_fp(P, l, tag)))
            steps.append((f'hyc{l}{tag}', lambda l=l, tag=tag: stage_hy_conv(P, l, tag, side=(lambda S: mod_body(S, P, 1, sw=256)) if (l == 0 and tag == 'L') else None)))
        steps.append((f'of{l}', lambda l=l, last=last: stage_of(P, l, last)))
    for name, fn in steps:
        fn()
        if upto is not None and name == upto:
            break
    K.close()
    return K, P


def _f32(a):
    return np.ascontiguousarray(np.asarray(a, dtype=np.float32))


def shared_inputs(inp):
    g = {}
    f = _f32
    g['mod_w'] = f(inp['mod_w'])
    g['mod_bT'] = f(inp['mod_b'].reshape(2, 48, 128).transpose(0, 2, 1))
    g['norm1_gT'] = f(inp['norm1_g'].reshape(2, 8, 128).transpose(0, 2, 1))
    g['norm2_gT'] = f(inp['norm2_g'].reshape(2, 8, 128).transpose(0, 2, 1))
    g['final_gT'] = f(inp['final_norm_g'].reshape(8, 128).T)
    g['w_in'] = f(inp['w_in'])
    g['w_out'] = f(inp['w_out'])
    g['ffn_w_up'] = f(inp['ffn_w_up'])
    g['ffn_w_down'] = f(inp['ffn_w_down'])
    cw = np.zeros((2, 128, 4, 4), np.float32)
    w = np.asarray(inp['ssd_conv_w'])
    bb = np.asarray(inp['ssd_conv_b'])
    for l in range(2):
        for ci, (c0, n) in enumerate(((0, 128), (128, 128), (256, 64), (320, 64))):
            cw[l, :n, ci, 0:3] = w[l, :, c0:c0 + n].T
            cw[l, :n, ci, 3] = bb[l, c0:c0 + n]
    g['ssd_cw'] = cw
    g['ssd_alog8'] = f(inp['ssd_a_log'].reshape(2, 8, 1))
    g['ssd_dtb8'] = f(inp['ssd_dt_bias'].reshape(2, 8, 1))
    g['ssd_dexp'] = f(np.repeat(np.asarray(inp['ssd_d']), 64, axis=1))
    g['ssd_norm_g'] = f(inp['ssd_norm_g'])
    hw = np.zeros((2, 128, 6, 4), np.float32)
    w = np.asarray(inp['hy_conv_w'])
    bb = np.asarray(inp['hy_conv_b'])
    for l in range(2):
        for k in range(6):
            hw[l, :, k, 0:3] = w[l, :, k * 128:(k + 1) * 128].T
            hw[l, :, k, 3] = bb[l, k * 128:(k + 1) * 128]
    g['hy_cw'] = hw
    g['hy_w1'] = f(inp['hy_w1'])
    g['hy_b1c'] = f(inp['hy_b1'].reshape(2, 64, 1))
    g['hy_freqc'] = f(inp['hy_freq'].reshape(2, 64, 1))
    g['hy_w2'] = f(inp['hy_w2'])
    g['hy_b2c'] = f(inp['hy_b2'].reshape(2, 64, 1))
    g['hy_w3'] = f(inp['hy_w3'])
    g['hy_bias'] = f(inp['hy_bias'].reshape(2, 512))
    g['ret_decay8'] = f(inp['ret_decay'].reshape(2, 8))

    def smaj(a):
        a = np.asarray(a)
        lead = a.shape[:-2]
        return f(a.reshape(lead + (8, 128)).swapaxes(-1, -2))
    g['s5_are'] = smaj(inp['s5_a_re'])
    g['s5_aim'] = smaj(inp['s5_a_im'])
    g['s5_ldt'] = smaj(np.repeat(np.asarray(inp['s5_log_dt'])[..., None], 64, axis=-1))
    g['s5_bre'] = f(np.asarray(inp['s5_b_re']).reshape(2, 8, 128, 16).transpose(0, 2, 1, 3))
    g['s5_bim'] = f(np.asarray(inp['s5_b_im']).reshape(2, 8, 128, 16).transpose(0, 2, 1, 3))
    def cmaj(a):
        a = np.asarray(a).transpose(0, 1, 2, 4, 3)
        return f(a.reshape(2, 2, 8, 128, 16).transpose(0, 1, 3, 2, 4))
    g['s5_cre'] = cmaj(inp['s5_c_re'])
    g['s5_cim'] = cmaj(inp['s5_c_im'])
    g['s5_dT'] = f(np.asarray(inp['s5_d']).reshape(2, 2, 128).transpose(0, 2, 1))
    g['s5_glu_w'] = f(inp['s5_glu_w'])
    g['s5_glu_bT'] = f(np.asarray(inp['s5_glu_b']).reshape(2, 2, 128).transpose(0, 2, 1))
    for name, arr in host_consts().items():
        g['k_' + name] = arr
    return g


def core_inputs(inp, core):
    b0 = core * NB
    x = np.asarray(inp['x'])[b0:b0 + NB].reshape(NB * LL, D)
    ctx = np.asarray(inp['ctx'])[b0:b0 + NB].reshape(NB * LC, D)
    d = {}
    d['xinT'] = _f32(np.concatenate([ctx, x], axis=0).T)
    cc = np.concatenate([np.asarray(inp['c'])[b0:b0 + NB], np.asarray(inp['c_ctx'])[None, :]], axis=0)
    d['cT'] = _f32(cc.reshape(3, 8, 128).transpose(2, 1, 0))
    return d


_BUILD = {}


def kernel(**inputs):
    if 'nc' not in _BUILD:
        K, P = build()
        _BUILD['nc'] = K.nc
    nc = _BUILD['nc']
    sh = shared_inputs(inputs)
    in_maps = []
    for core in range(8):
        m = dict(sh)
        m.update(core_inputs(inputs, core))
        in_maps.append(m)
    res = run_bass_kernel_spmd(nc, in_maps, core_ids=list(range(8)))
    outs = [np.ascontiguousarray(np.asarray(r['out']).T).reshape(NB, LL, D) for r in res.results]
    return np.concatenate(outs, axis=0).astype(np.float32)
```

```python
import math
import os
from contextlib import ExitStack
import numpy as np
import ml_dtypes
import concourse.bass as bass
import concourse.mybir as mybir
from concourse.bass_utils import run_bass_kernel_spmd

F32 = mybir.dt.float32
BF16 = mybir.dt.bfloat16
AF = mybir.ActivationFunctionType
ALU = mybir.AluOpType
AX = mybir.AxisListType

ENGS = ['sp', 'pe', 'dve', 'act', 'pool']
NDMA = 8
NQ = 0
SAME_ENGINE_SYNC = True

D = 1024
NB = 2
LC = 256
LL = 2048
T = NB * (LC + LL)
NT = T // 512
EPS = 1e-6
DIN = 2696
DFF = 2816
C_Z = [(0, 128), (128, 128)]
C_X = [(256, 128), (384, 128)]
C_B = (512, 64)
C_C = (576, 64)
C_DT = (640, 8)
OFF_HY = 648
OFF_RET = 1416
OFF_S5 = 2440


def ctx_off(b):
    return LC * b


def lat_off(b):
    return NB * LC + LL * b


def tile_col(ti):
    if ti == 0:
        return 2
    return (ti - 1) // 4


class Trk:
    __slots__ = ('w', 'r')

    def __init__(self):
        self.w = None
        self.r = []


class Buf:
    def __init__(self, t, name, shape=None):
        self.t = t
        self.name = name
        self.trk = Trk()
        self.subs = {}
        self.shape = shape
        self.psum = False

    def __getitem__(self, idx):
        return self.t[idx]

    def s(self, key):
        if key not in self.subs:
            self.subs[key] = Buf(self.t, f"{self.name}.{key}", self.shape)
        return self.subs[key]


def AP(buf, off, dims):
    t = buf.t if isinstance(buf, Buf) else buf
    tt = t.tensor if isinstance(t, bass.AP) else t
    return bass.AP(tt, off, [list(d) for d in dims])


class Kern:
    def __init__(self):
        self.nc = bass.Bass("TRN2", target_bir_lowering=False)
        self.es = ExitStack()
        self.sems = {}
        for e in ENGS:
            self.sems[('e', e)] = self.es.enter_context(self.nc.semaphore(f"s_{e}"))
        for k in range(NDMA):
            self.sems[('d', k)] = self.es.enter_context(self.nc.semaphore(f"s_d{k}"))
        for k in range(NQ):
            self.sems[('q', k)] = self.es.enter_context(self.nc.semaphore(f"s_q{k}"))
        self.nstage = 0
        self.ninst = 0

    def dram(self, name, shape, dtype, kind="Internal"):
        t = self.nc.dram_tensor(name, list(shape), dtype, kind=kind)
        return Buf(t.ap(), name, shape)

    def sb(self, name, shape, dtype):
        t = self.es.enter_context(self.nc.sbuf_tensor(name, list(shape), dtype))
        return Buf(t, name, shape)

    def stage(self, name):
        self.nstage += 1
        return Stage(self, f"{name}{self.nstage}")

    def close(self):
        self.es.close()


class Stage:
    def __init__(self, K, name):
        self.K = K
        self.nc = K.nc
        self.name = name
        self.ops = {e: [] for e in ENGS}
        self.cnt = {e: 0 for e in ENGS}
        self.dma_n = 0
        self.qn = 0
        self.waited = {e: {} for e in ENGS}
        self.es = ExitStack()
        self.nbuf = 0
        self.touched = {}

    def sb(self, name, shape, dtype):
        self.nbuf += 1
        t = self.es.enter_context(self.nc.sbuf_tensor(f"{self.name}_{name}_{self.nbuf}", list(shape), dtype))
        return Buf(t, name, shape)

    def ps(self, name, shape=(128, 512), dtype=F32):
        self.nbuf += 1
        t = self.es.enter_context(self.nc.psum_tensor(f"{self.name}_{name}_{self.nbuf}", list(shape), dtype))
        b = Buf(t, name, shape)
        b.psum = True
        return b

    def op(self, eng, fn, reads=(), writes=(), dma=False):
        pr = [b for b in reads if b.psum]
        if pr:
            reads = [b for b in reads if not b.psum]
            writes = list(writes) + [b for b in pr if b not in writes]
        deps = []
        own = ('e', eng)
        for b in reads:
            if b.trk.w is not None:
                deps.append(b.trk.w)
        for b in writes:
            if b.trk.w is not None:
                deps.append(b.trk.w)
            deps.extend(b.trk.r)
        waits = {}
        for (sk, val) in deps:
            if sk == ('e', eng) and (eng == 'pe' or not SAME_ENGINE_SYNC):
                continue
            if waits.get(sk, 0) < val:
                waits[sk] = val
        if dma and eng == 'pool' and NQ > 0:
            assert self.qn < NQ
            done = (('q', self.qn), 16)
            self.qn += 1
            inc = 16
        elif dma:
            k = self.dma_n % NDMA
            gen = self.dma_n // NDMA
            self.dma_n += 1
            sk = ('d', k)
            if gen > 0 and waits.get(sk, 0) < 16 * gen:
                waits[sk] = 16 * gen
            done = (sk, 16 * (gen + 1))
            inc = 16
        else:
            self.cnt[eng] += 1
            done = (('e', eng), self.cnt[eng])
            inc = 1
        wl = []
        for sk, val in waits.items():
            if self.waited[eng].get(sk, 0) >= val:
                continue
            self.waited[eng][sk] = val
            wl.append((sk, val))
        self.ops[eng].append((wl, fn, done[0], inc))
        for b in reads:
            b.trk.r.append(done)
            self.touched[id(b)] = b
        for b in writes:
            b.trk.w = done
            b.trk.r = []
            self.touched[id(b)] = b
        return done

    def dma(self, out_ap, in_ap, R=(), W=(), eng='sp', **kw):
        return self.op(eng, lambda e: e.dma_start(out=out_ap, in_=in_ap, **kw), R, W, dma=True)

    def mm(self, out, lhsT, rhs, start, stop, R, W):
        return self.op('pe', lambda e: e.matmul(out, lhsT=lhsT, rhs=rhs, start=start, stop=stop), R, W)

    def tr(self, out, in_, ident, R, W):
        return self.op('pe', lambda e: e.transpose(out=out, in_=in_, identity=ident), R, W)

    def act(self, out, in_, func, R, W, **kw):
        return self.op('act', lambda e: e.activation(out=out, in_=in_, func=func, **kw), R, W)

    def tt(self, eng, out, in0, in1, op, R, W):
        return self.op(eng, lambda e: e.tensor_tensor(out=out, in0=in0, in1=in1, op=op), R, W)

    def ts(self, eng, out, in0, s1, s2, op0, op1, R, W):
        if op1 is None:
            return self.op(eng, lambda e: e.tensor_scalar(out=out, in0=in0, scalar1=s1, scalar2=None, op0=op0), R, W)
        return self.op(eng, lambda e: e.tensor_scalar(out=out, in0=in0, scalar1=s1, scalar2=s2, op0=op0, op1=op1), R, W)

    def stt(self, out, in0, scalar, in1, op0, op1, R, W):
        return self.op('dve', lambda e: e.scalar_tensor_tensor(out=out, in0=in0, scalar=scalar, in1=in1, op0=op0, op1=op1), R, W)

    def cp(self, eng, out, in_, R, W):
        if eng == 'act':
            return self.op('act', lambda e: e.activation(out=out, in_=in_, func=AF.Identity), R, W)
        return self.op(eng, lambda e: e.tensor_copy(out=out, in_=in_), R, W)

    def ms(self, eng, ap, val, W):
        return self.op(eng, lambda e: e.memset(ap, val), (), W)

    def emit(self):
        nc = self.nc
        sems = self.K.sems
        with nc.Block() as blk:
            def clr(e):
                for h in sems.values():
                    e.sem_clear(h)
            blk.sync(clr)
        ops = self.ops
        dma_n = self.dma_n
        qn = self.qn

        def mk(engname):
            def body(e):
                for (wl, fn, sk, inc) in ops[engname]:
                    for (wsk, val) in wl:
                        e.wait_ge(sems[wsk], val)
                    fn(e).then_inc(sems[sk], inc)
                if engname == 'sp':
                    for k in range(min(NDMA, dma_n)):
                        tot = (dma_n - k + NDMA - 1) // NDMA
                        e.wait_ge(sems[('d', k)], 16 * tot)
                    for k in range(qn):
                        e.wait_ge(sems[('q', k)], 16)
            return body
        with nc.Block() as blk:
            blk.sync(mk('sp'))
            blk.tensor(mk('pe'))
            blk.vector(mk('dve'))
            blk.scalar(mk('act'))
            blk.gpsimd(mk('pool'))
        n = 0
        for e in ENGS:
            n += len(ops[e]) + sum(len(o[0]) for o in ops[e])
        self.K.ninst += n
        for b in self.touched.values():
            b.trk.w = None
            b.trk.r = []
        self.touched = {}
        self.es.close()
        self.ops = None


_CONST_CACHE = {}


def dft_tables(L):
    nt = L // 128
    idx = np.arange(L, dtype=np.int64)
    prod = (idx[:, None] * idx[None, :]) % (2 * L)
    ang = np.pi * prod.astype(np.float64) / L
    Cfull = np.cos(ang)
    Sfull = -np.sin(ang)
    alt = np.where(idx % 2 == 0, 1.0, -1.0)
    Sf_full = Sfull.copy()
    Sf_full[0, :] = alt
    def blk_fwd(M):
        return M.reshape(nt, 128, nt, 128).transpose(0, 3, 2, 1)
    def blk_inv(M):
        return M.reshape(nt, 128, nt, 128).transpose(2, 1, 0, 3)
    C = np.ascontiguousarray(blk_fwd(Cfull)).astype(ml_dtypes.bfloat16)
    Sf = np.ascontiguousarray(blk_fwd(Sf_full)).astype(ml_dtypes.bfloat16)
    Si = np.ascontiguousarray(blk_inv(Sf_full)).astype(ml_dtypes.bfloat16)
    return C, Sf, Si


def hy_feats(L):
    f32 = np.float32
    t = np.linspace(0.0, 1.0, L, dtype=f32)[:, None]
    w = (2.0 * math.pi * np.arange(L, dtype=f32)[:, None] / L).astype(f32)
    bands = np.linspace(1e-4, 16 - 1, 16, dtype=f32)[None, :]
    feats = np.concatenate([t, np.cos(bands * w), -np.sin(bands * w)], axis=-1).astype(f32)
    max_decay = math.log(1e-2) / 0.3
    min_decay = math.log(1e-2) / 1.5
    deltas = np.abs(np.linspace(min_decay, max_decay, 1024, dtype=f32))
    dec = np.exp(-t * deltas).astype(f32)
    return np.ascontiguousarray(feats.T), dec


def host_consts():
    if _CONST_CACHE:
        return _CONST_CACHE
    c = {}
    c['ident_b'] = np.eye(128).astype(ml_dtypes.bfloat16)
    c['ident_f'] = np.eye(128, dtype=np.float32)
    c['ones_f'] = np.ones((128, 128), np.float32)
    k = np.arange(128)
    c['ule'] = (k[:, None] <= k[None, :]).astype(np.float32)
    c['uge'] = (k[:, None] >= k[None, :]).astype(np.float32)
    sel = np.zeros((8, 8, 128), np.float32)
    for i in range(8):
        sel[i, i, :] = 1.0
    c['sel'] = sel
    mn = np.zeros((2, 128, 128), np.float32)
    mn[0][k[:, None] > k[None, :]] = -30000.0
    mn[1][k[:, None] < k[None, :]] = -30000.0
    c['mneg4'] = np.ascontiguousarray(np.broadcast_to(mn.transpose(1, 0, 2)[:, :, None, :], (128, 2, 4, 128))).astype(ml_dtypes.bfloat16)
    c['idiff'] = (k[None, :] - k[:, None]).astype(np.float32)
    c['ramp'] = np.broadcast_to((k[None, :] + 1).astype(np.float32), (128, 128)).copy()
    c['pidx'] = k[:, None].astype(np.float32).copy()
    t = np.arange(LL)
    row = (t // 64).astype(np.float32)
    col = (t % 64).astype(np.float32)
    inv = (10000.0 ** (-np.arange(16, dtype=np.float32) / 16)).astype(np.float32)
    cosT = np.zeros((128, LL), np.float32)
    sinT = np.zeros((128, LL), np.float32)
    Pm = np.zeros((128, 128), np.float32)
    for p in range(128):
        d = p % 64
        half = d // 32
        i = d % 16
        pos = row if half == 0 else col
        ang = (pos * inv[i]).astype(np.float32)
        cosT[p] = np.cos(ang)
        sinT[p] = np.sin(ang)
        if (d % 32) < 16:
            Pm[p + 16, p] = -1.0
        else:
            Pm[p - 16, p] = 1.0
    c['rope_cos'] = cosT
    c['rope_sin'] = sinT
    c['rope_p'] = Pm.astype(ml_dtypes.bfloat16)
    for L, tag in ((LL, 'L'), (LC, 'C')):
        C, Sf, Si = dft_tables(L)
        c[f'dftc_{tag}'] = C
        c[f'dftsf_{tag}'] = Sf
        c[f'dftsi_{tag}'] = Si
        ft, dec = hy_feats(L)
        c[f'feat_{tag}'] = ft
        c[f'dec_{tag}'] = dec
        nt = L // 128
        wf = np.full((128, nt), 2.0 / (2 * L), np.float32)
        wf[0, 0] = 1.0 / (2 * L)
        c[f'wf_{tag}'] = wf
        alt = np.where(np.arange(L) % 2 == 0, 1.0, -1.0).astype(np.float32)
        c[f'alt_{tag}'] = np.ascontiguousarray(alt.reshape(nt, 128).T).astype(ml_dtypes.bfloat16)
    _CONST_CACHE.update(c)
    return _CONST_CACHE

class Prog:
    pass


def declare(K, dbg):
    P = Prog()
    P.K = K
    cst = host_consts()
    P.cin = {}

    def inp(name, shape, dt=F32):
        b = K.dram(name, shape, dt, kind="ExternalInput")
        setattr(P, name, b)
        return b
    inp('xinT', [D, T])
    inp('cT', [128, 8, 3])
    inp('mod_w', [2, D, 6 * D])
    inp('mod_bT', [2, 128, 48])
    inp('norm1_gT', [2, 128, 8])
    inp('norm2_gT', [2, 128, 8])
    inp('final_gT', [128, 8])
    inp('w_in', [2, D, DIN])
    inp('w_out', [2, D, D])
    inp('ffn_w_up', [2, D, 2 * DFF])
    inp('ffn_w_down', [2, DFF, D])
    inp('ssd_cw', [2, 128, 4, 4])
    inp('ssd_alog8', [2, 8, 1])
    inp('ssd_dtb8', [2, 8, 1])
    inp('ssd_dexp', [2, 256])
    inp('ssd_norm_g', [2, 256])
    inp('hy_cw', [2, 128, 6, 4])
    inp('hy_w1', [2, 33, 64])
    inp('hy_b1c', [2, 64, 1])
    inp('hy_freqc', [2, 64, 1])
    inp('hy_w2', [2, 64, 64])
    inp('hy_b2c', [2, 64, 1])
    inp('hy_w3', [2, 64, 1024])
    inp('hy_bias', [2, 512])
    inp('ret_decay8', [2, 8])
    inp('s5_are', [2, 2, 128, 8])
    inp('s5_aim', [2, 2, 128, 8])
    inp('s5_ldt', [2, 2, 128, 8])
    inp('s5_bre', [2, 128, 8, 16])
    inp('s5_bim', [2, 128, 8, 16])
    inp('s5_cre', [2, 2, 128, 8, 16])
    inp('s5_cim', [2, 2, 128, 8, 16])
    inp('s5_dT', [2, 128, 2])
    inp('s5_glu_w', [2, 256, 256])
    inp('s5_glu_bT', [2, 128, 2])
    for name, arr in cst.items():
        dt = BF16 if arr.dtype == ml_dtypes.bfloat16 else F32
        b = K.dram('k_' + name, list(arr.shape), dt, kind="ExternalInput")
        setattr(P, 'k_' + name, b)
    kind_dbg = "ExternalOutput" if dbg else "Internal"
    P.out = K.dram('out', [D, NB * LL], F32, kind="ExternalOutput")
    P.hT = K.dram('hT', [D, T], F32, kind=kind_dbg)
    P.uT = K.dram('uT', [DIN, T], BF16, kind=kind_dbg)
    P.udt = K.dram('udt', [8, T], F32, kind=kind_dbg)
    P.mixT = K.dram('mixT', [D, T], BF16, kind=kind_dbg)
    P.w_out_b = K.dram('w_out_b', [2, D, D], BF16)
    P.w_up_b = K.dram('w_up_b', [2, D, 2 * DFF], BF16)
    P.w_dn_b = K.dram('w_dn_b', [2, DFF, D], BF16)
    P.glu_w_b = K.dram('glu_w_b', [2, 256, 256], BF16)
    P.ident_b = K.sb('ident_b', [128, 128], BF16)
    P.ident_f = K.sb('ident_f', [128, 128], F32)
    P.ones_f = K.sb('ones_f', [128, 128], F32)
    P.mod = [K.sb(f'mod{l}', [128, 48, 3], F32) for l in range(2)]
    P.scs = K.sb('scs', [128, 8, 3], F32)
    P.A1 = [K.sb(f'A1_{l}', [128, 8, 3], F32) for l in range(2)]
    P.A2 = [K.sb(f'A2_{l}', [128, 8, 3], F32) for l in range(2)]
    return P


def conv_job(S, P, l, engs, small_only=False, bw=1024, nbuf=4):
    stf = [S.sb(f'cvf{l}{i}', [128, bw], F32) for i in range(nbuf)]
    stb = [S.sb(f'cvb{l}{i}', [128, bw], BF16) for i in range(nbuf)]
    if small_only:
        jobs = ((P.glu_w_b, P.s5_glu_w, 256, 256),)
    else:
        jobs = ((P.w_out_b, P.w_out, D, D), (P.w_up_b, P.ffn_w_up, D, 2 * DFF), (P.w_dn_b, P.ffn_w_down, DFF, D))
    blocks = []
    for (dst, src, R_, C_) in jobs:
        for r0 in range(0, R_, 128):
            for c0 in range(0, C_, bw):
                blocks.append((dst, src, r0, c0, min(bw, C_ - c0)))

    def store(i):
        dst, src, r0, c0, cw_ = blocks[i]
        b_ = stb[i % nbuf]
        S.dma(dst[l, r0:r0 + 128, c0:c0 + cw_], b_[:, 0:cw_], [b_], [dst.s((l, r0, c0))])
    for i, (dst, src, r0, c0, cw_) in enumerate(blocks):
        f_ = stf[i % nbuf]
        b_ = stb[i % nbuf]
        S.dma(f_[:, 0:cw_], src[l, r0:r0 + 128, c0:c0 + cw_], [], [f_])
        S.cp(engs[i % len(engs)], b_[:, 0:cw_], f_[:, 0:cw_], [f_], [b_])
        if i >= 2:
            store(i - 2)
        yield
    for i in range(max(0, len(blocks) - 2), len(blocks)):
        store(i)
    yield


def mod_body(S, P, l, sw=512):
    scs = P.scs
    wsl = [S.sb(f'mwsl{l}{i}', [128, 8, sw], F32) for i in range(2)]
    pm = S.ps(f'pmod{l}')
    mb = S.sb(f'mb{l}', [128, 48], F32)
    g1 = S.sb(f'g1{l}', [128, 8], F32)
    g2 = S.sb(f'g2{l}', [128, 8], F32)
    tmp = S.sb(f'mtmp{l}', [128, 8, 3], F32)
    n = 0
    for cs in range(6144 // sw):
        w = wsl[n % 2]
        n += 1
        S.dma(w[:], P.mod_w[l, :, cs * sw:(cs + 1) * sw].rearrange("(k p) c -> p k c", p=128), [], [w])
        for j in range(sw // 128):
            fc = cs * (sw // 128) + j
            for k in range(8):
                S.mm(pm[:, fc * 3:(fc + 1) * 3], w[:, k, j * 128:(j + 1) * 128], scs[:, k, :], k == 0, k == 7, [w, scs], [pm])
        yield
    S.dma(mb[:], P.mod_bT[l], [], [mb])
    S.tt('dve', P.mod[l][:], pm[:, 0:144].rearrange("p (j c) -> p j c", c=3),
         AP(mb, 0, [[48, 128], [1, 48], [0, 3]]), ALU.add, [pm, mb], [P.mod[l]])
    S.dma(g1[:], P.norm1_gT[l], [], [g1])
    S.dma(g2[:], P.norm2_gT[l], [], [g2])
    for (A, g, j0) in ((P.A1[l], g1, 8), (P.A2[l], g2, 32)):
        S.ts('dve', tmp[:], P.mod[l][:, j0:j0 + 8, :], 1.0, None, ALU.add, None, [P.mod[l]], [tmp])
        S.tt('dve', A[:], tmp[:], AP(g, 0, [[8, 128], [1, 8], [0, 3]]), ALU.mult, [tmp, g], [A])
    yield


def stage_p0(P):
    K = P.K
    S = K.stage("p0")
    S.dma(P.ident_b[:], P.k_ident_b[:], [P.k_ident_b], [P.ident_b])
    S.dma(P.ident_f[:], P.k_ident_f[:], [P.k_ident_f], [P.ident_f])
    S.dma(P.ones_f[:], P.k_ones_f[:], [P.k_ones_f], [P.ones_f])
    for l_ in range(2):
        for _ in conv_job(S, P, l_, ('dve', 'pool'), small_only=True, bw=256, nbuf=2):
            pass
    cts = S.sb('cts', [128, 8, 3], F32)
    S.dma(cts[:], P.cT[:], [P.cT], [cts])
    S.act(P.scs[:], cts[:], AF.Silu, [cts], [P.scs])
    for _ in mod_body(S, P, 0):
        pass
    S.emit()


def norm_mod(S, P, h, xn, A, shift_j0, l, col, sq, pss, rstd, tmpn):
    S.act(sq[:], h[:], AF.Square, [h], [sq])
    for k in range(8):
        S.mm(pss[:], P.ones_f[:], sq[:, k, :], k == 0, k == 7, [P.ones_f, sq], [pss])
    S.act(rstd[:], pss[:], AF.Sqrt, [pss], [rstd], scale=1.0 / D, bias=EPS)
    S.op('dve', lambda e: e.reciprocal(out=rstd[:], in_=rstd[:]), [rstd], [rstd])
    for k in range(8):
        S.tt('dve', tmpn[:, k, :], h[:, k, :], rstd[:], ALU.mult, [h, rstd], [tmpn.s(k)])
        S.act(xn[:, k, :], tmpn[:, k, :], AF.Identity, [tmpn.s(k), A, P.mod[l]], [xn.s(k)],
              scale=A[:, k, col:col + 1], bias=P.mod[l][:, shift_j0 + k, col:col + 1])


IP_CHUNKS = (C_Z + C_X + [C_B, C_C, C_DT] + [(OFF_HY + 128 * i, 128) for i in range(6)]
             + [(OFF_RET + 128 * i, 128) for i in range(8)] + [(OFF_S5 + 128 * i, 128) for i in range(2)])


def stage_ip(P, l):
    K = P.K
    S = K.stage(f"ip{l}")
    win = S.sb('win', [128, 8, DIN], BF16)
    wst = [S.sb(f'wst{i}', [128, DIN], F32) for i in range(2)]
    for k in range(8):
        S.dma(wst[k % 2][:], P.w_in[l, k * 128:(k + 1) * 128, :], [], [wst[k % 2]])
        S.cp(('act', 'dve')[k % 2], win[:, k, :], wst[k % 2][:], [wst[k % 2]], [win.s(k)])
    hb = [S.sb(f'h{i}', [128, 8, 512], F32) for i in range(2)]
    sq = S.sb('sq', [128, 8, 512], F32)
    tmpn = S.sb('tmpn', [128, 8, 512], F32)
    xnb = [S.sb(f'xn{i}', [128, 8, 512], BF16) for i in range(2)]
    rstd = S.sb('rstd', [128, 512], F32)
    pss = S.ps('pss')
    pu = [S.ps(f'pu{i}') for i in range(4)]
    ust = [S.sb(f'ust{i}', [128, 512], BF16) for i in range(4)]
    udts = S.sb('udts', [8, 512], F32)
    u8s = [S.sb(f'u8s{i}', [128, 8, 64], BF16) for i in range(2)]
    m = 0

    def prep(ti):
        h = hb[ti % 2]
        hsrc = P.xinT if l == 0 else P.hT
        S.dma(h[:], hsrc[:, ti * 512:(ti + 1) * 512].rearrange("(k p) t -> p k t", p=128), [], [h])
        norm_mod(S, P, h, xnb[ti % 2], P.A1[l], 0, l, tile_col(ti), sq, pss, rstd, tmpn)
    prep(0)
    for ti in range(NT):
        xn = xnb[ti % 2]
        if ti + 1 < NT:
            prep(ti + 1)
        xr = [xn.s(k) for k in range(8)]
        for (c0, M) in IP_CHUNKS:
            p_ = pu[m % 4]
            u_ = ust[m % 4]
            m += 1
            for k in range(8):
                S.mm(p_[0:M, :], win[:, k, c0:c0 + M], xn[:, k, :], k == 0, k == 7, [win.s(k), xr[k]], [p_])
            if (c0, M) == C_DT:
                S.cp('dve', udts[:], p_[0:8, :], [p_], [udts])
                S.dma(P.udt[:, ti * 512:(ti + 1) * 512], udts[:], [udts], [P.udt.s(ti)])
            else:
                S.cp('act' if m % 2 else 'dve', u_[0:M, :], p_[0:M, :], [p_], [u_])
                S.dma(P.uT[c0:c0 + M, ti * 512:(ti + 1) * 512], u_[0:M, :], [u_], [P.uT.s((c0, ti))])
                if c0 >= OFF_S5:
                    u8_ = u8s[m % 2]
                    S.cp('pool', u8_[:], u_[:, :].rearrange("p (a j) -> p j a", j=8), [u_], [u8_])
                    S.dma(P.u8T[c0 - OFF_S5:c0 - OFF_S5 + 128, :, ti * 64:(ti + 1) * 64], u8_[:], [u8_], [P.u8T.s((c0, ti))])
    S.emit()

import os
KCUT = os.environ.get('KCUT', '')


class CutStage(Exception):
    pass


def cutpt(name):
    if KCUT == name:
        raise CutStage()


NCH = (LC + LL) // 128
LB = LC + LL
FWD_CHAIN = list(range(NCH))
BWD_CHAIN = [1, 0] + list(range(NCH - 1, 1, -1))


def load_seq(S, dst_ap_fn, P, row0, nrows, b, dstbuf, eng='sp'):
    S.dma(dst_ap_fn(0, LC), P.uT[row0:row0 + nrows, ctx_off(b):ctx_off(b) + LC], [P.uT], [dstbuf], eng=eng)
    S.dma(dst_ap_fn(LC, LB), P.uT[row0:row0 + nrows, lat_off(b):lat_off(b) + LL], [P.uT], [dstbuf], eng=eng)


def store_mix(S, P, mixs, row0, b, skip_ctx=False):
    for k in range(2):
        if not skip_ctx:
            S.dma(P.mixT[row0 + k * 128:row0 + (k + 1) * 128, ctx_off(b):ctx_off(b) + LC], mixs[:, k, 0:LC], [mixs], [P.mixT.s((row0, k, b, 0))])
        S.dma(P.mixT[row0 + k * 128:row0 + (k + 1) * 128, lat_off(b):lat_off(b) + LL], mixs[:, k, LC:LB], [mixs], [P.mixT.s((row0, k, b, 1))])


def stage_ssd(P, l):
    K = P.K
    S = K.stage(f"ssd{l}")
    try:
        _stage_ssd(P, l, S)
    except CutStage:
        pass
    S.emit()


def _stage_ssd(P, l, S):
    K = P.K
    ule = S.sb('ule', [128, 128], F32)
    uge = S.sb('uge', [128, 128], F32)
    S.dma(ule[:], P.k_ule[:], [P.k_ule], [ule])
    S.dma(uge[:], P.k_uge[:], [P.k_uge], [uge])
    mneg4 = S.sb('mneg4', [128, 2, 512], BF16)
    S.dma(mneg4[:], P.k_mneg4[:].rearrange("p r h i -> p r (h i)"), [], [mneg4])
    rbs = [S.sb(f'rb{i}', [8, 4, 128], F32) for i in range(4)]
    cw = S.sb('cw', [128, 4, 4], F32)
    S.dma(cw[:], P.ssd_cw[l], [P.ssd_cw], [cw])
    dtb = S.sb('dtb', [8, 1], F32)
    S.dma(dtb[:], P.ssd_dtb8[l], [P.ssd_dtb8], [dtb])
    a_bc = S.sb('a_bc', [128, 8], F32)
    S.dma(a_bc[:], AP(P.ssd_alog8, l * 8, [[0, 128], [1, 8]]), [P.ssd_alog8], [a_bc])
    S.act(a_bc[:], a_bc[:], AF.Exp, [a_bc], [a_bc])
    S.ts('dve', a_bc[:], a_bc[:], -1.0, None, ALU.mult, None, [a_bc], [a_bc])
    dsk = S.sb('dsk', [128, 256], F32)
    gnm = S.sb('gnm', [128, 256], F32)
    S.dma(dsk[:], AP(P.ssd_dexp, l * 256, [[0, 128], [1, 256]]), [P.ssd_dexp], [dsk])
    S.dma(gnm[:], AP(P.ssd_norm_g, l * 256, [[0, 128], [1, 256]]), [P.ssd_norm_g], [gnm])
    xr = S.sb('xr', [128, 2, LB], BF16)
    br = S.sb('br', [64, LB], BF16)
    cr = S.sb('cr', [64, LB], BF16)
    zr = S.sb('zr', [128, 2, LB], BF16)
    dtr = S.sb('dtr', [8, LB], F32)
    acc = S.sb('acc', [128, LB], F32)
    xa = S.sb('xa', [128, 2, LB], BF16)
    ba = S.sb('ba', [64, LB], BF16)
    ca = S.sb('ca', [64, LB], BF16)
    xtok = S.sb('xtok', [128, NCH, 256], BF16)
    btok = S.sb('btok', [128, NCH, 64], BF16)
    zs = S.sb('zs', [128, NCH, 256], BF16)
    dtk = S.sb('dtk', [128, NCH, 8], F32)
    lak = S.sb('lak', [128, 8], F32)
    acsk = S.sb('acsk', [128, NCH, 8], F32)
    acsT = S.sb('acsT', [8, NCH, 256], F32)
    etot = S.sb('etot', [64, NCH, 8], F32)
    gmt = S.sb('gmt', [128, NCH, 2, 128], F32)
    sball = S.sb('sball', [64, NCH, 256], F32)
    hinf = S.sb('hinf', [64, NCH, 256], BF16)
    hinb = S.sb('hinb', [64, NCH, 256], BF16)
    hst = S.sb('hst', [64, 256], F32)
    tmp8 = S.sb('tmp8', [128, 8], F32)
    wcol = S.sb('wcol', [128, 8], F32)
    xw = S.sb('xw', [128, 4, 2, 64], BF16)
    mixs = S.sb('mixs', [128, 2, LB], BF16)
    T1 = [S.sb(f'T1{i}', [128, 4, 128], F32) for i in range(2)]
    ST = [S.sb(f'ST{i}', [128, 4, 128], BF16) for i in range(4)]
    Ee = [S.sb(f'E{i}', [64, 4, 128], F32) for i in range(2)]
    CsT = [S.sb(f'CsT{i}', [64, 4, 128], BF16) for i in range(4)]
    acsk2 = S.sb('acsk2', [128, NCH, 8], F32)
    lnd = S.sb('lnd', [128, 8], F32)
    y1 = S.sb('y1', [128, 256], F32)
    y2 = S.sb('y2', [128, 256], F32)
    y3 = S.sb('y3', [128, 256], BF16)
    junk = S.sb('junk', [128, 256], F32)
    ss = S.sb('ss', [128, 1], F32)
    ptb = S.ps('ptb', [128, 1024], BF16)
    ptb2 = S.ps('ptb2', [128, 1024], BF16)
    ptf = S.ps('ptf')
    ptf2 = S.ps('ptf2')
    pg = S.ps('pg')
    pst = S.ps('pst')
    pa = [S.ps(f'pa{i}') for i in range(2)]
    py = pst

    for b in range(NB):
        for k in range(2):
            load_seq(S, lambda a, e, k=k: xr[:, k, a:e], P, C_X[k][0], 128, b, xr.s(k))
            load_seq(S, lambda a, e, k=k: zr[:, k, a:e], P, C_Z[k][0], 128, b, zr.s(k))
        load_seq(S, lambda a, e: br[:, a:e], P, C_B[0], 64, b, br)
        load_seq(S, lambda a, e: cr[:, a:e], P, C_C[0], 64, b, cr)
        S.dma(dtr[:, 0:LC], P.udt[:, ctx_off(b):ctx_off(b) + LC], [P.udt], [dtr])
        S.dma(dtr[:, LC:LB], P.udt[:, lat_off(b):lat_off(b) + LL], [P.udt], [dtr])
        cutpt('load')
        for (src, srcb, dst, dstb, ci, np_) in ((lambda a, e: xr[:, 0, a:e], xr.s(0), lambda a, e: xa[:, 0, a:e], xa.s(0), 0, 128),
                                                 (lambda a, e: xr[:, 1, a:e], xr.s(1), lambda a, e: xa[:, 1, a:e], xa.s(1), 1, 128),
                                                 (lambda a, e: br[:, a:e], br, lambda a, e: ba[:, a:e], ba, 2, 64),
                                                 (lambda a, e: cr[:, a:e], cr, lambda a, e: ca[:, a:e], ca, 3, 64)):
            for (a, e) in ((0, LC), (LC, LB)):
                S.ts('dve', acc[0:np_, a:e], src(a, e), cw[0:np_, ci, 1:2], cw[0:np_, ci, 3:4], ALU.mult, ALU.add, [srcb, cw], [acc])
                S.stt(acc[0:np_, a + 1:e], src(a, e - 1), cw[0:np_, ci, 0:1], acc[0:np_, a + 1:e], ALU.mult, ALU.add, [srcb, cw, acc], [acc])
                S.stt(acc[0:np_, a:e - 1], src(a + 1, e), cw[0:np_, ci, 2:3], acc[0:np_, a:e - 1], ALU.mult, ALU.add, [srcb, cw, acc], [acc])
            S.act(dst(0, LB)[0:np_], acc[0:np_, :], AF.Silu, [acc], [dstb])
        cutpt('conv')
        S.act(dtr[:], dtr[:], AF.Exp, [dtr, dtb], [dtr], bias=dtb[:, 0:1], scale=1.0)
        S.act(dtr[:], dtr[:], AF.Ln, [dtr], [dtr], bias=1.0, scale=1.0)
        for k in range(2):
            S.act(zr[:, k, :], zr[:, k, :], AF.Silu, [zr.s(k)], [zr.s(k)])
        cutpt('dt')
        S.ms('dve', hst[:], 0.0, [hst])
        xws = [xw, S.sb(f'xwb{b}', [128, 4, 2, 64], BF16)]

        def ssd_a1(ci):
            xw = xws[ci % 2]
            c0, c1 = ci * 128, (ci + 1) * 128
            for k in range(2):
                S.tr(ptb[:, k * 128:(k + 1) * 128], xa[:, k, c0:c1], P.ident_b[:], [xa.s(k), P.ident_b], [ptb])
            S.tr(ptb[:, 256:320], ba[0:64, c0:c1], P.ident_b[0:64, 0:64], [ba, P.ident_b], [ptb])
            S.cp('act', xtok[:, ci, :], ptb[:, 0:256], [ptb], [xtok.s(ci)])
            S.cp('act', btok[:, ci, :], ptb[:, 256:320], [ptb], [btok.s(ci)])
            for k in range(2):
                S.tr(ptb2[:, k * 128:(k + 1) * 128], zr[:, k, c0:c1], P.ident_b[:], [zr.s(k), P.ident_b], [ptb2])
            S.cp('act', zs[:, ci, :], ptb2[:, 0:256], [ptb2], [zs.s(ci)])
            S.tr(ptf[:, 0:8], dtr[0:8, c0:c1], P.ident_f[0:8, 0:8], [dtr, P.ident_f], [ptf])
            S.cp('dve', dtk[:, ci, :], ptf[:, 0:8], [ptf], [dtk.s(ci)])
            S.tt('dve', lak[:], dtk[:, ci, :], a_bc[:], ALU.mult, [dtk.s(ci), a_bc], [lak])
            S.mm(ptf[:, 8:12], ule[:], lak[:, 0:4], True, True, [ule, lak], [ptf])
            S.mm(ptf[:, 12:16], uge[:], lak[:, 4:8], True, True, [uge, lak], [ptf])
            S.mm(ptf[:, 16:24], P.ones_f[:], lak[:], True, True, [P.ones_f, lak], [ptf])
            S.mm(ptf2[0:8, 0:128], lak[:], ule[:], True, True, [ule, lak], [ptf2])
            S.mm(ptf2[0:8, 128:256], lak[:], uge[:], True, True, [uge, lak], [ptf2])
            S.cp('dve', acsk[:, ci, :], ptf[:, 8:16], [ptf], [acsk.s(ci)])
            S.cp('act', acsT[:, ci, :], ptf2[0:8, 0:256], [ptf2], [acsT.s(ci)])
            S.act(lnd[:], dtk[:, ci, :], AF.Ln, [dtk.s(ci)], [lnd])
            S.tt('dve', acsk2[:, ci, :], acsk[:, ci, :], lnd[:], ALU.subtract, [acsk.s(ci), lnd], [acsk2.s(ci)])
            S.tt('dve', tmp8[:], ptf[:, 16:24], acsk[:, ci, :], ALU.subtract, [ptf, acsk.s(ci)], [tmp8])
            S.act(tmp8[:], tmp8[:], AF.Exp, [tmp8], [tmp8])
            S.tt('dve', wcol[:], tmp8[:], dtk[:, ci, :], ALU.mult, [tmp8, dtk.s(ci)], [wcol])
            S.act(etot[:, ci, :], ptf[0:64, 16:24], AF.Exp, [ptf], [etot.s(ci)])
            S.mm(pg[:, 0:128], ba[0:64, c0:c1], ca[0:64, c0:c1], True, True, [ba, ca], [pg])
            S.tt('dve', gmt[:, ci, 0, :], pg[:, 0:128], ule[:], ALU.mult, [pg, ule], [gmt.s(ci)])
            S.tt('dve', gmt[:, ci, 1, :], pg[:, 0:128], uge[:], ALU.mult, [pg, uge], [gmt.s(ci)])
            S.tt('dve', xw[:], AP(xtok, ci * 256, [[NCH * 256, 128], [64, 4], [0, 2], [1, 64]]),
                 AP(wcol, 0, [[8, 128], [1, 4], [4, 2], [0, 64]]), ALU.mult, [xtok.s(ci), wcol], [xw])

        def ssd_a2(ci):
            xw = xws[ci % 2]
            c0, c1 = ci * 128, (ci + 1) * 128
            for h in range(4):
                S.mm(pst[0:64, h * 128:(h + 1) * 128], btok[:, ci, :], xw[:, h, :, :].rearrange("p r q -> p (r q)"), True, True, [btok.s(ci), xw], [pst])
            S.cp('act', hinf[:, ci, :], hst[:], [hst], [hinf.s(ci)])
            S.tt('dve', hst[:].rearrange("n (h p) -> n h p", h=4), hst[:].rearrange("n (h p) -> n h p", h=4),
                 AP(etot, ci * 8, [[NCH * 8, 64], [1, 4], [0, 64]]), ALU.mult, [hst, etot.s(ci)], [hst])
            S.tt('dve', hst[:].rearrange("n (h p) -> n h p", h=4), AP(pst, 0, [[512, 64], [128, 4], [1, 64]]),
                 hst[:].rearrange("n (h p) -> n h p", h=4), ALU.add, [hst, pst], [hst])
            S.cp('act', sball[:, ci, :].rearrange("n (h p) -> n h p", h=4), AP(pst, 64, [[512, 64], [128, 4], [1, 64]]), [pst], [sball.s(ci)])

        ssd_a1(0)
        for ci in FWD_CHAIN:
            if ci + 1 < NCH:
                ssd_a1(ci + 1)
            ssd_a2(ci)
        cutpt('passA')
        S.ms('dve', hst[:], 0.0, [hst])
        for ci in BWD_CHAIN:
            S.cp('act', hinb[:, ci, :], hst[:], [hst], [hinb.s(ci)])
            S.tt('dve', hst[:].rearrange("n (h p) -> n h p", h=4), hst[:].rearrange("n (h p) -> n h p", h=4),
                 AP(etot, ci * 8 + 4, [[NCH * 8, 64], [1, 4], [0, 64]]), ALU.mult, [hst, etot.s(ci)], [hst])
            S.tt('dve', hst[:], hst[:], sball[:, ci, :], ALU.add, [hst, sball.s(ci)], [hst])
        cutpt('bwd')
        pes = [ptf2, pg]
        pys = [pst, ptf]

        def ssd_rb(ci):
            for r in range(2):
                rb = rbs[2 * (ci % 2) + r]
                S.tt('dve', rb[:], AP(acsT, ci * 256 + r * 128, [[NCH * 256, 8], [0, 4], [1, 128]]),
                     AP(P.ident_f, 4 * r, [[128, 8], [1, 4], [0, 128]]), ALU.mult, [acsT.s(ci), P.ident_f], [rb])

        def ssd_front(ci):
            c0, c1 = ci * 128, (ci + 1) * 128
            for r in range(2):
                p_ = pa[r]
                pe_ = pes[r]
                t1 = T1[r]
                e_ = Ee[r]
                rb = rbs[2 * (ci % 2) + r]
                st = ST[(2 * (ci % 2) + r)]
                cs = CsT[(2 * (ci % 2) + r)]
                rb2 = rb[:].rearrange("c h i -> c (h i)")
                S.mm(p_[:, 0:512], P.ones_f[0:8, :], rb2, True, False, [P.ones_f, rb], [p_])
                S.mm(p_[:, 0:512], P.ident_b[:], mneg4[:, r, :], False, True, [P.ident_b, mneg4], [p_])
                S.mm(pe_[0:64, 0:512], P.ones_f[0:8, 0:64], rb2, True, True, [P.ones_f, rb], [pe_])
                pv = p_[:].rearrange("p (h i) -> p h i", h=4)
                S.tt('dve', t1[:], pv, AP(acsk2, ci * 8 + 4 * r, [[NCH * 8, 128], [1, 4], [0, 128]]), ALU.subtract, [p_, acsk2.s(ci)], [t1])
                S.act(t1[:], t1[:], AF.Exp, [t1], [t1])
                S.tt('dve', st[:], t1[:], AP(gmt, (ci * 2 + r) * 128, [[NCH * 256, 128], [0, 4], [1, 128]]), ALU.mult, [t1, gmt.s(ci)], [st])
                S.act(e_[:], pe_[0:64, :].rearrange("p (h i) -> p h i", h=4), AF.Exp, [pe_], [e_])
                S.tt('pool', cs[:], e_[:], AP(ca, c0, [[LB, 64], [0, 4], [1, 128]]), ALU.mult, [ca, e_], [cs])

        y3s = [y3, S.sb(f'y3b{b}', [128, 256], BF16)]

        def ssd_mm(ci):
            py = pys[ci % 2]
            sts = [ST[2 * (ci % 2)], ST[2 * (ci % 2) + 1]]
            css = [CsT[2 * (ci % 2)], CsT[2 * (ci % 2) + 1]]
            for h in range(4):
                hs_ = slice(h * 64, (h + 1) * 64)
                S.mm(py[:, hs_], sts[0][:, h, :], xtok[:, ci, hs_], True, False, [sts[0], xtok.s(ci)], [py])
                S.mm(py[:, hs_], css[0][:, h, :], hinf[:, ci, hs_], False, False, [css[0], hinf.s(ci)], [py])
                S.mm(py[:, hs_], sts[1][:, h, :], xtok[:, ci, hs_], False, False, [sts[1], xtok.s(ci)], [py])
                S.mm(py[:, hs_], css[1][:, h, :], hinb[:, ci, hs_], False, True, [css[1], hinb.s(ci)], [py])

        def ssd_epi(ci):
            py = pys[ci % 2]
            S.tt('pool', junk[:], xtok[:, ci, :], dsk[:], ALU.mult, [xtok.s(ci), dsk], [junk])
            S.tt('dve', y1[:], py[:, 0:256], junk[:], ALU.add, [py, junk], [y1])
            S.tt('dve', y2[:], y1[:], zs[:, ci, :], ALU.mult, [y1, zs.s(ci)], [y2])
            S.act(junk[:], y2[:], AF.Square, [y2], [junk, ss], accum_out=ss[:])
            S.act(ss[:], ss[:], AF.Ln, [ss], [ss], scale=1.0 / 256, bias=EPS)
            S.act(ss[:], ss[:], AF.Exp, [ss], [ss], scale=-0.5)
            S.stt(y3s[ci % 2][:], y2[:], ss[:, 0:1], gnm[:], ALU.mult, ALU.mult, [y2, ss, gnm], [y3s[ci % 2]])

        def ssd_tail(ci):
            c0, c1 = ci * 128, (ci + 1) * 128
            y3_ = y3s[ci % 2]
            for k in range(2):
                S.tr(ptb[:, 512 + k * 128:512 + (k + 1) * 128], y3_[:, k * 128:(k + 1) * 128], P.ident_b[:], [y3_, P.ident_b], [ptb])
            S.cp('act', mixs[:, :, c0:c1], ptb[:, 512:768].rearrange("p (k t) -> p k t", k=2), [ptb], [mixs])

        c_lo = 2 if l == 1 else 0
        ssd_rb(c_lo)
        ssd_rb(c_lo + 1)
        ssd_front(c_lo)
        ssd_rb(c_lo + 2)
        ssd_front(c_lo + 1)
        ssd_mm(c_lo)
        for ci in range(c_lo, NCH):
            if ci + 3 < NCH:
                ssd_rb(ci + 3)
            if ci + 2 < NCH:
                ssd_front(ci + 2)
            if ci + 1 < NCH:
                ssd_mm(ci + 1)
            ssd_epi(ci)
            if ci >= c_lo + 1:
                ssd_tail(ci - 1)
        ssd_tail(NCH - 1)
        store_mix(S, P, mixs, 0, b, skip_ctx=(l == 1))


def stage_ret(P, l):
    K = P.K
    S = K.stage(f"ret{l}")
    scale = 64 ** -0.5
    ule = S.sb('ule', [128, 128], F32)
    uge = S.sb('uge', [128, 128], F32)
    idf = S.sb('idf', [128, 128], F32)
    ramp = S.sb('ramp', [128, 128], F32)
    pidx = S.sb('pidx', [128, 1], F32)
    S.dma(ule[:], P.k_ule[:], [P.k_ule], [ule])
    S.dma(uge[:], P.k_uge[:], [P.k_uge], [uge])
    S.dma(idf[:], P.k_idiff[:], [P.k_idiff], [idf])
    S.dma(ramp[:], P.k_ramp[:], [P.k_ramp], [ramp])
    S.dma(pidx[:], P.k_pidx[:], [P.k_pidx], [pidx])
    rcos = S.sb('rcos', [64, LL], F32)
    rsin = S.sb('rsin', [64, LL], F32)
    rp = S.sb('rp', [64, 64], BF16)
    S.dma(rcos[:], P.k_rope_cos[0:64, :], [P.k_rope_cos], [rcos])
    S.dma(rsin[:], P.k_rope_sin[0:64, :], [P.k_rope_sin], [rsin])
    S.dma(rp[:], P.k_rope_p[0:64, 0:64], [P.k_rope_p], [rp])
    lg = S.sb('lg', [128, 8], F32)
    S.dma(lg[:], AP(P.ret_decay8, l * 8, [[0, 128], [1, 8]]), [P.ret_decay8], [lg])
    S.act(lg[:], lg[:], AF.Exp, [lg], [lg])
    S.ts('dve', lg[:], lg[:], -1.0, None, ALU.mult, None, [lg], [lg])
    Dm = S.sb('Dm', [128, 8, 128], F32)
    Ec = S.sb('Ec', [64, 8, 128], F32)
    wc = S.sb('wc', [128, 8], F32)
    et = S.sb('et', [64, 8], F32)
    tmpd = S.sb('tmpd', [128, 128], F32)
    tmpc = S.sb('tmpc', [128, 1], F32)
    for r in range(2):
        for h in range(4):
            c8 = 4 * r + h
            sgn = 1.0 if r == 0 else -1.0
            S.ts('dve', tmpd[:], idf[:], lg[:, c8:c8 + 1], sgn, ALU.mult, ALU.mult, [idf, lg], [tmpd])
            S.ts('dve', tmpd[:], tmpd[:], 0.0, None, ALU.min, None, [tmpd], [tmpd])
            S.act(tmpd[:], tmpd[:], AF.Exp, [tmpd], [tmpd])
            S.stt(Dm[:, c8, :], tmpd[:], scale, (ule if r == 0 else uge)[:], ALU.mult, ALU.mult, [tmpd, ule, uge], [Dm])
            if r == 0:
                S.ts('dve', tmpd[0:64, :], ramp[0:64, :], lg[0:64, c8:c8 + 1], None, ALU.mult, None, [ramp, lg], [tmpd])
            else:
                S.ts('dve', tmpd[0:64, :], ramp[0:64, :], -1.0, 129.0, ALU.mult, ALU.add, [ramp], [tmpd])
                S.ts('dve', tmpd[0:64, :], tmpd[0:64, :], lg[0:64, c8:c8 + 1], None, ALU.mult, None, [tmpd, lg], [tmpd])
            S.act(Ec[:, c8, :], tmpd[0:64, :], AF.Exp, [tmpd], [Ec])
            if r == 0:
                S.ts('dve', tmpc[:], pidx[:], -1.0, 127.0, ALU.mult, ALU.add, [pidx], [tmpc])
                S.tt('dve', tmpc[:], tmpc[:], lg[:, c8:c8 + 1], ALU.mult, [tmpc, lg], [tmpc])
            else:
                S.tt('dve', tmpc[:], pidx[:], lg[:, c8:c8 + 1], ALU.mult, [pidx, lg], [tmpc])
            S.act(wc[:, c8:c8 + 1], tmpc[:], AF.Exp, [tmpc], [wc])
    S.ts('dve', wc[:], wc[:], scale, None, ALU.mult, None, [wc], [wc])
    S.ts('dve', et[:], lg[0:64, :], 128.0, None, ALU.mult, None, [lg], [et])
    S.act(et[:], et[:], AF.Exp, [et], [et])
    qh = S.sb('qh', [64, 4, LB], BF16)
    kh = S.sb('kh', [64, 4, LB], BF16)
    qa = qh
    ka = kh
    vr = S.sb('vr', [128, 2, LB], BF16)
    gr = S.sb('gr', [128, 2, LB], BF16)
    vtok = S.sb('vtok', [128, NCH, 256], BF16)
    ktok = S.sb('ktok', [128, NCH, 256], BF16)
    gs = S.sb('gs', [128, NCH, 256], BF16)
    sball = S.sb('sball', [64, NCH, 256], F32)
    hinf = S.sb('hinf', [64, NCH, 256], BF16)
    hinb = S.sb('hinb', [64, NCH, 256], BF16)
    hst = S.sb('hst', [64, 256], F32)
    xw = S.sb('xw', [128, 4, 2, 64], BF16)
    mixs = S.sb('mixs', [128, 2, LB], BF16)
    rt1 = S.sb('rt1', [64, 512], F32)
    rt2 = S.sb('rt2', [64, 512], F32)
    rt1b = S.sb('rt1b', [64, 512], F32)
    rt2b = S.sb('rt2b', [64, 512], F32)
    ST = [S.sb(f'ST{i}', [128, 4, 128], BF16) for i in range(4)]
    CsT = [S.sb(f'CsT{i}', [64, 4, 128], BF16) for i in range(4)]
    ysb = S.sb('ysb', [128, 4, 64], F32)
    yc = S.sb('yc', [128, 4, 64], F32)
    ysq = S.sb('ysq', [128, 4, 64], F32)
    s1 = S.sb('s1', [128, 4], F32)
    s2 = S.sb('s2', [128, 4], F32)
    s3 = S.sb('s3', [128, 4], F32)
    y3 = S.sb('y3', [128, 256], BF16)
    ptb = S.ps('ptb', [128, 1024], BF16)
    ptb2 = S.ps('ptb2', [128, 1024], BF16)
    prp = S.ps('prp')
    pg = S.ps('pg')
    pst = S.ps('pst')
    py = S.ps('py')

    for b in range(NB):
        for h in range(4):
            load_seq(S, lambda a, e, h=h: qh[:, h, a:e], P, OFF_RET + 64 * h, 64, b, qh.s(h))
            load_seq(S, lambda a, e, h=h: kh[:, h, a:e], P, OFF_RET + 256 + 64 * h, 64, b, kh.s(h))
        for k in range(2):
            load_seq(S, lambda a, e, k=k: vr[:, k, a:e], P, OFF_RET + 512 + 128 * k, 128, b, vr.s(k))
            load_seq(S, lambda a, e, k=k: gr[:, k, a:e], P, OFF_RET + 768 + 128 * k, 128, b, gr.s(k))
        for (src, dst) in ((qh, qa), (kh, ka)):
            for h in range(4):
                for tt_ in range(4):
                    a, e = LC + tt_ * 512, LC + (tt_ + 1) * 512
                    prp_ = (prp, pg, pst, py)[tt_]
                    r1_ = (rt1, rt1b)[tt_ % 2]
                    r2_ = (rt2, rt2b)[tt_ % 2]
                    S.mm(prp_[0:64, :], rp[:], src[:, h, a:e], True, True, [rp, src.s(h)], [prp_])
                    S.tt('dve', r1_[:], src[:, h, a:e], rcos[:, tt_ * 512:(tt_ + 1) * 512], ALU.mult, [src.s(h), rcos], [r1_])
                    S.tt('dve', r2_[:], prp_[0:64, :], rsin[:, tt_ * 512:(tt_ + 1) * 512], ALU.mult, [prp_, rsin], [r2_])
                    S.tt('pool', dst[:, h, a:e], r1_[:], r2_[:], ALU.add, [r1_, r2_], [dst.s(h)])
        for k in range(2):
            S.act(gr[:, k, :], gr[:, k, :], AF.Silu, [gr.s(k)], [gr.s(k)])
        S.ms('dve', hst[:], 0.0, [hst])
        xws = [xw, S.sb(f'xwb{b}', [128, 4, 2, 64], BF16)]

        def ret_a1(ci):
            xw = xws[ci % 2]
            c0, c1 = ci * 128, (ci + 1) * 128
            for k in range(2):
                S.tr(ptb[:, k * 128:(k + 1) * 128], vr[:, k, c0:c1], P.ident_b[:], [vr.s(k), P.ident_b], [ptb])
            for h in range(4):
                S.tr(ptb[:, 256 + h * 64:256 + (h + 1) * 64], ka[0:64, h, c0:c1], P.ident_b[0:64, 0:64], [ka.s(h), P.ident_b], [ptb])
            S.cp('act', vtok[:, ci, :], ptb[:, 0:256], [ptb], [vtok.s(ci)])
            S.cp('dve', ktok[:, ci, :], ptb[:, 256:512], [ptb], [ktok.s(ci)])
            for k in range(2):
                S.tr(ptb2[:, k * 128:(k + 1) * 128], gr[:, k, c0:c1], P.ident_b[:], [gr.s(k), P.ident_b], [ptb2])
            S.cp('act', gs[:, ci, :], ptb2[:, 0:256], [ptb2], [gs.s(ci)])
            S.tt('dve', xw[:], AP(vtok, ci * 256, [[NCH * 256, 128], [64, 4], [0, 2], [1, 64]]),
                 AP(wc, 0, [[8, 128], [1, 4], [4, 2], [0, 64]]), ALU.mult, [vtok.s(ci), wc], [xw])

        def ret_a2(ci):
            xw = xws[ci % 2]
            for h in range(4):
                S.mm(pst[0:64, h * 128:(h + 1) * 128], ktok[:, ci, h * 64:(h + 1) * 64], xw[:, h, :, :].rearrange("p r q -> p (r q)"), True, True, [ktok.s(ci), xw], [pst])
            S.cp('act', hinf[:, ci, :], hst[:], [hst], [hinf.s(ci)])
            S.tt('dve', hst[:].rearrange("n (h p) -> n h p", h=4), hst[:].rearrange("n (h p) -> n h p", h=4),
                 AP(et, 0, [[8, 64], [1, 4], [0, 64]]), ALU.mult, [hst, et], [hst])
            S.tt('dve', hst[:].rearrange("n (h p) -> n h p", h=4), AP(pst, 0, [[512, 64], [128, 4], [1, 64]]),
                 hst[:].rearrange("n (h p) -> n h p", h=4), ALU.add, [hst, pst], [hst])
            S.cp('act', sball[:, ci, :].rearrange("n (h p) -> n h p", h=4), AP(pst, 64, [[512, 64], [128, 4], [1, 64]]), [pst], [sball.s(ci)])

        ret_a1(0)
        for ci in FWD_CHAIN:
            if ci + 1 < NCH:
                ret_a1(ci + 1)
            ret_a2(ci)
        S.ms('dve', hst[:], 0.0, [hst])
        for ci in BWD_CHAIN:
            S.cp('act', hinb[:, ci, :], hst[:], [hst], [hinb.s(ci)])
            S.tt('dve', hst[:].rearrange("n (h p) -> n h p", h=4), hst[:].rearrange("n (h p) -> n h p", h=4),
                 AP(et, 4, [[8, 64], [1, 4], [0, 64]]), ALU.mult, [hst, et], [hst])
            S.tt('dve', hst[:], hst[:], sball[:, ci, :], ALU.add, [hst, sball.s(ci)], [hst])
        pgs = [pg, prp]
        pys = [py, pst]

        def ret_front(ci):
            c0, c1 = ci * 128, (ci + 1) * 128
            pg_ = pgs[ci % 2]
            for h in range(4):
                S.mm(pg_[:, h * 128:(h + 1) * 128], ka[0:64, h, c0:c1], qa[0:64, h, c0:c1], True, True, [ka.s(h), qa.s(h)], [pg_])
            for r in range(2):
                st = ST[2 * (ci % 2) + r]
                cs = CsT[2 * (ci % 2) + r]
                S.tt('dve', st[:], pg_[:].rearrange("p (h i) -> p h i", h=4), Dm[:, 4 * r:4 * r + 4, :], ALU.mult, [pg_, Dm], [st])
                S.tt('pool', cs[:], AP(qa, c0, [[4 * LB, 64], [LB, 4], [1, 128]]), Ec[:, 4 * r:4 * r + 4, :], ALU.mult, [qa.s(0), qa.s(1), qa.s(2), qa.s(3), Ec], [cs])

        y3s = [y3, S.sb(f'y3b{b}', [128, 256], BF16)]

        def ret_mm(ci):
            py_ = pys[ci % 2]
            sts = [ST[2 * (ci % 2)], ST[2 * (ci % 2) + 1]]
            css = [CsT[2 * (ci % 2)], CsT[2 * (ci % 2) + 1]]
            for h in range(4):
                hs_ = slice(h * 64, (h + 1) * 64)
                S.mm(py_[:, hs_], sts[0][:, h, :], vtok[:, ci, hs_], True, False, [sts[0], vtok.s(ci)], [py_])
                S.mm(py_[:, hs_], css[0][:, h, :], hinf[:, ci, hs_], False, False, [css[0], hinf.s(ci)], [py_])
                S.mm(py_[:, hs_], sts[1][:, h, :], vtok[:, ci, hs_], False, False, [sts[1], vtok.s(ci)], [py_])
                S.mm(py_[:, hs_], css[1][:, h, :], hinb[:, ci, hs_], False, True, [css[1], hinb.s(ci)], [py_])

        def ret_epi(ci):
            py_ = pys[ci % 2]
            S.cp('act', ysb[:], py_[:, 0:256].rearrange("p (h q) -> p h q", h=4), [py_], [ysb])
            S.tt('pool', ysq[:], ysb[:], ysb[:], ALU.mult, [ysb], [ysq])
            S.op('dve', lambda e: e.tensor_reduce(out=s1[:], in_=ysb[:], axis=AX.X, op=ALU.add), [ysb], [s1])
            S.op('dve', lambda e: e.tensor_reduce(out=s2[:], in_=ysq[:], axis=AX.X, op=ALU.add), [ysq], [s2])
            S.ts('dve', s1[:], s1[:], 1.0 / 64, None, ALU.mult, None, [s1], [s1])
            S.tt('dve', s3[:], s1[:], s1[:], ALU.mult, [s1], [s3])
            S.stt(s2[:], s2[:], 1.0 / 64, s3[:], ALU.mult, ALU.subtract, [s2, s3], [s2])
            S.act(s2[:], s2[:], AF.Ln, [s2], [s2], scale=1.0, bias=EPS)
            S.act(s2[:], s2[:], AF.Exp, [s2], [s2], scale=-0.5)
            S.tt('dve', yc[:], ysb[:], AP(s1, 0, [[4, 128], [1, 4], [0, 64]]), ALU.subtract, [ysb, s1], [yc])
            S.tt('pool', yc[:], yc[:], AP(s2, 0, [[4, 128], [1, 4], [0, 64]]), ALU.mult, [yc, s2], [yc])
            S.tt('dve', y3s[ci % 2][:], yc[:].rearrange("p h q -> p (h q)"), gs[:, ci, :], ALU.mult, [yc, gs.s(ci)], [y3s[ci % 2]])

        def ret_tail(ci):
            c0, c1 = ci * 128, (ci + 1) * 128
            y3_ = y3s[ci % 2]
            for k in range(2):
                S.tr(ptb[:, 512 + k * 128:512 + (k + 1) * 128], y3_[:, k * 128:(k + 1) * 128], P.ident_b[:], [y3_, P.ident_b], [ptb])
            S.cp('act', mixs[:, :, c0:c1], ptb[:, 512:768].rearrange("p (k t) -> p k t", k=2), [ptb], [mixs])

        c_lo = 2 if l == 1 else 0
        ret_front(c_lo)
        ret_front(c_lo + 1)
        ret_mm(c_lo)
        for ci in range(c_lo, NCH):
            if ci + 2 < NCH:
                ret_front(ci + 2)
            if ci + 1 < NCH:
                ret_mm(ci + 1)
            ret_epi(ci)
            if ci >= c_lo + 1:
                ret_tail(ci - 1)
        ret_tail(NCH - 1)
        store_mix(S, P, mixs, 512, b, skip_ctx=(l == 1))
    S.emit()

PI = math.pi


def hy_seq(tag):
    L = LL if tag == 'L' else LC
    off = lat_off if tag == 'L' else ctx_off
    return L, L // 128, off


def hyf_body(S, P, l, tag):
    K = P.K
    L, nt, _ = hy_seq(tag)
    pk = [S.ps(f'pk{i}') for i in range(2)]
    feat = S.sb('feat', [33, L], F32)
    w1 = S.sb('w1', [33, 64], F32)
    w2 = S.sb('w2', [64, 64], F32)
    w3 = S.sb('w3', [64, 1024], F32)
    b1 = S.sb('b1', [64, 1], F32)
    b2 = S.sb('b2', [64, 1], F32)
    fq = S.sb('fq', [64, 1], F32)
    fb1 = S.sb('fb1', [64, 1], F32)
    fb2 = S.sb('fb2', [64, 1], F32)
    S.dma(feat[:], getattr(P, f'k_feat_{tag}')[:], [], [feat])
    S.dma(w1[:], P.hy_w1[l], [], [w1])
    S.dma(w2[:], P.hy_w2[l], [], [w2])
    S.dma(w3[:], P.hy_w3[l], [], [w3])
    S.dma(b1[:], P.hy_b1c[l], [], [b1])
    S.dma(b2[:], P.hy_b2c[l], [], [b2])
    S.dma(fq[:], P.hy_freqc[l], [], [fq])
    S.tt('dve', fb1[:], b1[:], fq[:], ALU.mult, [b1, fq], [fb1])
    S.tt('dve', fb2[:], b2[:], fq[:], ALU.mult, [b2, fq], [fb2])
    h1 = S.sb('h1', [64, L], F32)
    h2 = S.sb('h2', [64, L], F32)
    arg = S.sb('arg', [64, 512], F32)
    msk = S.sb('msk', [64, 512], F32)
    ph = pk[0]
    W = min(512, L)
    for (wm, src, fb, dst, kk) in ((w1, feat, fb1, h1, 33), (w2, h1, fb2, h2, 64)):
        for t0 in range(0, L, W):
            S.mm(ph[0:64, 0:W], wm[0:kk, :], src[0:kk, t0:t0 + W], True, True, [wm, src], [ph])
            S.ts('dve', arg[:, 0:W], ph[0:64, 0:W], fq[:, 0:1], fb[:, 0:1], ALU.mult, ALU.add, [ph, fq, fb], [arg])
            S.ts('dve', msk[:, 0:W], arg[:, 0:W], 1e30, -PI * 1e30, ALU.mult, ALU.add, [arg], [msk])
            S.ts('dve', msk[:, 0:W], msk[:, 0:W], 0.0, 1.0, ALU.max, ALU.min, [msk], [msk])
            S.stt(arg[:, 0:W], msk[:, 0:W], -2 * PI, arg[:, 0:W], ALU.mult, ALU.add, [msk, arg], [arg])
            S.ts('dve', msk[:, 0:W], arg[:, 0:W], -1e30, -PI * 1e30, ALU.mult, ALU.add, [arg], [msk])
            S.ts('dve', msk[:, 0:W], msk[:, 0:W], 0.0, 1.0, ALU.max, ALU.min, [msk], [msk])
            S.stt(arg[:, 0:W], msk[:, 0:W], 2 * PI, arg[:, 0:W], ALU.mult, ALU.add, [msk, arg], [arg])
            S.act(dst[:, t0:t0 + W], arg[:, 0:W], AF.Sin, [arg], [dst])
            yield
    dec = [S.sb(f'dec{i}', [128, 1024], F32) for i in range(2)]
    fsb = S.sb('fsb', [128, 1024], F32)
    absf = S.sb('absf', [128, 1024], F32)
    hs = S.sb('hs', [128, nt, 512], BF16)
    hd = S.sb('hd', [128, nt, 512], BF16)
    pf = [S.ps(f'pf{i}') for i in range(2)]
    pn = [S.ps(f'pn{i}') for i in range(2)]
    dect = getattr(P, f'k_dec_{tag}')
    for mc in range(nt):
        d_ = dec[mc % 2]
        S.dma(d_[:], dect[mc * 128:(mc + 1) * 128, :], [], [d_])
        for hf in range(2):
            S.mm(pf[hf][:], h2[:, mc * 128:(mc + 1) * 128], w3[:, hf * 512:(hf + 1) * 512], True, True, [h2, w3], [pf[hf]])
            S.tt('dve', fsb[:, hf * 512:(hf + 1) * 512], pf[hf][:], d_[:, hf * 512:(hf + 1) * 512], ALU.mult, [pf[hf], d_], [fsb])
        if mc == 0:
            S.ms('dve', fsb[0:1, 512:1024], 0.0, [fsb])
        S.act(absf[:], fsb[:], AF.Abs, [fsb], [absf])
        for hf in range(2):
            S.mm(pn[hf][:], P.ones_f[:], absf[:, hf * 512:(hf + 1) * 512], mc == 0, mc == nt - 1, [P.ones_f, absf], [pn[hf]])
        S.tt('dve', hs[:, mc, :], fsb[:, 0:512], fsb[:, 512:1024], ALU.add, [fsb], [hs.s(mc)])
        S.tt('pool', hd[:, mc, :], fsb[:, 0:512], fsb[:, 512:1024], ALU.subtract, [fsb], [hd.s(mc)])
        yield
    inv = S.sb('inv', [128, 512], F32)
    S.cp('dve', inv[:], pn[0][:], [pn[0]], [inv])
    S.tt('dve', inv[:], pn[1][:], inv[:], ALU.add, [inv, pn[1]], [inv])
    S.ts('dve', inv[:], inv[:], EPS, None, ALU.add, None, [inv], [inv])
    S.op('dve', lambda e: e.reciprocal(out=inv[:], in_=inv[:]), [inv], [inv])
    bias = S.sb('bias', [128, 512], F32)
    S.dma(bias[:], AP(P.hy_bias, l * 512, [[0, 128], [1, 512]]), [], [bias])
    wf = S.sb('wf', [128, nt], F32)
    S.dma(wf[:], getattr(P, f'k_wf_{tag}')[:], [], [wf])
    alt = S.sb('alt', [128, nt], BF16)
    S.dma(alt[:], getattr(P, f'k_alt_{tag}')[:], [], [alt])
    hsr = [hs.s(c) for c in range(nt)]
    hdr = [hd.s(c) for c in range(nt)]
    pq = pf[0]
    for c in range(nt):
        S.mm(pq[0:1, :], alt[:, c:c + 1], hs[:, c, :], c == 0, c == nt - 1, [alt, hsr[c]], [pq])
    knq = S.sb('knq', [1, 512], F32)
    S.tt('dve', knq[:], pq[0:1, :], inv[0:1, :], ALU.mult, [pq, inv], [knq])
    S.tt('dve', knq[:], knq[:], bias[0:1, :], ALU.add, [knq, bias], [knq])
    S.ts('dve', knq[:], knq[:], wf[0:1, 0:1], None, ALU.mult, None, [knq, wf], [knq])
    cb = [S.sb(f'cb{i}', [128, nt, 128], BF16) for i in range(3)]
    sbk = [S.sb(f'sbk{i}', [128, nt, 128], BF16) for i in range(3)]
    kr = [S.sb(f'kr{i}', [128, 512], BF16) for i in range(2)]
    ki = [S.sb(f'ki{i}', [128, 512], BF16) for i in range(2)]
    tk = S.sb('tk', [128, 512], F32)
    dc = getattr(P, f'k_dftc_{tag}')
    dsf = getattr(P, f'k_dftsf_{tag}')
    hyK = P.hyK[tag]
    for ft in range(nt):
        c_ = cb[ft % 3]
        s_ = sbk[ft % 3]
        S.dma(c_[:], dc[ft], [], [c_])
        S.dma(s_[:], dsf[ft], [], [s_])
        for c in range(nt):
            S.mm(pk[0][:], c_[:, c, :], hs[:, c, :], c == 0, c == nt - 1, [c_, hsr[c]], [pk[0]])
        for c in range(nt):
            S.mm(pk[1][:], s_[:, c, :], hd[:, c, :], c == 0, c == nt - 1, [s_, hdr[c]], [pk[1]])
        kr_ = kr[ft % 2]
        ki_ = ki[ft % 2]
        S.tt('dve', tk[:], pk[0][:], inv[:], ALU.mult, [pk[0], inv], [tk])
        S.tt('pool', tk[:], tk[:], bias[:], ALU.add, [tk, bias], [tk])
        S.ts('dve', kr_[:], tk[:], wf[:, ft:ft + 1], None, ALU.mult, None, [tk, wf], [kr_])
        S.stt(ki_[:], pk[1][:], wf[:, ft:ft + 1], inv[:], ALU.mult, ALU.mult, [pk[1], wf, inv], [ki_])
        if ft == 0:
            S.cp('dve', ki_[0:1, :], knq[:], [knq, ki_], [ki_])
        S.dma(hyK[0, ft * 128:(ft + 1) * 128, :], kr_[:], [kr_], [hyK.s((0, ft))])
        S.dma(hyK[1, ft * 128:(ft + 1) * 128, :], ki_[:], [ki_], [hyK.s((1, ft))])
        yield


def hyp_body(S, P, l, tag):
    K = P.K
    L, nt, off = hy_seq(tag)
    cw = S.sb('cw', [128, 6, 4], F32)
    S.dma(cw[:], P.hy_cw[l], [], [cw])
    raw = S.sb('raw', [128, 6, L], BF16)
    acc = [S.sb(f'acc{i}', [128, L], F32) for i in range(2)]
    cv = S.sb('cv', [128, 6, L], BF16)
    ptb = [S.ps(f'ptb{i}', [128, 1024], BF16) for i in range(2)]
    stg = [S.sb(f'stg{i}', [128, 3, 256], BF16) for i in range(2)]
    hyX = P.hyX[tag]
    n = 0
    for b in range(NB):
        for k in range(6):
            S.dma(raw[:, k, :], P.uT[OFF_HY + k * 128:OFF_HY + (k + 1) * 128, off(b):off(b) + L], [], [raw.s(k)])
        for k in range(6):
            a_ = acc[k % 2]
            S.ts('dve', a_[:], raw[:, k, :], cw[:, k, 1:2], cw[:, k, 3:4], ALU.mult, ALU.add, [raw.s(k), cw], [a_])
            S.stt(a_[:, 1:L], raw[:, k, 0:L - 1], cw[:, k, 0:1], a_[:, 1:L], ALU.mult, ALU.add, [raw.s(k), cw, a_], [a_])
            S.stt(a_[:, 0:L - 1], raw[:, k, 1:L], cw[:, k, 2:3], a_[:, 0:L - 1], ALU.mult, ALU.add, [raw.s(k), cw, a_], [a_])
            S.cp('act', cv[:, k, :], a_[:], [a_], [cv.s(k)])
            yield
        for tc in range(nt):
            p_ = ptb[n % 2]
            s_ = stg[n % 2]
            n += 1
            for k in range(6):
                S.tr(p_[:, k * 128:(k + 1) * 128], cv[:, k, tc * 128:(tc + 1) * 128], P.ident_b[:], [cv.s(k), P.ident_b], [p_])
            S.cp('act' if n % 2 else 'dve', s_[:].rearrange("p j c -> p (j c)"), p_[:, 0:768], [p_], [s_])
            S.dma(hyX[:, tc * 128:(tc + 1) * 128, b * 256:(b + 1) * 256].rearrange("j p c -> p j c"), s_[:], [s_], [hyX.s((tc, b))])
            yield


def stage_hy_conv(P, l, tag, side=None):
    S = P.K.stage(f"hyc{l}{tag}")
    gens = [hyc_body(S, P, l, tag)]
    if side is not None:
        gens.append(side(S))
    live = list(gens)
    while live:
        for g in list(live):
            try:
                next(g)
            except StopIteration:
                live.remove(g)
    S.emit()


def hyc_body(S, P, l, tag):
    K = P.K
    L, nt, off = hy_seq(tag)
    hyX = P.hyX[tag]
    hyK = P.hyK[tag]
    Z = [S.sb(f'Z{i}', [128, nt, 512], BF16) for i in range(2)]
    for tc in range(nt):
        S.dma(Z[0][:, tc, :], hyX[2, tc * 128:(tc + 1) * 128, :], [], [Z[0].s(tc)])
    Yr = S.sb('Yr', [128, nt, 512], BF16)
    Yi = S.sb('Yi', [128, nt, 512], BF16)
    cb = [S.sb(f'cb{i}', [128, nt, 128], BF16) for i in range(3)]
    sbk = [S.sb(f'sbk{i}', [128, nt, 128], BF16) for i in range(3)]
    kr = [S.sb(f'kr{i}', [128, 256], BF16) for i in range(2)]
    ki = [S.sb(f'ki{i}', [128, 256], BF16) for i in range(2)]
    gt = [S.sb(f'gt{i}', [128, 512], BF16) for i in range(2)]
    t1 = S.sb('t1', [128, 512], F32)
    t2 = S.sb('t2', [128, 512], F32)
    t3 = S.sb('t3', [128, 512], F32)
    t4 = S.sb('t4', [128, 512], F32)
    zo = S.sb('zo', [128, 512], BF16)
    mixh = S.sb('mixh', [128, 2, NB, L], BF16)
    px = [S.ps(f'px{i}') for i in range(4)]
    py = [S.ps(f'py{i}') for i in range(2)]
    ptb = S.ps('ptb', [128, 1024], BF16)
    dc = getattr(P, f'k_dftc_{tag}')
    dsf = getattr(P, f'k_dftsf_{tag}')
    dsi = getattr(P, f'k_dftsi_{tag}')
    n = 0
    for o in range(2):
        zin = Z[o]
        zr_ = [zin.s(c) for c in range(nt)]
        for ft in range(nt):
            c_ = cb[n % 3]
            s_ = sbk[n % 3]
            kr_ = kr[n % 2]
            ki_ = ki[n % 2]
            pr = px[(2 * n) % 4]
            pi_ = px[(2 * n + 1) % 4]
            n += 1
            S.dma(c_[:], dc[ft], [], [c_])
            S.dma(s_[:], dsf[ft], [], [s_])
            S.dma(kr_[:], hyK[0, ft * 128:(ft + 1) * 128, o * 256:(o + 1) * 256], [], [kr_])
            S.dma(ki_[:], hyK[1, ft * 128:(ft + 1) * 128, o * 256:(o + 1) * 256], [], [ki_])
            for c in range(nt):
                S.mm(pr[:], c_[:, c, :], zin[:, c, :], c == 0, c == nt - 1, [c_, zr_[c]], [pr])
            for c in range(nt):
                S.mm(pi_[:], s_[:, c, :], zin[:, c, :], c == 0, c == nt - 1, [s_, zr_[c]], [pi_])
            krb = AP(kr_, 0, [[256, 128], [0, 2], [1, 256]])
            kib = AP(ki_, 0, [[256, 128], [0, 2], [1, 256]])
            v3 = lambda t: t[:].rearrange("p (b c) -> p b c", b=2)
            S.tt('dve', v3(t1), v3(pr), krb, ALU.mult, [pr, kr_], [t1])
            S.tt('dve', v3(t2), v3(pi_), kib, ALU.mult, [pi_, ki_], [t2])
            S.tt('dve', v3(t3), v3(pr), kib, ALU.mult, [pr, ki_], [t3])
            S.tt('dve', v3(t4), v3(pi_), krb, ALU.mult, [pi_, kr_], [t4])
            S.tt('pool', Yr[:, ft, :], t1[:], t2[:], ALU.subtract, [t1, t2], [Yr.s(ft)])
            S.tt('pool', Yi[:, ft, :], t3[:], t4[:], ALU.add, [t3, t4], [Yi.s(ft)])
            if ft == 0:
                S.tt('dve', Yr[0:1, 0, :].rearrange("p (b c) -> p b c", b=2), pr[0:1, :].rearrange("p (b c) -> p b c", b=2),
                     AP(kr_, 0, [[256, 1], [0, 2], [1, 256]]), ALU.mult, [pr, kr_, Yr.s(0)], [Yr.s(0)])
                S.tt('dve', Yi[0:1, 0, :].rearrange("p (b c) -> p b c", b=2), pi_[0:1, :].rearrange("p (b c) -> p b c", b=2),
                     AP(ki_, 0, [[256, 1], [0, 2], [1, 256]]), ALU.mult, [pi_, ki_, Yi.s(0)], [Yi.s(0)])
            yield
        yrr = [Yr.s(c) for c in range(nt)]
        yir = [Yi.s(c) for c in range(nt)]
        for tt_ in range(nt):
            c_ = cb[n % 3]
            s_ = sbk[n % 3]
            g_ = gt[n % 2]
            p_ = py[n % 2]
            n += 1
            S.dma(c_[:], dc[tt_], [], [c_])
            S.dma(s_[:], dsi[tt_], [], [s_])
            S.dma(g_[:], hyX[o, tt_ * 128:(tt_ + 1) * 128, :], [], [g_])
            for c in range(nt):
                S.mm(p_[:], c_[:, c, :], Yr[:, c, :], c == 0, False, [c_, yrr[c]], [p_])
            for c in range(nt):
                S.mm(p_[:], s_[:, c, :], Yi[:, c, :], False, c == nt - 1, [s_, yir[c]], [p_])
            if o == 0:
                S.tt('dve', Z[1][:, tt_, :], p_[:], g_[:], ALU.mult, [p_, g_], [Z[1].s(tt_)])
            else:
                S.tt('dve', zo[:], p_[:], g_[:], ALU.mult, [p_, g_], [zo])
                for b in range(NB):
                    for k in range(2):
                        j = b * 2 + k
                        S.tr(ptb[:, j * 128:(j + 1) * 128], zo[:, b * 256 + k * 128:b * 256 + (k + 1) * 128], P.ident_b[:], [zo, P.ident_b], [ptb])
                S.cp('act', mixh[:, :, :, tt_ * 128:(tt_ + 1) * 128].rearrange("p k b t -> p b k t"),
                     ptb[:, 0:512].rearrange("p (b k t) -> p b k t", b=2, k=2), [ptb], [mixh])
            yield
    for b in range(NB):
        for k in range(2):
            S.dma(P.mixT[256 + k * 128:256 + (k + 1) * 128, off(b):off(b) + L], mixh[:, k, b, :], [mixh], [P.mixT.s((256, k, b, tag))])
    yield


def stage_hy_fp(P, l, tag):
    S = P.K.stage(f"hyfp{l}{tag}")
    gens = [hyp_body(S, P, l, tag), hyf_body(S, P, l, tag)]
    live = list(gens)
    while live:
        for g in list(live):
            try:
                next(g)
            except StopIteration:
                live.remove(g)
    S.emit()

TCW = 288
W8 = 8


def stage_s5(P, l):
    K = P.K
    S = K.stage(f"s5{l}")
    NK = 8
    bre = S.sb('bre', [128, NK, 16], F32)
    bim = S.sb('bim', [128, NK, 16], F32)
    S.dma(bre[:], P.s5_bre[l], [], [bre])
    S.dma(bim[:], P.s5_bim[l], [], [bim])
    Rr = S.sb('Rr', [128, NK, TCW], F32)
    Ri = S.sb('Ri', [128, NK, TCW], F32)
    rho8 = S.sb('rho8', [128, NK], F32)
    cth = S.sb('cth', [128, NK], F32)
    sth = S.sb('sth', [128, NK], F32)
    BdT = [[[S.sb(f'BdT{m}{k}{c}', [128, 128], BF16) for c in range(2)] for k in range(NK)] for m in range(W8)]
    CdA = [[[S.sb(f'CdA{m}{k}{c}', [128, 128], BF16) for c in range(2)] for k in range(NK)] for m in range(W8)]
    KernT = [[S.sb(f'KT{m}{kc}', [128, 128], BF16) for kc in range(2)] for m in range(W8)]
    CdF = [[S.sb(f'CdF{k}{c}', [128, 128], F32) for c in range(2)] for k in range(NK)]
    bdz = [[S.sb(f'bdz{q}{c}', [128, 128], F32) for c in range(2)] for q in range(4)]
    are = S.sb('are', [128, NK], F32)
    aim = S.sb('aim', [128, NK], F32)
    dt = S.sb('dt', [128, NK], F32)
    t_a = S.sb('t_a', [128, NK], F32)
    t_b = S.sb('t_b', [128, NK], F32)
    t_c = S.sb('t_c', [128, NK], F32)
    mag = S.sb('mag', [128, NK], F32)
    cc = S.sb('cc', [128, NK], F32)
    sn = S.sb('sn', [128, NK], F32)
    zr = S.sb('zr', [128, NK], F32)
    zi = S.sb('zi', [128, NK], F32)
    ar_ = S.sb('ar_', [128, NK], F32)
    ai_ = S.sb('ai_', [128, NK], F32)
    pwr = S.sb('pwr', [128, W8 + 1, NK], F32)
    pwi = S.sb('pwi', [128, W8 + 1, NK], F32)
    bbr = S.sb('bbr', [128, NK, 16], F32)
    bbi = S.sb('bbi', [128, NK, 16], F32)
    xr_ = S.sb('xr_', [128, NK, 16], F32)
    xi_ = S.sb('xi_', [128, NK, 16], F32)
    tb1 = S.sb('tb1', [128, NK, 16], F32)
    tb2 = S.sb('tb2', [128, NK, 16], F32)
    cre = S.sb('cre', [128, NK, 16], F32)
    cim = S.sb('cim', [128, NK, 16], F32)
    wr = S.sb('wr', [128, NK], F32)
    wi = S.sb('wi', [128, NK], F32)
    ptp = S.ps('ptp')
    pkn = S.ps('pkn')
    hpi = math.pi / 2
    NWB = LB // W8
    u8 = S.sb('u8', [128, 2, W8, NWB], BF16)
    ysA = S.sb('ysA', [128, 2, LB], F32)
    Gr = S.sb('Gr', [128, NK, TCW], F32)
    Gi = S.sb('Gi', [128, NK, TCW], F32)
    Hr = S.sb('Hr', [128, NK, TCW + 1], BF16)
    Hi = S.sb('Hi', [128, NK, TCW + 1], BF16)
    hpr = S.sb('hpr', [128, NK], F32)
    hpi_ = S.sb('hpi', [128, NK], F32)
    inr = S.sb('inr', [128, NK], F32)
    ini = S.sb('ini', [128, NK], F32)
    w1 = [S.sb(f'w1{i}', [128, TCW], F32) for i in range(2)]
    w2 = [S.sb(f'w2{i}', [128, TCW], F32) for i in range(2)]
    w3 = [S.sb('w3', [128, TCW], F32)] * 2
    w4 = [S.sb('w4', [128, TCW], F32)] * 2
    zr_ = [S.sb(f'zr{i}', [128, TCW], F32) for i in range(2)]
    zi_ = [S.sb(f'zi{i}', [128, TCW], F32) for i in range(2)]
    pb = [S.ps(f'pb{i}') for i in range(4)]
    pyy = [S.ps(f'pyy{i}') for i in range(2)]

    def V(eng, out, a, b, op):
        S.tt(eng, out[:], a[:], b[:], op, [a, b], [out])

    def cmul(outr, outi, ar, ai, br, bi, R, ta, tb):
        S.tt('dve', ta[0], ar, br, ALU.mult, R, [ta[1]])
        S.tt('dve', tb[0], ai, bi, ALU.mult, R, [tb[1]])
        S.tt('dve', outr[0], ta[0], tb[0], ALU.subtract, [ta[1], tb[1]], [outr[1]])
        S.tt('dve', ta[0], ar, bi, ALU.mult, R, [ta[1]])
        S.tt('dve', tb[0], ai, br, ALU.mult, R, [tb[1]])
        S.tt('dve', outi[0], ta[0], tb[0], ALU.add, [ta[1], tb[1]], [outi[1]])

    for q in range(4):
        for c in range(2):
            S.ms('pool', bdz[q][c][:], 0.0, [bdz[q][c]])
    for m in range(W8):
        for k in range(NK):
            for c in range(2):
                S.ms('pool', CdA[m][k][c][:], 0.0, [CdA[m][k][c]])
    for k in range(NK):
        for c in range(2):
            S.ms('pool', CdF[k][c][:], 0.0, [CdF[k][c]])

    segs = ((0, 0, LC // W8), (LC, LC // W8, LL // W8))

    bg = conv_job(S, P, l, ('act',))

    def bgstep(n=1):
        for _ in range(n):
            next(bg, None)

    for r in range(2):
        rv = (r == 1)
        S.dma(are[:], P.s5_are[l, r], [], [are])
        S.dma(aim[:], P.s5_aim[l, r], [], [aim])
        S.dma(dt[:], P.s5_ldt[l, r], [], [dt])
        S.act(dt[:], dt[:], AF.Exp, [dt], [dt])
        V('dve', t_a, are, dt, ALU.mult)
        S.act(mag[:], t_a[:], AF.Exp, [t_a], [mag])
        V('dve', t_a, aim, dt, ALU.mult)
        S.ts('dve', t_a[:], t_a[:], 1.0 / 16, None, ALU.mult, None, [t_a], [t_a])
        S.act(sn[:], t_a[:], AF.Sin, [t_a], [sn])
        S.ts('dve', t_b[:], t_a[:], hpi, None, ALU.add, None, [t_a], [t_b])
        S.act(cc[:], t_b[:], AF.Sin, [t_b], [cc])

        def sq_angle():
            V('dve', t_a, cc, cc, ALU.mult)
            V('dve', t_b, sn, sn, ALU.mult)
            V('dve', t_c, sn, cc, ALU.mult)
            V('dve', cc, t_a, t_b, ALU.subtract)
            S.ts('dve', sn[:], t_c[:], 2.0, None, ALU.mult, None, [t_c], [sn])
        for _ in range(4):
            sq_angle()
        V('dve', ar_, cc, mag, ALU.mult)
        V('dve', ai_, sn, mag, ALU.mult)
        S.ts('dve', t_c[:], ar_[:], -1.0, None, ALU.add, None, [ar_], [t_c])
        V('dve', t_a, are, are, ALU.mult)
        V('dve', t_b, aim, aim, ALU.mult)
        V('dve', t_a, t_a, t_b, ALU.add)
        S.op('dve', lambda e: e.reciprocal(out=t_a[:], in_=t_a[:]), [t_a], [t_a])
        V('dve', t_b, t_c, are, ALU.mult)
        V('dve', zr, ai_, aim, ALU.mult)
        V('dve', t_b, t_b, zr, ALU.add)
        V('dve', zr, t_b, t_a, ALU.mult)
        V('dve', t_b, ai_, are, ALU.mult)
        V('dve', zi, t_c, aim, ALU.mult)
        V('dve', t_b, t_b, zi, ALU.subtract)
        V('dve', zi, t_b, t_a, ALU.mult)
        for _ in range(3):
            sq_angle()
        S.cp('dve', cth[:], cc[:], [cc], [cth])
        S.cp('dve', sth[:], sn[:], [sn], [sth])
        V('dve', t_a, mag, mag, ALU.mult)
        V('dve', t_b, t_a, t_a, ALU.mult)
        V('dve', rho8, t_b, t_b, ALU.mult)
        S.ms('dve', pwr[:, 0, :], 1.0, [pwr])
        S.ms('dve', pwi[:, 0, :], 0.0, [pwi])
        for m in range(1, W8 + 1):
            cmul((pwr[:, m, :], pwr), (pwi[:, m, :], pwi), pwr[:, m - 1, :], pwi[:, m - 1, :], ar_[:], ai_[:],
                 [pwr, pwi, ar_, ai_], (t_a[:], t_a), (t_b[:], t_b))
        zrb = AP(zr, 0, [[NK, 128], [1, NK], [0, 16]])
        zib = AP(zi, 0, [[NK, 128], [1, NK], [0, 16]])
        cmul((bbr[:], bbr), (bbi[:], bbi), bre[:], bim[:], zrb, zib, [bre, bim, zr, zi], (tb1[:], tb1), (tb2[:], tb2))
        S.dma(cre[:], P.s5_cre[l, r], [], [cre])
        S.dma(cim[:], P.s5_cim[l, r], [], [cim])
        for k in range(NK):
            c0 = 32 * (k % 4)
            for c, (src, sg) in enumerate(((cre, 1.0), (cim, -1.0))):
                d_ = CdF[k][c]
                S.ts('dve', d_[0:64, c0:c0 + 16], src[0:64, k, :], sg, None, ALU.mult, None, [src, d_], [d_])
                S.ts('dve', d_[64:128, c0 + 16:c0 + 32], src[64:128, k, :], sg, None, ALU.mult, None, [src, d_], [d_])
        for m in range(W8):
            pmr = AP(pwr, m * NK, [[(W8 + 1) * NK, 128], [1, NK], [0, 16]])
            pmi = AP(pwi, m * NK, [[(W8 + 1) * NK, 128], [1, NK], [0, 16]])
            cmul((xr_[:], xr_), (xi_[:], xi_), bbr[:], bbi[:], pmr, pmi, [bbr, bbi, pwr, pwi], (tb1[:], tb1), (tb2[:], tb2))
            for kc in range(2):
                for kk in range(4):
                    k = kc * 4 + kk
                    c0 = 32 * kk
                    for c, src in enumerate((xr_, xi_)):
                        bd = bdz[kk][c]
                        S.cp('dve', bd[0:64, c0:c0 + 16], src[0:64, k, :], [src, bd], [bd])
                        S.cp('dve', bd[64:128, c0 + 16:c0 + 32], src[64:128, k, :], [src, bd], [bd])
                        S.tr(ptp[:, c * 128:(c + 1) * 128], bd[:], P.ident_f[:], [bd, P.ident_f], [ptp])
                        S.cp('act', BdT[m][k][c][:], ptp[:, c * 128:(c + 1) * 128], [ptp], [BdT[m][k][c]])
                        S.mm(pkn[:, kc * 128:(kc + 1) * 128], bd[:], CdF[k][c][:], kk == 0 and c == 0, kk == 3 and c == 1, [bd, CdF[k][c]], [pkn])
                S.cp('act', KernT[m][kc][:], pkn[:, kc * 128:(kc + 1) * 128], [pkn], [KernT[m][kc]])
            pmr1 = AP(pwr, (m + 1) * NK, [[(W8 + 1) * NK, 128], [1, NK], [0, 16]])
            pmi1 = AP(pwi, (m + 1) * NK, [[(W8 + 1) * NK, 128], [1, NK], [0, 16]])
            cmul((xr_[:], xr_), (xi_[:], xi_), cre[:], cim[:], pmr1, pmi1, [cre, cim, pwr, pwi], (tb1[:], tb1), (tb2[:], tb2))
            for k in range(NK):
                c0 = 32 * (k % 4)
                for c, (src, sg) in enumerate(((xr_, 1.0), (xi_, -1.0))):
                    d_ = CdA[m][k][c]
                    S.ts('dve', d_[0:64, c0:c0 + 16], src[0:64, k, :], sg, None, ALU.mult, None, [src, d_], [d_])
                    S.ts('dve', d_[64:128, c0 + 16:c0 + 32], src[64:128, k, :], sg, None, ALU.mult, None, [src, d_], [d_])
        S.ms('dve', Rr[:, :, 0:1], 1.0, [Rr])
        S.ms('dve', Ri[:, :, 0:1], 0.0, [Ri])
        S.cp('dve', wr[:], cth[:], [cth], [wr])
        S.ts('dve', wi[:], sth[:], -1.0, None, ALU.mult, None, [sth], [wi])
        n = 1
        while n < TCW:
            m_ = min(n, TCW - n)
            wrb = AP(wr, 0, [[NK, 128], [1, NK], [0, m_]])
            wib = AP(wi, 0, [[NK, 128], [1, NK], [0, m_]])
            tmpa, tmpb = Gr, Gi
            S.tt('dve', tmpa[:, :, 0:m_], Rr[:, :, 0:m_], wrb, ALU.mult, [Rr, wr], [tmpa])
            S.tt('dve', tmpb[:, :, 0:m_], Ri[:, :, 0:m_], wib, ALU.mult, [Ri, wi], [tmpb])
            S.tt('dve', Rr[:, :, n:n + m_], tmpa[:, :, 0:m_], tmpb[:, :, 0:m_], ALU.subtract, [tmpa, tmpb, Rr], [Rr])
            S.tt('dve', tmpa[:, :, 0:m_], Rr[:, :, 0:m_], wib, ALU.mult, [Rr, wi], [tmpa])
            S.tt('dve', tmpb[:, :, 0:m_], Ri[:, :, 0:m_], wrb, ALU.mult, [Ri, wr], [tmpb])
            S.tt('dve', Ri[:, :, n:n + m_], tmpa[:, :, 0:m_], tmpb[:, :, 0:m_], ALU.add, [tmpa, tmpb, Ri], [Ri])
            V('dve', t_a, wr, wr, ALU.mult)
            V('dve', t_b, wi, wi, ALU.mult)
            V('dve', t_c, wr, wi, ALU.mult)
            V('dve', wr, t_a, t_b, ALU.subtract)
            S.ts('dve', wi[:], t_c[:], 2.0, None, ALU.mult, None, [t_c], [wi])
            n *= 2
        nW = NWB
        for b in range(NB):
            cw_, lw_ = LC // W8, LL // W8
            for k in range(2):
                csrc = P.u8T[k * 128:(k + 1) * 128, :, ctx_off(b) // W8:ctx_off(b) // W8 + cw_]
                lsrc = P.u8T[k * 128:(k + 1) * 128, :, lat_off(b) // W8:lat_off(b) // W8 + lw_]
                if rv:
                    S.dma(u8[:, k, :, 0:lw_], lsrc, [], [u8.s(k)])
                    S.dma(u8[:, k, :, lw_:NWB], csrc, [], [u8.s(k)])
                else:
                    S.dma(u8[:, k, :, 0:cw_], csrc, [], [u8.s(k)])
                    S.dma(u8[:, k, :, cw_:NWB], lsrc, [], [u8.s(k)])
            u = 0
            ny = 0

            def uwin(kc, j):
                if rv:
                    return AP(u8, (kc * W8 + (W8 - 1 - j)) * NWB + nW - 1, [[2 * W8 * NWB, 128], [-1, nW]])
                return AP(u8, (kc * W8 + j) * NWB, [[2 * W8 * NWB, 128], [1, nW]])
            hks = [Hr.s(k) for k in range(NK)] + [Hi.s(k) for k in range(NK)]
            S.ms('pool', AP(Hr, 0, [[NK * (TCW + 1), 128], [TCW + 1, NK]]), 0.0, hks[:NK])
            S.ms('pool', AP(Hi, 0, [[NK * (TCW + 1), 128], [TCW + 1, NK]]), 0.0, hks[NK:])
            for k in range(NK):
                bgstep(2)
                kc = k // 4
                i2 = u % 2
                pzr = pb[(2 * u) % 4]
                pzi = pb[(2 * u + 1) % 4]
                u += 1
                for j in range(W8):
                    S.mm(pzr[:, 0:nW], BdT[W8 - 1 - j][k][0][:], uwin(kc, j), j == 0, j == W8 - 1, [BdT[W8 - 1 - j][k][0], u8.s(kc)], [pzr])
                for j in range(W8):
                    S.mm(pzi[:, 0:nW], BdT[W8 - 1 - j][k][1][:], uwin(kc, j), j == 0, j == W8 - 1, [BdT[W8 - 1 - j][k][1], u8.s(kc)], [pzi])
                rr_, ri_ = Rr[:, k, 0:nW], Ri[:, k, 0:nW]
                S.tt('dve', w1[i2][:, 0:nW], pzr[:, 0:nW], rr_, ALU.mult, [pzr, Rr], [w1[i2]])
                S.tt('dve', w2[i2][:, 0:nW], pzi[:, 0:nW], ri_, ALU.mult, [pzi, Ri], [w2[i2]])
                S.tt('dve', w3[i2][:, 0:nW], pzi[:, 0:nW], rr_, ALU.mult, [pzi, Rr], [w3[i2]])
                S.tt('dve', w4[i2][:, 0:nW], pzr[:, 0:nW], ri_, ALU.mult, [pzr, Ri], [w4[i2]])
                S.tt('pool', zr_[i2][:, 0:nW], w1[i2][:, 0:nW], w2[i2][:, 0:nW], ALU.subtract, [w1[i2], w2[i2]], [zr_[i2]])
                S.tt('pool', zi_[i2][:, 0:nW], w3[i2][:, 0:nW], w4[i2][:, 0:nW], ALU.add, [w3[i2], w4[i2]], [zi_[i2]])
                rhob = AP(rho8, k, [[NK, 128], [0, nW]])
                S.op('dve', lambda e, k=k, i2=i2, rhob=rhob: e.tensor_tensor_scan(out=Gr[:, k, 0:nW], data0=rhob, data1=zr_[i2][:, 0:nW], initial=0.0, op0=ALU.mult, op1=ALU.add),
                     [rho8, zr_[i2]], [Gr.s(k)])
                S.op('dve', lambda e, k=k, i2=i2, rhob=rhob: e.tensor_tensor_scan(out=Gi[:, k, 0:nW], data0=rhob, data1=zi_[i2][:, 0:nW], initial=0.0, op0=ALU.mult, op1=ALU.add),
                     [rho8, zi_[i2]], [Gi.s(k)])
                S.tt('dve', w1[i2][:, 0:nW], Gr[:, k, 0:nW], rr_, ALU.mult, [Gr.s(k), Rr], [w1[i2]])
                S.tt('pool', w2[i2][:, 0:nW], Gi[:, k, 0:nW], ri_, ALU.mult, [Gi.s(k), Ri], [w2[i2]])
                S.tt('dve', w3[i2][:, 0:nW], Gi[:, k, 0:nW], rr_, ALU.mult, [Gi.s(k), Rr], [w3[i2]])
                S.tt('pool', w4[i2][:, 0:nW], Gr[:, k, 0:nW], ri_, ALU.mult, [Gr.s(k), Ri], [w4[i2]])
                S.tt('pool', Hr[:, k, 1:nW + 1], w1[i2][:, 0:nW], w2[i2][:, 0:nW], ALU.add, [w1[i2], w2[i2]], [Hr.s(k)])
                S.tt('pool', Hi[:, k, 1:nW + 1], w3[i2][:, 0:nW], w4[i2][:, 0:nW], ALU.subtract, [w3[i2], w4[i2]], [Hi.s(k)])
            for kc in range(2):
                for j in range(W8):
                    bgstep()
                    p_ = pyy[ny % 2]
                    ny += 1
                    first = True
                    for kk in range(4):
                        k = kc * 4 + kk
                        S.mm(p_[:, 0:nW], CdA[j][k][0][:], Hr[:, k, 0:nW], first, False, [CdA[j][k][0], Hr.s(k)], [p_])
                        first = False
                        S.mm(p_[:, 0:nW], CdA[j][k][1][:], Hi[:, k, 0:nW], False, False, [CdA[j][k][1], Hi.s(k)], [p_])
                    for jp in range(j + 1):
                        S.mm(p_[:, 0:nW], KernT[j - jp][kc][:], uwin(kc, jp), False, jp == j, [KernT[j - jp][kc], u8.s(kc)], [p_])
                    if rv:
                        S.cp('act', AP(ysA, kc * LB + LC - 1 - j, [[2 * LB, 128], [-W8, cw_]]), p_[:, 0:cw_], [p_], [ysA])
                        S.cp('act', AP(ysA, kc * LB + LC + LL - 1 - j, [[2 * LB, 128], [-W8, lw_]]), p_[:, cw_:nW], [p_], [ysA])
                    else:
                        S.cp('act', AP(ysA, kc * LB + j, [[2 * LB, 128], [W8, nW]]), p_[:, 0:nW], [p_], [ysA])
            for kc in range(2):
                S.dma(P.s5y[r, b, kc * 128:(kc + 1) * 128, :], ysA[:, kc, :], [ysA], [P.s5y.s((r, b, kc))])
    for _ in bg:
        pass
    S.emit()


def stage_s5_epi(P, l):
    K = P.K
    S = K.stage(f"s5e{l}")
    dT = S.sb('dT', [128, 2], F32)
    S.dma(dT[:], P.s5_dT[l], [], [dT])
    gb = S.sb('gb', [128, 2], F32)
    S.dma(gb[:], P.s5_glu_bT[l], [], [gb])
    gw = S.sb('gw', [128, 2, 256], BF16)
    S.dma(gw[:], P.glu_w_b[l].rearrange("(k p) c -> p k c", p=128), [], [gw])
    u5 = S.sb('u5', [128, 2, LB], BF16)
    ysA = S.sb('ysA', [128, 2, LB], F32)
    ysB = S.sb('ysB', [128, 2, LB], F32)
    yv = S.sb('yv', [128, LB], F32)
    x2 = S.sb('x2', [128, LB], F32)
    gy = S.sb('gy', [128, 2, LB], BF16)
    sg = [S.sb(f'sg{i}', [128, 512], F32) for i in range(2)]
    mixs = S.sb('mixs', [128, 2, LB], BF16)
    pgl = [S.ps(f'pgl{i}') for i in range(2)]
    for b in range(NB):
        for k in range(2):
            load_seq(S, lambda a, e, k=k: u5[:, k, a:e], P, OFF_S5 + 128 * k, 128, b, u5.s(k))
            S.dma(ysA[:, k, :], P.s5y[0, b, k * 128:(k + 1) * 128, :], [], [ysA.s(k)])
            S.dma(ysB[:, k, :], P.s5y[1, b, k * 128:(k + 1) * 128, :], [], [ysB.s(k)])
        for kc in range(2):
            S.stt(yv[:], u5[:, kc, :], dT[:, kc:kc + 1], ysA[:, kc, :], ALU.mult, ALU.add, [u5.s(kc), dT, ysA.s(kc)], [yv])
            S.tt('pool', yv[:], yv[:], ysB[:, kc, :], ALU.add, [yv, ysB.s(kc)], [yv])
            S.tt('pool', x2[:], yv[:], yv[:], ALU.mult, [yv], [x2])
            S.ts('dve', x2[:], x2[:], 0.0713548162726, 1.5957691216057, ALU.mult, ALU.add, [x2], [x2])
            S.tt('dve', x2[:], x2[:], yv[:], ALU.mult, [x2, yv], [x2])
            S.act(x2[:], x2[:], AF.Sigmoid, [x2], [x2])
            S.tt('dve', gy[:, kc, :], yv[:], x2[:], ALU.mult, [yv, x2], [gy.s(kc)])
        n = 0
        for a in range(0, LB, 512):
            cw_ = min(512, LB - a)
            for oc in range(2):
                p_ = pgl[n % 2]
                s_ = sg[n % 2]
                n += 1
                for kc in range(2):
                    S.mm(p_[:, 0:cw_], gw[:, kc, oc * 128:(oc + 1) * 128], gy[:, kc, a:a + cw_], kc == 0, kc == 1, [gw, gy.s(kc)], [p_])
                S.act(s_[:, 0:cw_], p_[:, 0:cw_], AF.Sigmoid, [p_, gb], [s_], bias=gb[:, oc:oc + 1], scale=1.0)
                S.tt('dve', mixs[:, oc, a:a + cw_], gy[:, oc, a:a + cw_], s_[:, 0:cw_], ALU.mult, [gy.s(oc), s_], [mixs])
        store_mix(S, P, mixs, 768, b)
    S.emit()


def stage_of(P, l, last):
    K = P.K
    S = K.stage(f"of{l}")
    wout = S.sb('wout', [128, 8, D], BF16)
    S.dma(wout[:], P.w_out_b[l].rearrange("(k p) c -> p k c", p=128), [], [wout])
    fg = S.sb('fg', [128, 8], F32)
    if last:
        S.dma(fg[:], P.final_gT[:], [], [fg])
    mixb = [S.sb(f'mix{i}', [128, 8, 512], BF16) for i in range(2)]
    hb = [S.sb(f'h{i}', [128, 8, 512], F32) for i in range(2)]
    sq = [S.sb(f'sq{i}', [128, 512], F32) for i in range(2)]
    tmpn = [S.sb(f'tmpn{i}', [128, 512], F32) for i in range(2)]
    xnb = [S.sb(f'xn{i}', [128, 8, 512], BF16) for i in range(2)]
    rstd = S.sb('rstd', [128, 512], F32)
    wsl = [S.sb(f'wsl{i}', [128, 8, 512], BF16) for i in range(3)]
    HT = S.sb('HT', [128, 22, 512], BF16)
    wd = [S.sb(f'wd{i}', [128, 22, 512], BF16) for i in range(2)]
    sg = [S.sb(f'sg{i}', [128, 512], F32) for i in range(2)]
    osb = [S.sb(f'osb{i}', [128, D], F32) for i in range(2)]
    B = [S.ps(f'B{i}') for i in range(8)]
    mod = P.mod[l]
    pss = B[2]
    tiles = [ti for ti in range(NT) if not (last and ti == 0)]
    cnt = {'nsl': 0}

    def prep(idx):
        ti = tiles[idx]
        mix, h, xn = mixb[idx % 2], hb[idx % 2], xnb[idx % 2]
        col = tile_col(ti)
        tsl = slice(ti * 512, (ti + 1) * 512)
        S.dma(mix[:], P.mixT[:, tsl].rearrange("(k p) t -> p k t", p=128), [], [mix])
        hsrc = P.xinT if l == 0 else P.hT
        S.dma(h[:], hsrc[:, tsl].rearrange("(k p) t -> p k t", p=128), [P.hT.s(ti)], [h])
        for dc in range(8):
            p_ = B[dc % 2]
            for k in range(8):
                S.mm(p_[:], wout[:, k, dc * 128:(dc + 1) * 128], mix[:, k, :], k == 0, k == 7, [wout, mix], [p_])
            S.stt(h[:, dc, :], p_[:], mod[:, 16 + dc, col:col + 1], h[:, dc, :], ALU.mult, ALU.add, [p_, mod, h], [h])

    def prep_b(idx):
        ti = tiles[idx]
        mix, h, xn = mixb[idx % 2], hb[idx % 2], xnb[idx % 2]
        col = tile_col(ti)
        for k in range(8):
            S.act(sq[k % 2][:], h[:, k, :], AF.Square, [h], [sq[k % 2]])
            S.mm(pss[:], P.ones_f[:], sq[k % 2][:], k == 0, k == 7, [P.ones_f, sq[k % 2]], [pss])
        S.act(rstd[:], pss[:], AF.Sqrt, [pss], [rstd], scale=1.0 / D, bias=EPS)
        S.op('dve', lambda e: e.reciprocal(out=rstd[:], in_=rstd[:]), [rstd], [rstd])
        for k in range(8):
            t_ = tmpn[k % 2]
            S.tt('dve', t_[:], h[:, k, :], rstd[:], ALU.mult, [h, rstd], [t_])
            S.act(xn[:, k, :], t_[:], AF.Identity, [t_, P.A2[l], mod], [xn.s(k)],
                  scale=P.A2[l][:, k, col:col + 1], bias=mod[:, 24 + k, col:col + 1])

    def up(idx):
        xn = xnb[idx % 2]
        xr = [xn.s(k) for k in range(8)]
        for s_ in range(11):
            w = wsl[cnt['nsl'] % 3]
            cnt['nsl'] += 1
            S.dma(w[:, :, 0:256], P.w_up_b[l, :, 256 * s_:256 * s_ + 256].rearrange("(k p) c -> p k c", p=128), [], [w])
            S.dma(w[:, :, 256:512], P.w_up_b[l, :, DFF + 256 * s_:DFF + 256 * s_ + 256].rearrange("(k p) c -> p k c", p=128), [], [w])
            for jj in range(2):
                j = 2 * s_ + jj
                pg_ = B[(j % 2) * 2]
                pu_ = B[(j % 2) * 2 + 1]
                for k in range(8):
                    S.mm(pg_[:], w[:, k, jj * 128:(jj + 1) * 128], xn[:, k, :], k == 0, k == 7, [w, xr[k]], [pg_])
                for k in range(8):
                    S.mm(pu_[:], w[:, k, 256 + jj * 128:256 + (jj + 1) * 128], xn[:, k, :], k == 0, k == 7, [w, xr[k]], [pu_])
                S.act(sg[j % 2][:], pg_[:], AF.Silu, [pg_], [sg[j % 2]])
                S.tt('dve', HT[:, j, :], pu_[:], sg[j % 2][:], ALU.mult, [pu_, sg[j % 2]], [HT.s(j)])

    def down(idx, halves):
        ti = tiles[idx]
        h = hb[idx % 2]
        col = tile_col(ti)
        tsl = slice(ti * 512, (ti + 1) * 512)
        hr = [HT.s(j) for j in range(22)]
        for half in halves:
            w = wd[half]
            S.dma(w[:, 0:11, :], P.w_dn_b[l, 0:11 * 128, half * 512:(half + 1) * 512].rearrange("(j p) c -> p j c", p=128), [], [w])
            S.dma(w[:, 11:22, :], P.w_dn_b[l, 11 * 128:22 * 128, half * 512:(half + 1) * 512].rearrange("(j p) c -> p j c", p=128), [], [w])
            for dd in range(4):
                dc = half * 4 + dd
                p_ = B[4 + dd]
                for j in range(22):
                    S.mm(p_[:], w[:, j, dd * 128:(dd + 1) * 128], HT[:, j, :], j == 0, j == 21, [w, hr[j]], [p_])
                S.stt(h[:, dc, :], p_[:], mod[:, 40 + dc, col:col + 1], h[:, dc, :], ALU.mult, ALU.add, [p_, mod, h], [h])
        if 1 not in halves:
            return
        if not last:
            S.dma(P.hT[:, tsl].rearrange("(k p) t -> p k t", p=128), h[:], [h], [P.hT.s(ti)])
        else:
            for k in range(8):
                S.act(sq[k % 2][:], h[:, k, :], AF.Square, [h], [sq[k % 2]])
                S.mm(pss[:], P.ones_f[:], sq[k % 2][:], k == 0, k == 7, [P.ones_f, sq[k % 2]], [pss])
            S.act(rstd[:], pss[:], AF.Sqrt, [pss], [rstd], scale=1.0 / D, bias=EPS)
            S.op('dve', lambda e: e.reciprocal(out=rstd[:], in_=rstd[:]), [rstd], [rstd])
            for k in range(8):
                S.stt(h[:, k, :], h[:, k, :], fg[:, k:k + 1], rstd[:], ALU.mult, ALU.mult, [h, fg, rstd], [h])
            S.dma(P.out[:, (ti - 1) * 512:ti * 512].rearrange("(k p) t -> p k t", p=128), h[:], [h], [P.out.s(ti)])

    prep(0)
    prep_b(0)
    for idx in range(len(tiles)):
        up(idx)
        nxt = idx + 1 < len(tiles)
        if nxt:
            prep(idx + 1)
        down(idx, [0])
        if nxt:
            prep_b(idx + 1)
        down(idx, [1])
    S.emit()


def build(dbg=False, upto=None):
    K = Kern()
    P = declare(K, dbg)
    P.hyK = {'L': K.dram('hyK_L', [2, LL, 512], BF16), 'C': K.dram('hyK_C', [2, LC, 512], BF16)}
    P.s5y = K.dram('s5y', [2, NB, 256, LB], F32)
    P.u8T = K.dram('u8T', [256, 8, T // 8], BF16)
    P.hyX = {'L': K.dram('hyX_L', [3, LL, 512], BF16), 'C': K.dram('hyX_C', [3, LC, 512], BF16)}
    steps = [('p0', lambda: stage_p0(P))]
    for l in range(2):
        last = (l == 1)
        steps.append((f'ip{l}', lambda l=l: stage_ip(P, l)))
        steps.append((f'ssd{l}', lambda l=l: stage_ssd(P, l)))
        steps.append((f'ret{l}', lambda l=l: stage_ret(P, l)))
        steps.append((f's5m{l}', lambda l=l: stage_s5(P, l)))
        steps.append((f's5{l}', lambda l=l: stage_s5_epi(P, l)))
        for tag in (['L'] if last else ['L', 'C']):
            steps.append((f'hyp{l}{tag}', lambda l=l, tag=tag: stage_hy_fp(P, l, tag)))
            steps.append((f'hyc{l}{tag}', lambda l=l, tag=tag: stage_hy_conv(P, l, tag, side=(lambda S: mod_body(S, P, 1, sw=256)) if (l == 0 and tag == 'L') else None)))
        steps.append((f'of{l}', lambda l=l, last=last: stage_of(P, l, last)))
    for name, fn in steps:
        fn()
        if upto is not None and name == upto:
            break
    K.close()
    return K, P


def _f32(a):
    return np.ascontiguousarray(np.asarray(a, dtype=np.float32))


def shared_inputs(inp):
    g = {}
    f = _f32
    g['mod_w'] = f(inp['mod_w'])
    g['mod_bT'] = f(inp['mod_b'].reshape(2, 48, 128).transpose(0, 2, 1))
    g['norm1_gT'] = f(inp['norm1_g'].reshape(2, 8, 128).transpose(0, 2, 1))
    g['norm2_gT'] = f(inp['norm2_g'].reshape(2, 8, 128).transpose(0, 2, 1))
    g['final_gT'] = f(inp['final_norm_g'].reshape(8, 128).T)
    g['w_in'] = f(inp['w_in'])
    g['w_out'] = f(inp['w_out'])
    g['ffn_w_up'] = f(inp['ffn_w_up'])
    g['ffn_w_down'] = f(inp['ffn_w_down'])
    cw = np.zeros((2, 128, 4, 4), np.float32)
    w = np.asarray(inp['ssd_conv_w'])
    bb = np.asarray(inp['ssd_conv_b'])
    for l in range(2):
        for ci, (c0, n) in enumerate(((0, 128), (128, 128), (256, 64), (320, 64))):
            cw[l, :n, ci, 0:3] = w[l, :, c0:c0 + n].T
            cw[l, :n, ci, 3] = bb[l, c0:c0 + n]
    g['ssd_cw'] = cw
    g['ssd_alog8'] = f(inp['ssd_a_log'].reshape(2, 8, 1))
    g['ssd_dtb8'] = f(inp['ssd_dt_bias'].reshape(2, 8, 1))
    g['ssd_dexp'] = f(np.repeat(np.asarray(inp['ssd_d']), 64, axis=1))
    g['ssd_norm_g'] = f(inp['ssd_norm_g'])
    hw = np.zeros((2, 128, 6, 4), np.float32)
    w = np.asarray(inp['hy_conv_w'])
    bb = np.asarray(inp['hy_conv_b'])
    for l in range(2):
        for k in range(6):
            hw[l, :, k, 0:3] = w[l, :, k * 128:(k + 1) * 128].T
            hw[l, :, k, 3] = bb[l, k * 128:(k + 1) * 128]
    g['hy_cw'] = hw
    g['hy_w1'] = f(inp['hy_w1'])
    g['hy_b1c'] = f(inp['hy_b1'].reshape(2, 64, 1))
    g['hy_freqc'] = f(inp['hy_freq'].reshape(2, 64, 1))
    g['hy_w2'] = f(inp['hy_w2'])
    g['hy_b2c'] = f(inp['hy_b2'].reshape(2, 64, 1))
    g['hy_w3'] = f(inp['hy_w3'])
    g['hy_bias'] = f(inp['hy_bias'].reshape(2, 512))
    g['ret_decay8'] = f(inp['ret_decay'].reshape(2, 8))

    def smaj(a):
        a = np.asarray(a)
        lead = a.shape[:-2]
        return f(a.reshape(lead + (8, 128)).swapaxes(-1, -2))
    g['s5_are'] = smaj(inp['s5_a_re'])
    g['s5_aim'] = smaj(inp['s5_a_im'])
    g['s5_ldt'] = smaj(np.repeat(np.asarray(inp['s5_log_dt'])[..., None], 64, axis=-1))
    g['s5_bre'] = f(np.asarray(inp['s5_b_re']).reshape(2, 8, 128, 16).transpose(0, 2, 1, 3))
    g['s5_bim'] = f(np.asarray(inp['s5_b_im']).reshape(2, 8, 128, 16).transpose(0, 2, 1, 3))
    def cmaj(a):
        a = np.asarray(a).transpose(0, 1, 2, 4, 3)
        return f(a.reshape(2, 2, 8, 128, 16).transpose(0, 1, 3, 2, 4))
    g['s5_cre'] = cmaj(inp['s5_c_re'])
    g['s5_cim'] = cmaj(inp['s5_c_im'])
    g['s5_dT'] = f(np.asarray(inp['s5_d']).reshape(2, 2, 128).transpose(0, 2, 1))
    g['s5_glu_w'] = f(inp['s5_glu_w'])
    g['s5_glu_bT'] = f(np.asarray(inp['s5_glu_b']).reshape(2, 2, 128).transpose(0, 2, 1))
    for name, arr in host_consts().items():
        g['k_' + name] = arr
    return g


def core_inputs(inp, core):
    b0 = core * NB
    x = np.asarray(inp['x'])[b0:b0 + NB].reshape(NB * LL, D)
    ctx = np.asarray(inp['ctx'])[b0:b0 + NB].reshape(NB * LC, D)
    d = {}
    d['xinT'] = _f32(np.concatenate([ctx, x], axis=0).T)
    cc = np.concatenate([np.asarray(inp['c'])[b0:b0 + NB], np.asarray(inp['c_ctx'])[None, :]], axis=0)
    d['cT'] = _f32(cc.reshape(3, 8, 128).transpose(2, 1, 0))
    return d


_BUILD = {}


def kernel(**inputs):
    if 'nc' not in _BUILD:
        K, P = build()
        _BUILD['nc'] = K.nc
    nc = _BUILD['nc']
    sh = shared_inputs(inputs)
    in_maps = []
    for core in range(8):
        m = dict(sh)
        m.update(core_inputs(inputs, core))
        in_maps.append(m)
    res = run_bass_kernel_spmd(nc, in_maps, core_ids=list(range(8)))
    outs = [np.ascontiguousarray(np.asarray(r['out']).T).reshape(NB, LL, D) for r in res.results]
    return np.concatenate(outs, axis=0).astype(np.float32)
```
